# Optimizing a Trainium2 kernel written in Bass

```python
import math
import jax, jax.numpy as jnp
from jax import lax
import numpy as np

D_MODEL = 1024
BATCH = 16
SEQ = 2048
DEPTH = 4
DEC_BATCH = 1
DEC_SEQ = 16384
PAST_LEN = 128

EPS = 1e-6
N_EVEN = (DEPTH + 1) // 2
N_ODD = DEPTH // 2

A_HEADS = 8
A_KV_HEADS = 2
A_HEAD_DIM = 64
A_WINDOW = 128
A_BLOCK = 128
A_Q_DIM = A_HEADS * A_HEAD_DIM
A_KV_DIM = A_KV_HEADS * A_HEAD_DIM
T5_BUCKETS = 32
T5_MAX_DIST = 128

SSD_HEADS = 16
SSD_HEAD_DIM = 64
SSD_INNER = SSD_HEADS * SSD_HEAD_DIM
SSD_GROUPS = 2
SSD_STATE = 64
SSD_CONV = 5
SSD_CHUNK = 64
SSD_XBC = SSD_INNER + 2 * SSD_GROUPS * SSD_STATE

EV_SIZES = [A_Q_DIM, A_KV_DIM, A_KV_DIM, SSD_INNER, SSD_XBC, 2 * SSD_HEADS]
EV_IN = sum(EV_SIZES)
EV_SPLITS = np.cumsum(EV_SIZES)[:-1].tolist()
EV_MIX = A_Q_DIM + SSD_INNER
XBC_SPLITS = [SSD_INNER, SSD_INNER + SSD_GROUPS * SSD_STATE]

HG_HEADS = 8
HG_EXPAND = 128
HG_HEAD_V = D_MODEL // HG_HEADS
HG_FDIM = HG_HEADS * HG_EXPAND
HG_CHUNK = 32
OD_SIZES = [HG_FDIM, HG_FDIM, HG_FDIM, D_MODEL, D_MODEL]
OD_IN = sum(OD_SIZES)
OD_SPLITS = np.cumsum(OD_SIZES)[:-1].tolist()

D_FF = -(-8 * D_MODEL // (3 * 256)) * 256

kernel_name = "hybrid_bidir_swa_ssd_hgrn2_encoder"


def rmsnorm(x, w):
    xf = x.astype(jnp.float32)
    y = xf * lax.rsqrt(jnp.mean(xf * xf, axis=-1, keepdims=True) + EPS)
    return (y * w.astype(jnp.float32)).astype(x.dtype)


def _flip(t):
    return jnp.flip(t, axis=1)


def t5_bucket(rel):
    half = T5_BUCKETS // 2
    max_exact = half // 2
    n = np.abs(rel)
    large = max_exact + (np.log(np.maximum(n, 1) / max_exact)
                         / np.log(T5_MAX_DIST / max_exact) * (half - max_exact)).astype(np.int32)
    large = np.minimum(large, half - 1)
    return ((rel > 0).astype(np.int32) * half + np.where(n < max_exact, n, large)).astype(np.int32)


def windowed_gqa(q, k, v, sink, t5_bias):
    b, S = q.shape[:2]
    nb = S // A_BLOCK
    rep = A_HEADS // A_KV_HEADS
    pad = ((0, 0), (A_BLOCK, A_BLOCK), (0, 0), (0, 0))

    def bands(t):
        tb = jnp.pad(t, pad).reshape(b, nb + 2, A_BLOCK, A_KV_HEADS, A_HEAD_DIM)
        return jnp.concatenate([tb[:, :-2], tb[:, 1:-1], tb[:, 2:]], axis=2)

    kb, vb = bands(k), bands(v)
    qb = q.reshape(b, nb, A_BLOCK, A_KV_HEADS, rep, A_HEAD_DIM)
    s = jnp.einsum('bnqgrd,bnkgd->bngrqk', qb, kb,
                   preferred_element_type=jnp.float32) * (A_HEAD_DIM ** -0.5)
    qi = np.arange(A_BLOCK)[:, None]
    kj = np.arange(3 * A_BLOCK)[None, :] - A_BLOCK
    rel = kj - qi
    bias = jnp.transpose(t5_bias[t5_bucket(rel)], (2, 0, 1)).astype(jnp.float32)
    bias = bias.reshape(A_KV_HEADS, rep, A_BLOCK, 3 * A_BLOCK)
    kpos = np.arange(nb)[:, None, None] * A_BLOCK + kj[None]
    valid = (np.abs(rel)[None] <= A_WINDOW) & (kpos >= 0) & (kpos < S)
    s = jnp.where(valid[None, :, None, None], s + bias, -jnp.inf)
    sk = sink.astype(jnp.float32).reshape(1, 1, A_KV_HEADS, rep, 1, 1)
    m = jnp.maximum(jnp.max(s, axis=-1, keepdims=True), sk)
    p = jnp.exp(s - m)
    denom = jnp.sum(p, axis=-1, keepdims=True) + jnp.exp(sk - m)
    o = jnp.einsum('bngrqk,bnkgd->bnqgrd', (p / denom).astype(v.dtype), vb)
    return o.reshape(b, S, A_Q_DIM)


def dwconv_centred(x, w, bias):
    y = lax.conv_general_dilated(x, w[:, None, :].astype(x.dtype), window_strides=(1,),
                                 padding=((SSD_CONV // 2, SSD_CONV // 2),),
                                 dimension_numbers=('NWC', 'WIO', 'NWC'),
                                 feature_group_count=x.shape[-1])
    return y + bias.astype(x.dtype)


def ssd_scan(x, dt, a, bm, cm):
    b, S = x.shape[:2]
    L = SSD_CHUNK
    nc = S // L
    r = SSD_HEADS // SSD_GROUPS
    xc = (x.astype(jnp.float32) * dt[..., None]).reshape(b, nc, L, SSD_GROUPS, r, SSD_HEAD_DIM)
    cs = jnp.cumsum((dt * a).reshape(b, nc, L, SSD_GROUPS, r), axis=2)
    bc = bm.astype(jnp.float32).reshape(b, nc, L, SSD_GROUPS, SSD_STATE)
    cc = cm.astype(jnp.float32).reshape(b, nc, L, SSD_GROUPS, SSD_STATE)
    cst = jnp.moveaxis(cs, 2, -1)
    causal = np.tril(np.ones((L, L), dtype=bool))
    decay = jnp.exp(jnp.where(causal, cst[..., :, None] - cst[..., None, :], -jnp.inf))
    cb = jnp.einsum('bclgn,bcsgn->bcgls', cc, bc)
    y_diag = jnp.einsum('bcgrls,bcsgrp->bclgrp', cb[:, :, :, None] * decay, xc)
    xw = xc * jnp.exp(cs[:, :, -1:] - cs)[..., None]
    out_decay = jnp.exp(cs)
    chunk_decay = jnp.exp(cs[:, :, -1])

    def step(h, inp):
        b_c, x_c, c_c, od_c, dc = inp
        y = jnp.einsum('blgn,bgrpn->blgrp', c_c, h) * od_c[..., None]
        h = dc[..., None, None] * h + jnp.einsum('blgn,blgrp->bgrpn', b_c, x_c)
        return h, y

    h0 = jnp.zeros((b, SSD_GROUPS, r, SSD_HEAD_DIM, SSD_STATE), jnp.float32)
    seq = tuple(jnp.moveaxis(t, 1, 0) for t in (bc, xw, cc, out_decay, chunk_decay))
    _, y_off = lax.scan(step, h0, seq)
    y = y_diag + jnp.moveaxis(y_off, 0, 1)
    return y.reshape(b, S, SSD_HEADS, SSD_HEAD_DIM)


def ssd_mixer(z, xbc, dt_raw, conv_w, conv_b, a_log, dt_bias, d_skip, norm_w):
    b, S = z.shape[:2]
    xbc = jax.nn.silu(dwconv_centred(xbc, conv_w, conv_b))
    xs, bm, cm = jnp.split(xbc, XBC_SPLITS, axis=-1)
    xs = xs.reshape(b, S, SSD_HEADS, SSD_HEAD_DIM)
    bm = bm.reshape(b, S, SSD_GROUPS, SSD_STATE)
    cm = cm.reshape(b, S, SSD_GROUPS, SSD_STATE)
    dt = jax.nn.softplus(dt_raw.astype(jnp.float32).reshape(b, S, 2, SSD_HEADS)
                         + dt_bias.astype(jnp.float32))
    a = -jnp.exp(a_log.astype(jnp.float32))
    y = ssd_scan(xs, dt[:, :, 0], a[0], bm, cm)
    y = y + _flip(ssd_scan(_flip(xs), _flip(dt[:, :, 1]), a[1], _flip(bm), _flip(cm)))
    y = y + d_skip.astype(jnp.float32)[:, None] * xs.astype(jnp.float32)
    y = y.reshape(b, S, SSD_INNER) * jax.nn.silu(z.astype(jnp.float32))
    yg = y.reshape(b, S, SSD_GROUPS, SSD_INNER // SSD_GROUPS)
    yg = yg * lax.rsqrt(jnp.mean(yg * yg, axis=-1, keepdims=True) + EPS)
    return (yg.reshape(b, S, SSD_INNER) * norm_w.astype(jnp.float32)).astype(z.dtype)


def even_mixer(h, w_in, sink, t5_bias, conv_w, conv_b, a_log, dt_bias, d_skip, norm_w, w_out):
    b, S, _ = h.shape
    q, k, v, z, xbc, dt_raw = jnp.split(h @ w_in, EV_SPLITS, axis=-1)
    a_out = windowed_gqa(q.reshape(b, S, A_HEADS, A_HEAD_DIM),
                         k.reshape(b, S, A_KV_HEADS, A_HEAD_DIM),
                         v.reshape(b, S, A_KV_HEADS, A_HEAD_DIM), sink, t5_bias)
    b_out = ssd_mixer(z, xbc, dt_raw, conv_w, conv_b, a_log, dt_bias, d_skip, norm_w)
    return jnp.concatenate([a_out, b_out.astype(a_out.dtype)], axis=-1) @ w_out


def hgrn2_scan(q, k, v, logf):
    b, S = q.shape[:2]
    L = HG_CHUNK
    nc = S // L

    def chunks(t):
        return t.reshape(b, nc, L, *t.shape[2:])

    qc, kc, vc, gc = chunks(q), chunks(k), chunks(v), chunks(logf)
    G = jnp.cumsum(gc, axis=2)
    q_in = qc * jnp.exp(G)
    att = jnp.einsum('bclhk,bcshk->bchls', q_in, kc * jnp.exp(-G))
    att = jnp.where(np.tril(np.ones((L, L), dtype=bool)), att, 0.0)
    o_intra = jnp.einsum('bchls,bcshv->bclhv', att, vc)
    k_st = kc * jnp.exp(G[:, :, -1:] - G)
    chunk_decay = jnp.exp(G[:, :, -1])

    def step(st, inp):
        q_c, k_c, v_c, dc = inp
        o = jnp.einsum('blhk,bhkv->blhv', q_c, st)
        st = dc[..., None] * st + jnp.einsum('blhk,blhv->bhkv', k_c, v_c)
        return st, o

    s0 = jnp.zeros((b, HG_HEADS, HG_EXPAND, HG_HEAD_V), jnp.float32)
    seq = tuple(jnp.moveaxis(t, 1, 0) for t in (q_in, k_st, vc, chunk_decay))
    _, o_inter = lax.scan(step, s0, seq)
    return (o_intra + jnp.moveaxis(o_inter, 0, 1)).reshape(b, S, HG_HEADS, HG_HEAD_V)


def odd_mixer(h, w_in, lb, norm_w, w_out):
    b, S, _ = h.shape
    q, f_fwd, f_bwd, i, g = jnp.split(h @ w_in, OD_SPLITS, axis=-1)
    q = jax.nn.silu(q.astype(jnp.float32)).reshape(b, S, HG_HEADS, HG_EXPAND)
    i = i.astype(jnp.float32).reshape(b, S, HG_HEADS, HG_HEAD_V)
    lb = lb.reshape(HG_HEADS, HG_EXPAND)

    def gates(fraw):
        fr = fraw.astype(jnp.float32).reshape(b, S, HG_HEADS, HG_EXPAND)
        f = lb + (1.0 - lb) * jax.nn.sigmoid(fr)
        return (1.0 - lb) * jax.nn.sigmoid(-fr), jnp.log(f)

    k_f, lf_f = gates(f_fwd)
    k_b, lf_b = gates(f_bwd)
    o = hgrn2_scan(q, k_f, i, lf_f)
    o = o + _flip(hgrn2_scan(_flip(q), _flip(k_b), _flip(i), _flip(lf_b)))
    o = o * lax.rsqrt(jnp.mean(o * o, axis=-1, keepdims=True) + EPS) * norm_w.astype(jnp.float32)
    o = o.reshape(b, S, D_MODEL) * jax.nn.silu(g.astype(jnp.float32))
    return o.astype(h.dtype) @ w_out


def swiglu(h, wg, wu, wd):
    return (jax.nn.silu(h @ wg) * (h @ wu)) @ wd


def trunk(x, norm_gains, t5_bias, ev_w_in, attn_sink, ssd_conv_w, ssd_conv_b, ssd_a_log,
          ssd_dt_bias, ssd_d, ssd_norm_w, ev_w_out, od_w_in, hg_lower_bounds, hg_norm_w,
          od_w_out, ffn_w_gate, ffn_w_up, ffn_w_down):
    lb_soft = jax.nn.softmax(hg_lower_bounds.astype(jnp.float32), axis=0)
    lb_all = jnp.cumsum(lb_soft, axis=0) - lb_soft[0]
    for l in range(DEPTH):
        g = norm_gains[l]
        hn = rmsnorm(x, g[0])
        j = l // 2
        if l % 2 == 0:
            m = even_mixer(hn, ev_w_in[j], attn_sink[j], t5_bias, ssd_conv_w[j], ssd_conv_b[j],
                           ssd_a_log[j], ssd_dt_bias[j], ssd_d[j], ssd_norm_w[j], ev_w_out[j])
        else:
            m = odd_mixer(hn, od_w_in[j], lb_all[j], hg_norm_w[j], od_w_out[j])
        x = x + rmsnorm(m, g[1])
        f = swiglu(rmsnorm(x, g[2]), ffn_w_gate[l], ffn_w_up[l], ffn_w_down[l])
        x = x + rmsnorm(f, g[3])
    return x


def setup_inputs(seed: int = 0) -> dict:
    key = jax.random.key(seed)
    ks = jax.random.split(key, 20)

    def nrm(k, shape, scale):
        return scale * jax.random.normal(k, shape, jnp.float32)

    dt0 = jnp.exp(jax.random.uniform(ks[9], (N_EVEN, 2, SSD_HEADS), jnp.float32,
                                     minval=math.log(1e-3), maxval=math.log(1e-1)))
    return {
        "x_prompt": nrm(ks[0], (BATCH, SEQ, D_MODEL), 1.0),
        "x_sample": nrm(ks[1], (DEC_BATCH, DEC_SEQ, D_MODEL), 1.0),
        "norm_gains": 1.0 + nrm(ks[2], (DEPTH, 4, D_MODEL), 0.02),
        "t5_bias": nrm(ks[3], (T5_BUCKETS, A_HEADS), 0.5),
        "ev_w_in": nrm(ks[4], (N_EVEN, D_MODEL, EV_IN), D_MODEL ** -0.5),
        "attn_sink": nrm(ks[5], (N_EVEN, A_HEADS), 0.5),
        "ssd_conv_w": nrm(ks[6], (N_EVEN, SSD_CONV, SSD_XBC), SSD_CONV ** -0.5),
        "ssd_conv_b": nrm(ks[7], (N_EVEN, SSD_XBC), 0.01),
        "ssd_a_log": jnp.log(jax.random.uniform(ks[8], (N_EVEN, 2, SSD_HEADS), jnp.float32, 1.0, 16.0)),
        "ssd_dt_bias": dt0 + jnp.log(-jnp.expm1(-dt0)),
        "ssd_d": 1.0 + nrm(ks[10], (N_EVEN, SSD_HEADS), 0.1),
        "ssd_norm_w": 1.0 + nrm(ks[11], (N_EVEN, SSD_INNER), 0.02),
        "ev_w_out": nrm(ks[12], (N_EVEN, EV_MIX, D_MODEL), EV_MIX ** -0.5),
        "od_w_in": nrm(ks[13], (N_ODD, D_MODEL, OD_IN), D_MODEL ** -0.5),
        "hg_lower_bounds": nrm(ks[14], (N_ODD, HG_FDIM), 0.5),
        "hg_norm_w": 1.0 + nrm(ks[15], (N_ODD, HG_HEAD_V), 0.02),
        "od_w_out": nrm(ks[16], (N_ODD, D_MODEL, D_MODEL), D_MODEL ** -0.5),
        "ffn_w_gate": nrm(ks[17], (DEPTH, D_MODEL, D_FF), D_MODEL ** -0.5),
        "ffn_w_up": nrm(ks[18], (DEPTH, D_MODEL, D_FF), D_MODEL ** -0.5),
        "ffn_w_down": nrm(ks[19], (DEPTH, D_FF, D_MODEL), D_FF ** -0.5),
    }


def reference(x_prompt, x_sample, norm_gains, t5_bias, ev_w_in, attn_sink, ssd_conv_w, ssd_conv_b,
              ssd_a_log, ssd_dt_bias, ssd_d, ssd_norm_w, ev_w_out, od_w_in, hg_lower_bounds,
              hg_norm_w, od_w_out, ffn_w_gate, ffn_w_up, ffn_w_down):
    y_prompt = trunk(x_prompt, norm_gains, t5_bias, ev_w_in, attn_sink, ssd_conv_w, ssd_conv_b,
                     ssd_a_log, ssd_dt_bias, ssd_d, ssd_norm_w, ev_w_out, od_w_in, hg_lower_bounds,
                     hg_norm_w, od_w_out, ffn_w_gate, ffn_w_up, ffn_w_down)
    y_sample = trunk(x_sample, norm_gains, t5_bias, ev_w_in, attn_sink, ssd_conv_w, ssd_conv_b,
                     ssd_a_log, ssd_dt_bias, ssd_d, ssd_norm_w, ev_w_out, od_w_in, hg_lower_bounds,
                     hg_norm_w, od_w_out, ffn_w_gate, ffn_w_up, ffn_w_down)
    return (y_prompt, y_sample)
```

```python
import contextlib
import numpy as np
import concourse.bass as bass
import concourse.mybir as mybir
from concourse.bass_utils import run_bass_kernel_spmd

F32 = mybir.dt.float32
BF16 = mybir.dt.bfloat16
ALU = mybir.AluOpType
AF = mybir.ActivationFunctionType
AX = mybir.AxisListType

D = 1024
DC = 8
DFF = 2816
NFC = 22
EV_IN = 3104
OD_IN = 5120
EPS = 1e-6
NEG = -30000.0
LSUB = 64
NSB = 128 // LSUB


class Buf:
    __slots__ = ("t", "name", "w", "r", "excl")

    def __init__(self, t, name, excl=False):
        self.t = t
        self.name = name
        self.w = None
        self.r = {}
        self.excl = excl

    def __getitem__(self, idx):
        return self.t[idx]


class _Eng:
    def __init__(self, h, key, sem):
        self.h = h
        self.key = key
        self.sem = sem
        self.n = 0
        self.waited = {}


class FW:
    def __init__(self, nc, stack, same_engine_sync=("act", "dve", "pool")):
        self.nc = nc
        self.stack = stack
        self.sems = {}
        self.eng = {}
        for key, h in (("pe", nc.tensor), ("act", nc.scalar), ("dve", nc.vector),
                       ("pool", nc.gpsimd), ("sp", nc.sync)):
            s = stack.enter_context(nc.semaphore("s_" + key))
            self.sems[key] = s
            self.eng[key] = _Eng(h, key, s)
        self.same_engine_sync = set(same_engine_sync)
        n_slots = {"sp": 16, "act": 4, "pool": 12}
        self.slots = {}
        self.slot_rr = {}
        for q, n in n_slots.items():
            lst = []
            for i in range(n):
                k = "d_%s%d" % (q, i)
                s = stack.enter_context(nc.semaphore(k))
                self.sems[k] = s
                lst.append([k, 0])
            self.slots[q] = lst
            self.slot_rr[q] = 0
        self.n_wait = 0
        self.n_inst = 0
        self.uid = 0

    def sbuf(self, name, shape, dtype):
        self.uid += 1
        nm = "%s_%d" % (name, self.uid)
        t = self.stack.enter_context(self.nc.sbuf_tensor(nm, list(shape), dtype))
        return Buf(t, nm)

    def psum(self, name, shape, dtype):
        t = self.stack.enter_context(self.nc.psum_tensor(name, list(shape), dtype))
        return Buf(t, name, excl=True)

    def dram(self, name, shape, dtype, kind="Internal"):
        t = self.nc.dram_tensor(name, list(shape), dtype, kind=kind)
        return Buf(t, name)

    @contextlib.contextmanager
    def scope(self):
        old = self.stack
        self.stack = contextlib.ExitStack()
        try:
            yield
        finally:
            self.barrier()
            self.stack.close()
            self.stack = old

    def _wait(self, E, key, val):
        if E.waited.get(key, 0) >= val:
            return
        E.h.wait_ge(self.sems[key], val)
        E.waited[key] = val
        self.n_wait += 1

    def _deps(self, E, reads, writes):
        deps = {}
        for b in reads:
            if b.w is not None:
                k, v = b.w
                if deps.get(k, 0) < v:
                    deps[k] = v
            if b.excl:
                for k, v in b.r.items():
                    if k != E.key and deps.get(k, 0) < v:
                        deps[k] = v
        for b in writes:
            if b.w is not None:
                k, v = b.w
                if deps.get(k, 0) < v:
                    deps[k] = v
            for k, v in b.r.items():
                if deps.get(k, 0) < v:
                    deps[k] = v
        for k, v in deps.items():
            if k == E.key and k not in self.same_engine_sync:
                continue
            self._wait(E, k, v)

    def _mark(self, ev, reads, writes):
        k, v = ev
        for b in reads:
            if b.r.get(k, 0) < v:
                b.r[k] = v
        for b in writes:
            b.w = ev
            b.r = {}

    def op(self, eng, fn, reads=(), writes=()):
        E = self.eng[eng]
        self._deps(E, reads, writes)
        inst = fn(E.h)
        E.n += 1
        inst.then_inc(E.sem, 1)
        self.n_inst += 1
        self._mark((E.key, E.n), reads, writes)
        return inst

    def dma(self, q, out, in_, reads=(), writes=(), **kw):
        E = self.eng[q]
        self._deps(E, reads, writes)
        lst = self.slots[q]
        i = self.slot_rr[q]
        self.slot_rr[q] = (i + 1) % len(lst)
        slot = lst[i]
        if slot[1] > 0:
            self._wait(E, slot[0], 16 * slot[1])
        inst = E.h.dma_start(out=out, in_=in_, **kw)
        slot[1] += 1
        inst.then_inc(self.sems[slot[0]], 16)
        self.n_inst += 1
        self._mark((slot[0], 16 * slot[1]), reads, writes)
        return inst

    def collective(self, kind, op, rg, in_ap, out_ap, reads=(), writes=()):
        E = self.eng["pool"]
        self._deps(E, reads, writes)
        lst = self.slots["pool"]
        i = self.slot_rr["pool"]
        self.slot_rr["pool"] = (i + 1) % len(lst)
        slot = lst[i]
        if slot[1] > 0:
            self._wait(E, slot[0], 16 * slot[1])
        inst = E.h.collective_compute(kind, op, replica_groups=rg, ins=[in_ap], outs=[out_ap])
        slot[1] += 1
        inst.then_inc(self.sems[slot[0]], 16)
        self.n_inst += 1
        self._mark((slot[0], 16 * slot[1]), reads, writes)
        return inst

    def barrier(self):
        S = self.eng["sp"]
        for q, lst in self.slots.items():
            for k, c in lst:
                if c > 0:
                    self._wait(S, k, 16 * c)
        for key in ("pe", "act", "dve", "pool"):
            X = self.eng[key]
            if X.n > 0:
                self._wait(S, key, X.n)
        S.h.sem_inc(S.sem, 1)
        S.n += 1
        for key in ("pe", "act", "dve", "pool"):
            X = self.eng[key]
            self._wait(X, "sp", S.n)
            for k2 in ("pe", "act", "dve", "pool"):
                X.waited[k2] = max(X.waited.get(k2, 0), self.eng[k2].n)
            for q, lst in self.slots.items():
                for k, c in lst:
                    X.waited[k] = max(X.waited.get(k, 0), 16 * c)

    def finish(self):
        self.barrier()


def lockstep(gen_iter, width=2):
    it = iter(gen_iter)
    active = []
    done = False
    while True:
        while not done and len(active) < width:
            try:
                active.append(next(it))
            except StopIteration:
                done = True
        if not active:
            return
        for g in list(active):
            try:
                next(g)
            except StopIteration:
                active.remove(g)


def lockstep_gen(gens):
    active = list(gens)
    while active:
        for g in list(active):
            try:
                next(g)
                yield
            except StopIteration:
                active.remove(g)


class Cfg:
    def __init__(self, S=2048, nseg=8, layers=(0, 1, 2, 3), TS=512, mixers=True):
        self.S = S
        self.nseg = nseg
        self.layers = tuple(layers)
        self.TS = min(TS, S)
        self.mixers = mixers


W_SPECS = [
    ("norm_gains", [4, 4, 1024]), ("t5_bias", [32, 8]), ("ev_w_in", [2, 1024, 3104]),
    ("attn_sink", [2, 8]), ("ssd_conv_w", [2, 5, 1280]), ("ssd_conv_b", [2, 1280]),
    ("ssd_a_log", [2, 2, 16]), ("ssd_dt_bias", [2, 2, 16]), ("ssd_d", [2, 16]),
    ("ssd_norm_w", [2, 1024]), ("ev_w_out", [2, 1536, 1024]), ("od_w_in", [2, 1024, 5120]),
    ("hg_lower_bounds", [2, 1024]), ("hg_norm_w", [2, 128]), ("od_w_out", [2, 1024, 1024]),
    ("ffn_w_gate", [4, 1024, 2816]), ("ffn_w_up", [4, 1024, 2816]), ("ffn_w_down", [4, 2816, 1024]),
]


def t5_bucket_np(rel):
    half = 16
    max_exact = 8
    n = np.abs(rel)
    large = max_exact + (np.log(np.maximum(n, 1) / max_exact) / np.log(128 / max_exact) * (half - max_exact)).astype(np.int32)
    large = np.minimum(large, half - 1)
    return ((rel > 0).astype(np.int32) * half + np.where(n < max_exact, n, large)).astype(np.int32)


def host_consts():
    qi = np.arange(128)[:, None]
    kj = np.arange(384)[None, :] - 128
    rel = kj - qi
    valid = np.abs(rel) <= 128
    bucket = np.where(valid, t5_bucket_np(rel), -1).astype(np.float32)
    maskneg = np.where(valid, 0.0, NEG).astype(np.float32)
    return {"c_bucket": bucket, "c_maskneg": maskneg}

class KB:
    def __init__(self, cfg):
        self.cfg = cfg
        self.nc = bass.Bass("TRN2", target_bir_lowering=False)

    def build(self):
        cfg = self.cfg
        nc = self.nc
        S, nseg = cfg.S, cfg.nseg
        NT = S // 128
        self.x_in = nc.dram_tensor("x_in", [nseg, S, D], F32, kind="ExternalInput")
        self.y_out = nc.dram_tensor("y_out", [nseg, S, D], F32, kind="ExternalOutput")
        self.W = {}
        for name, shape in W_SPECS:
            self.W[name] = nc.dram_tensor(name, shape, F32, kind="ExternalInput")
        self.c_bucket = nc.dram_tensor("c_bucket", [128, 384], F32, kind="ExternalInput")
        self.c_maskneg = nc.dram_tensor("c_maskneg", [128, 384], F32, kind="ExternalInput")
        self.flags_in = nc.dram_tensor("flags", [1, 4 * nseg], F32, kind="ExternalInput")
        with contextlib.ExitStack() as st:
            import os as _os
            _ses = tuple(x for x in _os.environ.get("SES", "act,dve,pool").split(",") if x)
            f = FW(nc, st, same_engine_sync=_ses)
            self.f = f
            xa = [f.dram("xTa%d" % s, [128, DC, S], F32) for s in range(nseg)]
            xb = [f.dram("xTb%d" % s, [128, DC, S], F32) for s in range(nseg)]
            self.oT = [f.dram("oT%d" % s, [128, 12, S], BF16) for s in range(nseg)]
            self.ofwd = [f.dram("ofwd%d" % s, [128, 8, NT * 128], F32) for s in range(nseg)]
            self.qsp = [f.dram("qsp%d" % s, [128, 8, S], F32) for s in range(nseg)]
            self.hnsp = [f.dram("hnsp%d" % s, [128, DC, S], BF16) for s in range(nseg)]
            self.vsp = [f.dram("vsp%d" % s, [128, 8, NT * 128], BF16) for s in range(nseg)]
            self.spill = {}
            if cfg.mixers and any(l % 2 == 0 for l in cfg.layers):
                for s in range(nseg):
                    for g in range(2):
                        k = "%d_%d" % (s, g)
                        self.spill[(s, g)] = {
                            "xs": f.dram("sp_xs" + k, [128, NT, 512], BF16),
                            "BT": f.dram("sp_BT" + k, [128, S], BF16),
                            "CT": f.dram("sp_CT" + k, [128, S], BF16),
                            "Btm": f.dram("sp_Btm" + k, [128, NT, 128], BF16),
                            "dt": f.dram("sp_dt" + k, [128, NT, 16], F32),
                            "dtA": f.dram("sp_dtA" + k, [128, NT, 16], F32),
                            "y": f.dram("sp_y" + k, [128, NT, 512], F32),
                        }
            self.wg_t = f.dram("wg_t", [NFC, 128, DC * 128], BF16)
            self.wu_t = f.dram("wu_t", [NFC, 128, DC * 128], BF16)
            self.wd_t = f.dram("wd_t", [DC, 128, NFC * 128], BF16)
            self.wo_t = f.dram("wo_t", [128, 12 * D], BF16)
            self.banks = [f.psum("bank%d" % i, [128, 512], F32) for i in range(8)]
            self.consts()
            self.flags = f.sbuf("flags", [128, 4 * nseg], F32)
            f.dma("sp", self.flags[:], self.flags_in[0:1, :].partition_broadcast(128), writes=[self.flags])
            if cfg.mixers and any(l % 2 == 0 for l in cfg.layers):
                self.even_global_consts()
            self.phase_x0(xa)
            cur, nxt = xa, xb
            for li, l in enumerate(cfg.layers):
                last = li == len(cfg.layers) - 1
                self.convert_weights(l)
                with f.scope():
                    if cfg.mixers:
                        lc = self.layer_consts(l)
                        self.reset_carry(lc)
                        for s in range(nseg):
                            self.mixer_pass(l, s, 0, cur, lc)
                        self.reset_carry(lc)
                    for s in reversed(range(nseg)):
                        if cfg.mixers:
                            self.mixer_pass(l, s, 1, cur, lc)
                        self.phase_c(l, s, cur, nxt, last)
                cur, nxt = nxt, cur
            f.finish()
        return nc

    def convert_weights(self, l):
        f, cfg = self.f, self.cfg
        j = l // 2
        even = (l % 2 == 0)
        nO = 12 if even else 8
        wout = self.W["ev_w_out"] if even else self.W["od_w_out"]
        with f.scope():
            st8 = [f.sbuf("cv8_%d" % i, [128, DC, 128], BF16) for i in range(4)]
            st22 = [f.sbuf("cv22_%d" % i, [128, NFC, 128], BF16) for i in range(2)]
            sto = [f.sbuf("cvo_%d" % i, [128, 2, D], BF16) for i in range(2)]
            k = 0
            for fc in range(NFC):
                for src, dst in ((self.W["ffn_w_gate"], self.wg_t), (self.W["ffn_w_up"], self.wu_t)):
                    t_ = st8[k % 4]
                    k += 1
                    f.dma("pool", t_[:], src[l, :, fc * 128:(fc + 1) * 128].rearrange("(c p) n -> p c n", p=128), writes=[t_])
                    f.dma("sp", dst[fc].rearrange("p (c n) -> p c n", n=128), t_[:], reads=[t_], writes=[dst])
            for dc in range(DC):
                t_ = st22[dc % 2]
                f.dma("pool", t_[:], self.W["ffn_w_down"][l, :, dc * 128:(dc + 1) * 128].rearrange("(c p) n -> p c n", p=128), writes=[t_])
                f.dma("sp", self.wd_t[dc].rearrange("p (c n) -> p c n", n=128), t_[:], reads=[t_], writes=[self.wd_t])
            if cfg.mixers:
                for o2 in range(nO // 2):
                    t_ = sto[o2 % 2]
                    f.dma("pool", t_[:], wout[j, o2 * 256:(o2 + 1) * 256, :].rearrange("(c p) d -> p c d", p=128), writes=[t_])
                    f.dma("sp", self.wo_t[:, o2 * 2 * D:(o2 + 1) * 2 * D].rearrange("p (c d) -> p c d", d=D), t_[:], reads=[t_],
                          writes=[self.wo_t])

    def fl(self, kind, s):
        c = kind * self.cfg.nseg + s
        return self.flags[:, c:c + 1]

    def consts(self):
        f = self.f
        self.ident = f.sbuf("ident", [128, 128], F32)
        self.identb = f.sbuf("identb", [128, 128], BF16)
        self.onesb = f.sbuf("onesb", [128, 128], BF16)
        self.eps = f.sbuf("eps", [128, 1], F32)
        self.gall = f.sbuf("gall", [128, 128], F32)
        ident, identb = self.ident, self.identb
        f.op("pool", lambda e: e.memset(ident[:], 0.0), writes=[ident])
        f.op("pool", lambda e: e.affine_select(out=ident[:], in_=ident[:], pattern=[[-1, 128]],
                                               compare_op=ALU.not_equal, fill=1.0, base=0,
                                               channel_multiplier=1), reads=[ident], writes=[ident])
        f.op("dve", lambda e: e.tensor_copy(out=identb[:], in_=ident[:]), reads=[ident], writes=[identb])
        f.op("pool", lambda e: e.memset(self.onesb[:], 1.0), writes=[self.onesb])
        f.op("pool", lambda e: e.memset(self.eps[:], EPS), writes=[self.eps])
        with f.scope():
            tmp = f.sbuf("gtmp", [128, 128], F32)
            f.dma("sp", tmp[:], self.W["norm_gains"][:, :, :].rearrange("l k (c p) -> (l k c) p", p=128),
                  writes=[tmp])
            b = self.banks[0]
            f.op("pe", lambda e: e.transpose(b[:, 0:128], tmp[:], ident[:]), reads=[tmp, ident], writes=[b])
            f.op("dve", lambda e: e.tensor_copy(out=self.gall[:], in_=b[:, 0:128]), reads=[b], writes=[self.gall])

    def gcol(self, l, k, dc):
        c = (l * 4 + k) * 8 + dc
        return self.gall[:, c:c + 1]
    def phase_x0(self, xa):
        f, cfg = self.f, self.cfg
        S = cfg.S
        with f.scope():
            xin = [f.sbuf("xin", [128, D], F32) for _ in range(2)]
            xo = [f.sbuf("xo", [128, DC, 128], F32) for _ in range(2)]
            i = 0
            for s in range(cfg.nseg):
                for t in range(S // 128):
                    a, o = xin[i % 2], xo[i % 2]
                    f.dma("sp", a[:], self.x_in[s, t * 128:(t + 1) * 128, :], writes=[a])
                    for half in range(2):
                        b = self.banks[(2 * i + half) % 4]
                        for j in range(4):
                            dc = half * 4 + j
                            f.op("pe", lambda e, b=b, j=j, dc=dc, a=a: e.transpose(
                                b[:, j * 128:(j + 1) * 128], a[:, dc * 128:(dc + 1) * 128], self.ident[:]),
                                reads=[a, self.ident], writes=[b])
                        if half == 0:
                            f.op("act", lambda e, b=b, o=o, half=half: e.copy(
                                out=o[:, half * 4:half * 4 + 4, :], in_=b[:].rearrange("p (c t) -> p c t", t=128)),
                                reads=[b], writes=[o])
                        else:
                            f.op("dve", lambda e, b=b, o=o, half=half: e.tensor_copy(
                                out=o[:, half * 4:half * 4 + 4, :], in_=b[:].rearrange("p (c t) -> p c t", t=128)),
                                reads=[b], writes=[o])
                    f.dma("sp", xa[s][:, :, t * 128:(t + 1) * 128], o[:], reads=[o], writes=[xa[s]])
                    i += 1

    def rstd_from_sq(self, sq, rstd, bank, n):
        f = self.f
        for dc in range(DC):
            f.op("pe", lambda e, dc=dc: e.matmul(bank[:, 0:n], lhsT=self.onesb[:], rhs=sq[:, dc, 0:n],
                                                 start=(dc == 0), stop=(dc == DC - 1)),
                 reads=[self.onesb, sq], writes=[bank])
        f.op("act", lambda e: e.activation(out=rstd[:, 0:n], in_=bank[:, 0:n], func=AF.Ln,
                                           bias=self.eps[:, 0:1], scale=1.0 / D), reads=[bank, self.eps], writes=[rstd])
        f.op("act", lambda e: e.activation(out=rstd[:, 0:n], in_=rstd[:, 0:n], func=AF.Exp, scale=-0.5),
             reads=[rstd], writes=[rstd])
    def layer_consts(self, l):
        f = self.f
        j = l // 2
        if l % 2 == 0:
            lc = self.even_consts(j)
            lc["carry"] = [f.sbuf("carry%d" % g, [128, 8, 64], F32) for g in range(2)]
        else:
            lc = self.odd_consts(j)
            lc["carry"] = [f.sbuf("carry%d" % h, [128, 128], F32) for h in range(8)]
        return lc

    def reset_carry(self, lc):
        f = self.f
        for c in lc["carry"]:
            f.op("pool", lambda e, c=c: e.memset(c[:], 0.0), writes=[c])

    def mixer_pass(self, l, s, d, cur, lc):
        f, cfg = self.f, self.cfg
        S = cfg.S
        even = (l % 2 == 0)
        H = 128 if even else 0
        with f.scope():
            hnT = f.sbuf("hnT", [128, DC, S + 2 * H], BF16)
            if d == 0:
                self.phase_a0(l, s, hnT, H, cur, halo=even)
                f.dma("sp", self.hnsp[s][:, :, :], hnT[:, :, H:H + S], reads=[hnT], writes=[self.hnsp[s]])
            else:
                f.dma("sp", hnT[:, :, H:H + S], self.hnsp[s][:, :, :], reads=[self.hnsp[s]], writes=[hnT])
            if even:
                if d == 0:
                    for g in range(2):
                        self.attn_group(l, s, hnT, H, g, lc)
                    for g in range(2):
                        self.ssd_F(l, s, hnT, H, g, lc)
                else:
                    for g in range(2):
                        self.ssd_B(l, s, hnT, H, g, lc)
            else:
                self.odd_mixer(l, s, hnT, d, lc)

    def phase_a0(self, l, s, hnT, H, cur, halo):
        f, cfg = self.f, self.cfg
        S, TS = cfg.S, cfg.TS
        with f.scope():
            xt = [f.sbuf("a0x", [128, DC, TS], F32) for _ in range(2)]
            sq = f.sbuf("a0sq", [128, DC, TS], BF16)
            rstd = f.sbuf("a0rstd", [128, TS], F32)
            jobs = [(s, st_i * TS, TS, H + st_i * TS, None) for st_i in range(S // TS)]
            if halo:
                for side, nb in ((0, s - 1), (1, s + 1)):
                    dst = 0 if side == 0 else H + S
                    if 0 <= nb < cfg.nseg:
                        jobs.append((nb, (S - 128) if side == 0 else 0, 128, dst, self.fl(side, s)))
                    else:
                        f.op("pool", lambda e, dst=dst: e.memset(hnT[:, :, dst:dst + 128], 0.0), writes=[hnT])
            for ji, (src, c0, n, dst, flag) in enumerate(jobs):
                x_ = xt[ji % 2]
                f.dma("sp", x_[:, :, 0:n], cur[src][:, :, c0:c0 + n], reads=[cur[src]], writes=[x_])
                f.op("act", lambda e, x_=x_, n=n: e.activation(out=sq[:, :, 0:n], in_=x_[:, :, 0:n], func=AF.Square), reads=[x_], writes=[sq])
                self.rstd_from_sq(sq, rstd, self.banks[ji % 2], n)
                if flag is not None:
                    f.op("dve", lambda e, n=n, flag=flag: e.tensor_scalar(out=rstd[:, 0:n], in0=rstd[:, 0:n], scalar1=flag, scalar2=None,
                                                                          op0=ALU.mult), reads=[rstd, self.flags], writes=[rstd])
                for dc in range(DC):
                    f.op("dve", lambda e, dc=dc, x_=x_, n=n, dst=dst: e.scalar_tensor_tensor(
                        out=hnT[:, dc, dst:dst + n], in0=x_[:, dc, 0:n], scalar=self.gcol(l, 0, dc), in1=rstd[:, 0:n],
                        op0=ALU.mult, op1=ALU.mult), reads=[x_, self.gall, rstd], writes=[hnT])

    def proj_fm(self, w, hnT, c0, n, bank):
        f = self.f
        for dc in range(DC):
            f.op("pe", lambda e, dc=dc: e.matmul(bank[:, 0:n], lhsT=w[:, dc, :], rhs=hnT[:, dc, c0:c0 + n],
                                                 start=(dc == 0), stop=(dc == DC - 1)), reads=[w, hnT], writes=[bank])

    def proj_tm(self, w, wcols, hnT, c0, bank, o0):
        f = self.f
        lo, hi = wcols
        for dc in range(DC):
            f.op("pe", lambda e, dc=dc: e.matmul(bank[:, o0:o0 + (hi - lo)], lhsT=hnT[:, dc, c0:c0 + 128],
                                                 rhs=w[:, dc, lo:hi], start=(dc == 0), stop=(dc == DC - 1)),
                 reads=[w, hnT], writes=[bank])

    def odd_consts(self, j):
        f = self.f
        c = {}
        lbT = f.sbuf("lbT", [128, 16], F32)
        tmp = f.sbuf("lbtmp", [16, 128], F32)
        f.dma("sp", tmp[:], self.W["hg_lower_bounds"][:, :].rearrange("j (h p) -> (j h) p", p=128), writes=[tmp])
        b = self.banks[0]
        f.op("pe", lambda e: e.transpose(b[:, 0:16], tmp[:], self.ident[0:16, 0:16]), reads=[tmp, self.ident], writes=[b])
        f.op("dve", lambda e: e.tensor_copy(out=lbT[:], in_=b[:, 0:16]), reads=[b], writes=[lbT])
        lb = f.sbuf("lb", [128, 8], F32)
        oml = f.sbuf("oml", [128, 8], F32)
        noml = f.sbuf("noml", [128, 8], F32)
        if j == 0:
            f.op("dve", lambda e: e.memset(lb[:], 0.0), writes=[lb])
        else:
            m = f.sbuf("lbm", [128, 8], F32)
            e0 = f.sbuf("lbe0", [128, 8], F32)
            e1 = f.sbuf("lbe1", [128, 8], F32)
            f.op("dve", lambda e: e.tensor_tensor(out=m[:], in0=lbT[:, 0:8], in1=lbT[:, 8:16], op=ALU.max), reads=[lbT], writes=[m])
            f.op("dve", lambda e: e.tensor_tensor(out=e0[:], in0=lbT[:, 0:8], in1=m[:], op=ALU.subtract), reads=[lbT, m], writes=[e0])
            f.op("dve", lambda e: e.tensor_tensor(out=e1[:], in0=lbT[:, 8:16], in1=m[:], op=ALU.subtract), reads=[lbT, m], writes=[e1])
            f.op("act", lambda e: e.activation(out=e0[:], in_=e0[:], func=AF.Exp), reads=[e0], writes=[e0])
            f.op("act", lambda e: e.activation(out=e1[:], in_=e1[:], func=AF.Exp), reads=[e1], writes=[e1])
            f.op("dve", lambda e: e.tensor_tensor(out=e0[:], in0=e0[:], in1=e1[:], op=ALU.add), reads=[e0, e1], writes=[e0])
            f.op("dve", lambda e: e.reciprocal(out=e0[:], in_=e0[:]), reads=[e0], writes=[e0])
            f.op("dve", lambda e: e.tensor_tensor(out=lb[:], in0=e1[:], in1=e0[:], op=ALU.mult), reads=[e0, e1], writes=[lb])
        f.op("dve", lambda e: e.tensor_scalar(out=oml[:], in0=lb[:], scalar1=-1.0, scalar2=1.0, op0=ALU.mult, op1=ALU.add),
             reads=[lb], writes=[oml])
        f.op("dve", lambda e: e.tensor_scalar(out=noml[:], in0=lb[:], scalar1=1.0, scalar2=-1.0, op0=ALU.mult, op1=ALU.add),
             reads=[lb], writes=[noml])
        c["lb"], c["oml"], c["noml"] = lb, oml, noml
        mf = f.sbuf("hmf", [128, 128], F32)
        mb = f.sbuf("hmb", [128, 128], F32)
        blk = f.sbuf("hblk", [128, 128], F32)
        f.op("pool", lambda e: e.memset(blk[:], 0.0), writes=[blk])
        for c4 in range(NSB):
            f.op("pool", lambda e, c4=c4: e.memset(blk[c4 * LSUB:(c4 + 1) * LSUB, c4 * LSUB:(c4 + 1) * LSUB], 1.0), reads=[blk], writes=[blk])
        f.op("pool", lambda e: e.affine_select(out=mf[:], in_=blk[:], pattern=[[1, 128]], compare_op=ALU.is_ge, fill=0.0,
                                               base=0, channel_multiplier=-1), reads=[blk], writes=[mf])
        f.op("pool", lambda e: e.affine_select(out=mb[:], in_=blk[:], pattern=[[-1, 128]], compare_op=ALU.is_ge, fill=0.0,
                                               base=0, channel_multiplier=1), reads=[blk], writes=[mb])
        c["mf"], c["mb"] = mf, mb
        rm = f.sbuf("hrm", [128, NSB], F32)
        f.op("pool", lambda e: e.memset(rm[:], 0.0), writes=[rm])
        for c4 in range(NSB):
            f.op("pool", lambda e, c4=c4: e.memset(rm[c4 * LSUB:(c4 + 1) * LSUB, c4:c4 + 1], 1.0), reads=[rm], writes=[rm])
        c["rm"] = rm
        S = self.cfg.S
        sm = f.sbuf("hsm", [128, S], BF16)
        f.op("pool", lambda e: e.memset(sm[:], 1.0), writes=[sm])
        f.op("pool", lambda e: e.memset(sm[:].rearrange("p (a b) -> p a b", b=LSUB)[:, :, 0:1], 0.0), reads=[sm], writes=[sm])
        c["sm"] = sm
        nw = f.sbuf("hnw", [128, 128], F32)
        f.dma("sp", nw[:], self.W["hg_norm_w"][j:j + 1, :].partition_broadcast(128), writes=[nw])
        c["nw"] = nw
        return c
    def odd_mixer(self, l, s, hnT, d, oc):
        f, cfg = self.f, self.cfg
        S = cfg.S
        NT = S // 128
        NS = S // LSUB
        j = l // 2
        TS = cfg.TS
        B = self.banks
        win = self.W["od_w_in"]
        flag = self.fl(d, s)
        with f.scope():
            X = [f.sbuf("hX%d" % i, [128, S], F32) for i in range(7 if d == 1 else 6)]
            sigs = [X[0], X[0]]
            qss = [X[1], X[1]]
            sets = []
            for i in range(2):
                st_ = {"qd": f.sbuf("hqd%d" % i, [128, S], BF16), "qdc": f.sbuf("hqdc%d" % i, [128, S], BF16),
                       "kd": f.sbuf("hkd%d" % i, [128, S], BF16),
                       "kst": f.sbuf("hkst%d" % i, [128, S], BF16), "v": f.sbuf("hv%d" % i, [128, NT, 128], BF16),
                       "dcv": f.sbuf("hdcv%d" % i, [128, NS], F32), "o": f.sbuf("hoacc%d" % i, [128, NT, 128], F32)}
                if d == 1:
                    st_["gs"] = f.sbuf("hgs%d" % i, [128, NT, 128], BF16)
                sets.append(st_)
            wq = f.sbuf("hwq", [128, DC, 128], BF16)
            wf = f.sbuf("hwf", [128, DC, 128], BF16)
            wv = f.sbuf("hwv", [128, DC, 128], BF16)
            if d == 1:
                og = f.sbuf("hog", [128, NT, 128], BF16)
                oTh = f.sbuf("hoTh", [128, S], BF16)
                wgt = f.sbuf("hwg", [128, DC, 128], BF16)
                of_ = f.sbuf("hof", [128, NT, 128], F32)
                sqd = f.sbuf("hsqd", [128, 128], F32)
                ss = f.sbuf("hss", [128, NT], F32)
                rs = f.sbuf("hrs", [128, NT], F32)
            QW = 128 + LSUB
            gm = f.sbuf("hgm", [128, NS], F32)
            qd4 = [f.sbuf("hqd4_%d" % i, [128, NSB * QW], BF16) for i in range(2)]
            kstm = [f.sbuf("hkstm%d" % i, [128, NSB, 128], BF16) for i in range(2)]
            attm = [f.sbuf("hattm%d" % i, [128, 128], BF16) for i in range(2)]
            St = [f.sbuf("hS%d" % i, [128, 128], F32) for i in range(2)]
            Sb = [f.sbuf("hSb%d" % i, [128, 128], BF16) for i in range(2)]
            for i in range(2):
                f.op("pool", lambda e, t=qd4[i]: e.memset(t[:], 0.0), writes=[qd4[i]])

            def Pproj(h):
                sig, qs = sigs[h % 2], qss[h % 2]

                def wsl(off):
                    return win[j, :, off + h * 128: off + (h + 1) * 128].rearrange("(c p) n -> p c n", p=128)
                if d == 0:
                    f.dma("pool", wq[:], wsl(0), writes=[wq])
                else:
                    f.dma("sp", qs[:], self.qsp[s][:, h, :], reads=[self.qsp[s]], writes=[qs])
                f.dma("pool", wf[:], wsl(1024 + 1024 * d), writes=[wf])
                yield
                bi = 0
                for st_i in range(S // TS):
                    c0 = st_i * TS
                    if d == 0:
                        b = B[6 + bi % 2]; bi += 1
                        for dc in range(DC):
                            f.op("pe", lambda e, dc=dc, b=b: e.matmul(b[:, 0:TS], lhsT=wq[:, dc, :], rhs=hnT[:, dc, c0:c0 + TS],
                                                                      start=(dc == 0), stop=(dc == DC - 1)), reads=[wq, hnT], writes=[b])
                            yield
                        f.op("act", lambda e, b=b: e.activation(out=qs[:, c0:c0 + TS], in_=b[:, 0:TS], func=AF.Silu),
                             reads=[b], writes=[qs])
                        yield
                        if st_i == S // TS - 1:
                            f.dma("sp", self.qsp[s][:, h, :], qs[:], reads=[qs], writes=[self.qsp[s]])
                    b = B[6 + bi % 2]; bi += 1
                    for dc in range(DC):
                        f.op("pe", lambda e, dc=dc, b=b: e.matmul(b[:, 0:TS], lhsT=wf[:, dc, :], rhs=hnT[:, dc, c0:c0 + TS],
                                                                  start=(dc == 0), stop=(dc == DC - 1)), reads=[wf, hnT], writes=[b])
                        yield
                    f.op("act", lambda e, b=b: e.activation(out=sig[:, c0:c0 + TS], in_=b[:, 0:TS], func=AF.Sigmoid),
                         reads=[b], writes=[sig])
                    yield

            def Pelem(h):
                T = sets[h % 2]
                sig, qs, A1, A2, A3, A5 = sigs[h % 2], qss[h % 2], X[2], X[3], X[4], X[5]

                def wsl(off):
                    return win[j, :, off + h * 128: off + (h + 1) * 128].rearrange("(c p) n -> p c n", p=128)
                if d == 0:
                    f.dma("pool", wv[:], wsl(3072), writes=[wv])
                else:
                    f.dma("pool", wgt[:], wsl(4096), writes=[wgt])
                    f.dma("sp", T["v"][:].rearrange("p a b -> p (a b)"), self.vsp[s][:, h, :], reads=[self.vsp[s]], writes=[T["v"]])
                yield
                bi = 0
                for t4 in range(0, NT, 4):
                    nt = min(4, NT - t4)
                    if d == 0:
                        bv = B[6 + bi % 2]; bi += 1
                        for q in range(nt):
                            for dc in range(DC):
                                f.op("pe", lambda e, dc=dc, q=q, bv=bv: e.matmul(
                                    bv[:, q * 128:(q + 1) * 128], lhsT=hnT[:, dc, (t4 + q) * 128:(t4 + q + 1) * 128], rhs=wv[:, dc, :],
                                    start=(dc == 0), stop=(dc == DC - 1)), reads=[wv, hnT], writes=[bv])
                                yield
                        f.op("dve", lambda e, bv=bv: e.tensor_copy(
                            out=T["v"][:, t4:t4 + nt, :], in_=bv[:, 0:nt * 128].rearrange("p (a b) -> p a b", b=128)),
                            reads=[bv], writes=[T["v"]])
                        yield
                        if t4 + nt == NT:
                            f.dma("sp", self.vsp[s][:, h, :], T["v"][:].rearrange("p a b -> p (a b)"), reads=[T["v"]], writes=[self.vsp[s]])
                    if d == 1:
                        bg = B[6 + bi % 2]; bi += 1
                        for q in range(nt):
                            for dc in range(DC):
                                f.op("pe", lambda e, dc=dc, q=q, bg=bg: e.matmul(
                                    bg[:, q * 128:(q + 1) * 128], lhsT=hnT[:, dc, (t4 + q) * 128:(t4 + q + 1) * 128], rhs=wgt[:, dc, :],
                                    start=(dc == 0), stop=(dc == DC - 1)), reads=[wgt, hnT], writes=[bg])
                                yield
                        f.op("act", lambda e, bg=bg: e.activation(
                            out=T["gs"][:, t4:t4 + nt, :], in_=bg[:, 0:nt * 128].rearrange("p (a b) -> p a b", b=128), func=AF.Silu),
                            reads=[bg], writes=[T["gs"]])
                        yield
                lbc, omlc, nomlc = oc["lb"][:, h:h + 1], oc["oml"][:, h:h + 1], oc["noml"][:, h:h + 1]
                cdeps = [oc["lb"], oc["oml"], oc["noml"]]
                CW = min(512, S)
                pieces = [(c, c + CW) for c in range(0, S, CW)]
                A4 = sig
                EG = A2
                if d == 0:
                    Gd, Kx, En = A4, A1, A5
                else:
                    A6 = X[6]
                    Gd, Kx, En = A5, A6, A4
                dcol = (LSUB - 1) if d == 0 else 0
                for (a, b_) in pieces:
                    f.op("dve", lambda e: e.tensor_scalar(out=A1[:, a:b_], in0=sig[:, a:b_], scalar1=omlc, scalar2=lbc, op0=ALU.mult, op1=ALU.add),
                         reads=[sig] + cdeps, writes=[A1])
                    yield
                    f.op("act", lambda e: e.activation(out=A2[:, a:b_], in_=A1[:, a:b_], func=AF.Ln), reads=[A1], writes=[A2])
                    yield
                    f.op("pool", lambda e: e.tensor_scalar(out=A3[:, a:b_], in0=sig[:, a:b_], scalar1=nomlc, scalar2=omlc, op0=ALU.mult, op1=ALU.add),
                         reads=[sig] + cdeps, writes=[A3])
                    yield
                for (a, b_) in pieces:
                    f.op("dve", lambda e: e.tensor_tensor_scan(out=A4[:, a:b_], data0=oc["sm"][:, a:b_], data1=A2[:, a:b_], initial=0.0,
                                                               op0=ALU.mult, op1=ALU.add), reads=[oc["sm"], A2], writes=[A4])
                    yield
                    g3 = A4[:, a:b_].rearrange("p (a b) -> p a b", b=LSUB)
                    f.op("dve", lambda e: e.tensor_tensor(out=A1[:, a:b_].rearrange("p (a b) -> p a b", b=LSUB),
                                                          in0=g3[:, :, LSUB - 1:LSUB].to_broadcast([128, (b_ - a) // LSUB, LSUB]), in1=g3,
                                                          op=ALU.subtract), reads=[A4], writes=[A1])
                    yield
                    if d == 1:
                        f.op("dve", lambda e: e.tensor_tensor(out=A5[:, a:b_], in0=A1[:, a:b_], in1=A2[:, a:b_], op=ALU.add),
                             reads=[A1, A2], writes=[A5])
                        yield
                        f.op("pool", lambda e: e.tensor_tensor(out=A6[:, a:b_], in0=A4[:, a:b_], in1=A2[:, a:b_], op=ALU.subtract),
                             reads=[A4, A2], writes=[A6])
                        yield
                for (a, b_) in pieces:
                    f.op("act", lambda e: e.activation(out=EG[:, a:b_], in_=Gd[:, a:b_], func=AF.Exp), reads=[Gd], writes=[EG])
                    yield
                    f.op("dve", lambda e: e.tensor_tensor(out=T["qd"][:, a:b_], in0=qs[:, a:b_], in1=EG[:, a:b_], op=ALU.mult),
                         reads=[qs, EG], writes=[T["qd"]])
                    yield
                f.op("dve", lambda e: e.tensor_copy(out=gm[:], in_=Gd[:].rearrange("p (a b) -> p a b", b=LSUB)[:, :, LSUB // 2]),
                     reads=[Gd], writes=[gm])
                yield
                for (a, b_) in pieces:
                    ns_ = (b_ - a) // LSUB
                    f.op("dve", lambda e: e.tensor_tensor(out=Gd[:, a:b_].rearrange("p (a b) -> p a b", b=LSUB),
                                                          in0=Gd[:, a:b_].rearrange("p (a b) -> p a b", b=LSUB),
                                                          in1=gm[:, a // LSUB:a // LSUB + ns_].unsqueeze(2).to_broadcast([128, ns_, LSUB]),
                                                          op=ALU.subtract), reads=[Gd, gm], writes=[Gd])
                    yield
                    f.op("act", lambda e: e.activation(out=En[:, a:b_], in_=Gd[:, a:b_], func=AF.Exp), reads=[Gd], writes=[En])
                    yield
                    f.op("dve", lambda e: e.tensor_tensor(out=T["qdc"][:, a:b_], in0=qs[:, a:b_], in1=En[:, a:b_], op=ALU.mult),
                         reads=[qs, En], writes=[T["qdc"]])
                    yield
                    f.op("act", lambda e: e.activation(out=En[:, a:b_], in_=Gd[:, a:b_], func=AF.Exp, scale=-1.0), reads=[Gd], writes=[En])
                    yield
                    f.op("dve", lambda e: e.tensor_tensor(out=T["kd"][:, a:b_], in0=A3[:, a:b_], in1=En[:, a:b_], op=ALU.mult),
                         reads=[A3, En], writes=[T["kd"]])
                    yield
                    f.op("act", lambda e: e.activation(out=Kx[:, a:b_], in_=Kx[:, a:b_], func=AF.Exp), reads=[Kx], writes=[Kx])
                    yield
                    f.op("dve", lambda e: e.tensor_tensor(out=T["kst"][:, a:b_], in0=A3[:, a:b_], in1=Kx[:, a:b_], op=ALU.mult),
                         reads=[A3, Kx], writes=[T["kst"]])
                    yield
                f.op("pool", lambda e: e.tensor_copy(out=T["dcv"][:], in_=EG[:].rearrange("p (a b) -> p a b", b=LSUB)[:, :, dcol]),
                     reads=[EG], writes=[T["dcv"]])
                yield

            def LE(h):
                T = sets[h % 2]
                carry = oc["carry"][h]
                qd, qdc, kd, kst, v_tm, dcv, o_acc = T["qd"], T["qdc"], T["kd"], T["kst"], T["v"], T["dcv"], T["o"]
                msk = oc["mf"] if d == 0 else oc["mb"]
                f.op("dve", lambda e: e.tensor_scalar(out=St[0][:], in0=carry[:], scalar1=flag, scalar2=None, op0=ALU.mult),
                     reads=[carry, self.flags], writes=[St[0]])
                f.op("dve", lambda e: e.tensor_scalar(out=Sb[0][:], in0=carry[:], scalar1=flag, scalar2=None, op0=ALU.mult),
                     reads=[carry, self.flags], writes=[Sb[0]])
                yield

                def pre(ti):
                    t = ti if d == 0 else NT - 1 - ti
                    c0 = t * 128
                    bA = B[0]
                    am, km, q4 = attm[ti % 2], kstm[ti % 2], qd4[ti % 2]
                    f.op("pe", lambda e: e.matmul(bA[:, 0:128], lhsT=kd[:, c0:c0 + 128], rhs=qdc[:, c0:c0 + 128],
                                                  start=True, stop=True), reads=[kd, qdc], writes=[bA])
                    bAb = bA[:, 128:256].bitcast(BF16)
                    f.op("pe", lambda e: e.transpose(bAb[:, 0:128], kst[:, c0:c0 + 128], self.identb[:]),
                         reads=[kst, self.identb], writes=[bA])
                    yield
                    f.op("dve", lambda e: e.tensor_tensor(out=am[:], in0=bA[:, 0:128], in1=msk[:], op=ALU.mult),
                         reads=[bA, msk], writes=[am])
                    yield
                    f.op("dve", lambda e: e.tensor_tensor(
                        out=km[:], in0=bAb[:, 0:128].unsqueeze(1).to_broadcast([128, NSB, 128]),
                        in1=oc["rm"][:, :].unsqueeze(2).to_broadcast([128, NSB, 128]), op=ALU.mult),
                        reads=[bA, oc["rm"]], writes=[km])
                    yield
                    f.op("act", lambda e: e.copy(out=q4[:].rearrange("p (c w) -> p c w", w=QW)[:, :, 0:LSUB],
                                                 in_=qd[:, c0:c0 + 128].rearrange("p (c w) -> p c w", w=LSUB)),
                         reads=[qd], writes=[q4])
                    yield

                def chain(ti):
                    t = ti if d == 0 else NT - 1 - ti
                    bU, bO = (B[1], B[2]), B[3 + ti % 2]
                    am, km, q4 = attm[ti % 2], kstm[ti % 2], qd4[ti % 2]
                    order = range(NSB) if d == 0 else range(NSB - 1, -1, -1)
                    for n_i, c4 in enumerate(order):
                        k = ti * NSB + n_i
                        Sc, Sn = St[k % 2], St[(k + 1) % 2]
                        Sbc, Sbn = Sb[k % 2], Sb[(k + 1) % 2]
                        bu = bU[k % 2]
                        f.op("pe", lambda e: e.matmul(bO[:, 0:128], lhsT=q4[:, c4 * 128:c4 * 128 + 128], rhs=Sbc[:],
                                                      start=(n_i == 0), stop=False), reads=[q4, Sbc], writes=[bO])
                        f.op("pe", lambda e: e.matmul(bu[:, 0:128], lhsT=km[:, c4, :], rhs=v_tm[:, t, :], start=True, stop=True),
                             reads=[km, v_tm], writes=[bu])
                        yield
                        sc = t * NSB + c4
                        f.op("dve", lambda e: e.scalar_tensor_tensor(out=Sbn[:], in0=Sc[:], scalar=dcv[:, sc:sc + 1], in1=bu[:, 0:128],
                                                                     op0=ALU.mult, op1=ALU.add), reads=[Sc, dcv, bu], writes=[Sbn])
                        f.op("dve", lambda e: e.scalar_tensor_tensor(out=Sn[:], in0=Sc[:], scalar=dcv[:, sc:sc + 1], in1=bu[:, 0:128],
                                                                     op0=ALU.mult, op1=ALU.add), reads=[Sc, dcv, bu], writes=[Sn])
                        yield
                    f.op("pe", lambda e: e.matmul(bO[:, 0:128], lhsT=am[:], rhs=v_tm[:, t, :], start=False, stop=True),
                         reads=[am, v_tm], writes=[bO])
                    yield
                    f.op("act", lambda e: e.copy(out=o_acc[:, t, :], in_=bO[:, 0:128]), reads=[bO], writes=[o_acc])
                    yield

                yield from pre(0)
                for ti in range(NT):
                    gens = [chain(ti)] + ([pre(ti + 1)] if ti + 1 < NT else [])
                    yield from lockstep_gen(gens)
                Sfin = St[(NT * NSB) % 2]
                f.op("pool", lambda e: e.tensor_copy(out=carry[:], in_=Sfin[:]), reads=[Sfin], writes=[carry])
                yield
                if d == 0:
                    f.dma("sp", self.ofwd[s][:, h, :], o_acc[:].rearrange("p a b -> p (a b)"), reads=[o_acc], writes=[self.ofwd[s]])
                    return
                f.dma("sp", of_[:].rearrange("p a b -> p (a b)"), self.ofwd[s][:, h, :], reads=[self.ofwd[s]], writes=[of_])
                yield
                f.op("dve", lambda e: e.tensor_tensor(out=o_acc[:], in0=o_acc[:], in1=of_[:], op=ALU.add), reads=[o_acc, of_], writes=[o_acc])
                yield
                for t in range(NT):
                    f.op("act", lambda e, t=t: e.activation(out=sqd[:], in_=o_acc[:, t, :], func=AF.Square, accum_out=ss[:, t:t + 1]),
                         reads=[o_acc], writes=[sqd, ss])
                    yield
                f.op("act", lambda e: e.activation(out=rs[:], in_=ss[:], func=AF.Ln, bias=self.eps[:, 0:1], scale=1.0 / 128),
                     reads=[ss, self.eps], writes=[rs])
                f.op("act", lambda e: e.activation(out=rs[:], in_=rs[:], func=AF.Exp, scale=-0.5), reads=[rs], writes=[rs])
                yield
                f.op("dve", lambda e: e.tensor_tensor(out=o_acc[:], in0=o_acc[:], in1=rs[:, :].unsqueeze(2).to_broadcast([128, NT, 128]),
                                                      op=ALU.mult), reads=[o_acc, rs], writes=[o_acc])
                yield
                f.op("dve", lambda e: e.tensor_tensor(out=o_acc[:], in0=o_acc[:], in1=oc["nw"][:, :].unsqueeze(1).to_broadcast([128, NT, 128]),
                                                      op=ALU.mult), reads=[o_acc, oc["nw"]], writes=[o_acc])
                yield
                f.op("dve", lambda e: e.tensor_tensor(out=og[:], in0=o_acc[:], in1=T["gs"][:], op=ALU.mult), reads=[o_acc, T["gs"]], writes=[og])
                yield
                for t4 in range(0, NT, 4):
                    nt = min(4, NT - t4)
                    b = B[0]
                    bb = b[:, 0:256].bitcast(BF16)
                    for q in range(nt):
                        f.op("pe", lambda e, q=q, bb=bb: e.transpose(bb[:, q * 128:(q + 1) * 128], og[:, t4 + q, :], self.identb[:]),
                             reads=[og, self.identb], writes=[b])
                    yield
                    f.op("act", lambda e, bb=bb, nt=nt: e.copy(out=oTh[:, t4 * 128:(t4 + nt) * 128], in_=bb[:, 0:nt * 128]),
                         reads=[b], writes=[oTh])
                    yield
                f.dma("sp", self.oT[s][:, h, :], oTh[:], reads=[oTh], writes=[self.oT[s]])

            def P(h):
                yield from Pproj(h)
                yield from Pelem(h)
            lockstep([P(0)], 1)
            for h in range(8):
                gens = [LE(h)] + ([P(h + 1)] if h + 1 < 8 else [])
                lockstep(gens, 2)

    def even_global_consts(self):
        f = self.f
        B = self.banks
        self.cwT = f.sbuf("cwT", [128, 100], F32)
        self.cbT = f.sbuf("cbT", [128, 20], F32)
        self.cwBC = f.sbuf("cwBC", [128, 2, 2, 2, 5], F32)
        self.cbBC = f.sbuf("cbBC", [128, 2, 2, 2], F32)
        f.op("pool", lambda e: e.memset(self.cwBC[:], 0.0), writes=[self.cwBC])
        f.op("pool", lambda e: e.memset(self.cbBC[:], 0.0), writes=[self.cbBC])
        self.ones32 = f.sbuf("ones32", [128, 128], F32)
        self.tri = [f.sbuf("tri%d" % d, [128, 128], F32) for d in range(2)]
        self.negm = [f.sbuf("negm%d" % d, [128, 128], F32) for d in range(2)]
        f.op("pool", lambda e: e.memset(self.ones32[:], 1.0), writes=[self.ones32])
        f.op("pool", lambda e: e.affine_select(out=self.tri[0][:], in_=self.ones32[:], pattern=[[1, 128]], compare_op=ALU.is_ge,
                                               fill=0.0, base=0, channel_multiplier=-1), reads=[self.ones32], writes=[self.tri[0]])
        f.op("pool", lambda e: e.affine_select(out=self.tri[1][:], in_=self.ones32[:], pattern=[[-1, 128]], compare_op=ALU.is_ge,
                                               fill=0.0, base=0, channel_multiplier=1), reads=[self.ones32], writes=[self.tri[1]])
        for d in range(2):
            f.op("dve", lambda e, d=d: e.tensor_scalar(out=self.negm[d][:], in0=self.tri[d][:], scalar1=-1.0, scalar2=-NEG,
                                                       op0=ALU.add, op1=ALU.mult), reads=[self.tri[d]], writes=[self.negm[d]])
        with f.scope():
            tmp = f.sbuf("cwtmp", [100, 128], F32)
            f.dma("sp", tmp[:], self.W["ssd_conv_w"][:, :, :].rearrange("j k (c p) -> (j k c) p", p=128), writes=[tmp])
            f.op("pe", lambda e: e.transpose(B[0][:, 0:100], tmp[:], self.ident[0:100, 0:100]), reads=[tmp, self.ident], writes=[B[0]])
            f.op("dve", lambda e: e.tensor_copy(out=self.cwT[:], in_=B[0][:, 0:100]), reads=[B[0]], writes=[self.cwT])
            tmp2 = f.sbuf("cbtmp", [20, 128], F32)
            f.dma("sp", tmp2[:], self.W["ssd_conv_b"][:, :].rearrange("j (c p) -> (j c) p", p=128), writes=[tmp2])
            f.op("pe", lambda e: e.transpose(B[1][:, 0:20], tmp2[:], self.ident[0:20, 0:20]), reads=[tmp2, self.ident], writes=[B[1]])
            f.op("dve", lambda e: e.tensor_copy(out=self.cbT[:], in_=B[1][:, 0:20]), reads=[B[1]], writes=[self.cbT])
            for j in range(2):
                for wh in range(2):
                    for g in range(2):
                        c0 = 1024 + wh * 128 + g * 64
                        f.dma("sp", self.cwBC[0:64, j, wh, g, :], self.W["ssd_conv_w"][j, :, c0:c0 + 64].rearrange("k n -> n k"),
                              writes=[self.cwBC], allow_slow_non_contiguous=True)
                        f.dma("sp", self.cbBC[0:64, j, wh, g:g + 1], self.W["ssd_conv_b"][j:j + 1, c0:c0 + 64].rearrange("o n -> n o"),
                              writes=[self.cbBC], allow_slow_non_contiguous=True)

    def build_abias(self):
        f = self.f
        self.bhi = f.sbuf("abhi", [128, 8, 384], BF16)
        self.blo = f.sbuf("ablo", [128, 8, 384], BF16)
        with f.scope():
            self.abias = f.sbuf("abias", [128, 8, 384], F32)
            bk = f.sbuf("bk", [128, 384], F32)
            mk = f.sbuf("mk", [128, 384], F32)
            t5b = f.sbuf("t5b", [128, 256], F32)
            f.dma("sp", bk[:], self.c_bucket[:, :], writes=[bk])
            f.dma("sp", t5b[:], self.W["t5_bias"][:, :].rearrange("b h -> (b h)").unsqueeze(0).partition_broadcast(128)
                  if False else self.W["t5_bias"][:, :].rearrange("(o b) h -> o (b h)", o=1).partition_broadcast(128), writes=[t5b])
            for h in range(8):
                f.dma("sp", self.abias[:, h, :], self.c_maskneg[:, :], writes=[self.abias])
            for b in range(32):
                f.op("pool", lambda e, b=b: e.tensor_scalar(out=mk[:], in0=bk[:], scalar1=float(b), scalar2=None, op0=ALU.is_equal),
                     reads=[bk], writes=[mk])
                for h in range(8):
                    f.op("dve", lambda e, b=b, h=h: e.scalar_tensor_tensor(
                        out=self.abias[:, h, :], in0=mk[:], scalar=t5b[:, b * 8 + h:b * 8 + h + 1], in1=self.abias[:, h, :],
                        op0=ALU.mult, op1=ALU.add), reads=[mk, t5b, self.abias], writes=[self.abias])
            f.op("dve", lambda e: e.tensor_copy(out=self.bhi[:], in_=self.abias[:]), reads=[self.abias], writes=[self.bhi])
            f.op("dve", lambda e: e.tensor_tensor(out=self.abias[:], in0=self.abias[:], in1=self.bhi[:], op=ALU.subtract),
                 reads=[self.abias, self.bhi], writes=[self.abias])
            f.op("dve", lambda e: e.tensor_copy(out=self.blo[:], in_=self.abias[:]), reads=[self.abias], writes=[self.blo])

    def even_consts(self, j):
        f = self.f
        c = {}
        self.build_abias()
        c["sink"] = f.sbuf("sink", [128, 8], F32)
        f.dma("sp", c["sink"][:], self.W["attn_sink"][j:j + 1, :].partition_broadcast(128), writes=[c["sink"]])
        c["a"] = f.sbuf("ssa", [128, 32], F32)
        f.dma("sp", c["a"][:], self.W["ssd_a_log"][j:j + 1, :, :].rearrange("o d h -> o (d h)").partition_broadcast(128), writes=[c["a"]])
        f.op("act", lambda e: e.activation(out=c["a"][:], in_=c["a"][:], func=AF.Exp), reads=[c["a"]], writes=[c["a"]])
        f.op("dve", lambda e: e.tensor_scalar(out=c["a"][:], in0=c["a"][:], scalar1=-1.0, scalar2=None, op0=ALU.mult),
             reads=[c["a"]], writes=[c["a"]])
        c["dtb"] = f.sbuf("ssdtb", [128, 32], F32)
        f.dma("sp", c["dtb"][:], self.W["ssd_dt_bias"][j:j + 1, :, :].rearrange("o d h -> o (d h)").partition_broadcast(128), writes=[c["dtb"]])
        c["D"] = f.sbuf("ssD", [128, 16], F32)
        f.dma("sp", c["D"][:], self.W["ssd_d"][j:j + 1, :].partition_broadcast(128), writes=[c["D"]])
        c["nw"] = f.sbuf("ssnw", [128, 1024], F32)
        f.dma("sp", c["nw"][:], self.W["ssd_norm_w"][j:j + 1, :].partition_broadcast(128), writes=[c["nw"]])
        return c
    def attn_group(self, l, s, hnT, H, g, ec):
        f, cfg = self.f, self.cfg
        S, TS = cfg.S, cfg.TS
        NB = S // 128
        j = l // 2
        B = self.banks
        win = self.W["ev_w_in"]
        with f.scope():
            wq = f.sbuf("awq", [128, DC, 256], BF16)
            wk = f.sbuf("awk", [128, DC, 128], BF16)
            wv = f.sbuf("awv", [128, DC, 64], BF16)
            qT = f.sbuf("aqT", [128, 2, S], BF16)
            kTlo = f.sbuf("akTlo", [128, S + 2 * H], BF16)
            kThi = f.sbuf("akThi", [128, S + 2 * H], BF16)
            Vlo = f.sbuf("aVlo", [128, NB + 2, 128], BF16)
            Vhi = f.sbuf("aVhi", [128, NB + 2, 128], BF16)
            oTa = [f.sbuf("aoT%d" % i, [128, S], BF16) for i in range(2)]
            AW = 3
            s_sb = [f.sbuf("as%d" % i, [128, 384], F32) for i in range(AW)]
            p_sb = [f.sbuf("ap%d" % i, [128, 384], F32) for i in range(AW)]
            pn = [f.sbuf("apn%d" % i, [128, 384], BF16) for i in range(AW)]
            pT = [f.sbuf("apT%d" % i, [128, 3, 128], BF16) for i in range(AW)]
            sm = [f.sbuf("asm%d" % i, [128, 8], F32) for i in range(AW)]
            dgs = [f.sbuf("adg%d" % i, [128, 128], BF16) for i in range(AW)]

            def wsl(c0, n):
                return win[j, :, c0:c0 + n].rearrange("(c p) n -> p c n", p=128)
            f.dma("pool", wq[:], wsl(g * 256, 256), writes=[wq])
            f.dma("pool", wk[:, :, 0:64], wsl(512 + g * 64, 64), writes=[wk])
            f.dma("pool", wk[:, :, 64:128], wsl(512 + g * 64, 64), writes=[wk])
            f.dma("pool", wv[:], wsl(640 + g * 64, 64), writes=[wv])
            f.op("pool", lambda e: e.memset(Vlo[:], 0.0), writes=[Vlo])
            f.op("pool", lambda e: e.memset(Vhi[:], 0.0), writes=[Vhi])
            f.op("pool", lambda e: e.memset(kTlo[:], 0.0), writes=[kTlo])
            f.op("pool", lambda e: e.memset(kThi[:], 0.0), writes=[kThi])
            ranges = [(H + i * TS, TS) for i in range(S // TS)] + [(0, H), (H + S, H)]
            bi = 0
            for (c0, n) in ranges:
                main = (H <= c0 < H + S)
                if main:
                    for p in range(2):
                        b = B[6 + bi % 2]; bi += 1
                        for dc in range(DC):
                            f.op("pe", lambda e, dc=dc, b=b, p=p: e.matmul(b[:, 0:n], lhsT=wq[:, dc, p * 128:(p + 1) * 128],
                                                                            rhs=hnT[:, dc, c0:c0 + n], start=(dc == 0), stop=(dc == DC - 1)),
                                 reads=[wq, hnT], writes=[b])
                        f.op("act", lambda e, b=b, p=p: e.mul(out=qT[:, p, c0 - H:c0 - H + n], in_=b[:, 0:n], mul=0.125),
                             reads=[b], writes=[qT])
                b = B[6 + bi % 2]; bi += 1
                self.proj_fm(wk, hnT, c0, n, b)
                f.op("dve", lambda e, b=b: e.tensor_copy(out=kTlo[0:64, c0:c0 + n], in_=b[0:64, 0:n]), reads=[b], writes=[kTlo])
                f.op("dve", lambda e, b=b: e.tensor_copy(out=kThi[64:128, c0:c0 + n], in_=b[64:128, 0:n]), reads=[b], writes=[kThi])
            blocks = list(range(-1, NB + 1))
            for i0 in range(0, len(blocks), 8):
                grp = blocks[i0:i0 + 8]
                b = B[4 + (i0 // 8) % 2]
                for q, blk in enumerate(grp):
                    self.proj_tm(wv, (0, 64), hnT, H + blk * 128, b, q * 64)
                s0 = grp[0] + 1
                f.op("act", lambda e, b=b, s0=s0, n=len(grp): e.copy(out=Vlo[:, s0:s0 + n, 0:64],
                                                                      in_=b[:, 0:n * 64].rearrange("p (a c) -> p a c", c=64)),
                     reads=[b], writes=[Vlo])
                f.op("act", lambda e, b=b, s0=s0, n=len(grp): e.copy(out=Vhi[:, s0:s0 + n, 64:128],
                                                                      in_=b[:, 0:n * 64].rearrange("p (a c) -> p a c", c=64)),
                     reads=[b], writes=[Vhi])
            grp_cnt = {}

            def unit(it, p, qb, hh):
                kbs = (qb - 1, qb, qb + 1)
                nk = 384
                k0 = H + kbs[0] * 128
                bo = B[6 + qb % 2]
                head = 4 * g + 2 * p + hh
                bs = B[it % AW]
                bt = B[AW + it % AW]
                pp_, pT_, sm_, dg_ = pn[it % AW], pT[it % AW], sm[it % AW], dgs[it % AW]
                kT_ = kTlo if hh == 0 else kThi
                f.op("pe", lambda e: e.matmul(bs[:, 0:nk], lhsT=self.identb[:], rhs=self.bhi[:, head, :], start=True, stop=False),
                     reads=[self.identb, self.bhi], writes=[bs])
                f.op("pe", lambda e: e.matmul(bs[:, 0:nk], lhsT=self.identb[:], rhs=self.blo[:, head, :], start=False, stop=False),
                     reads=[self.identb, self.blo], writes=[bs])
                f.op("pe", lambda e: e.matmul(bs[:, 0:nk], lhsT=qT[:, p, qb * 128:(qb + 1) * 128],
                                              rhs=kT_[:, k0:k0 + nk], start=False, stop=True),
                     reads=[qT, kT_], writes=[bs])
                yield
                if qb == 0:
                    f.op("dve", lambda e: e.tensor_scalar(out=bs[:, 0:128], in0=bs[:, 0:128], scalar1=self.fl(2, s), scalar2=None,
                                                          op0=ALU.add), reads=[bs, self.flags], writes=[bs])
                    yield
                if qb == NB - 1:
                    f.op("dve", lambda e: e.tensor_scalar(out=bs[:, 256:384], in0=bs[:, 256:384], scalar1=self.fl(3, s), scalar2=None,
                                                          op0=ALU.add), reads=[bs, self.flags], writes=[bs])
                    yield
                f.op("dve", lambda e: e.tensor_reduce(out=sm_[:, 0:1], in_=bs[:, 0:nk], axis=AX.X, op=ALU.max),
                     reads=[bs], writes=[sm_])
                yield
                f.op("dve", lambda e: e.tensor_scalar(out=sm_[:, 1:2], in0=sm_[:, 0:1], scalar1=ec["sink"][:, head:head + 1],
                                                      scalar2=-1.0, op0=ALU.max, op1=ALU.mult),
                     reads=[sm_, ec["sink"]], writes=[sm_])
                yield
                f.op("act", lambda e: e.activation(out=pp_[:, 0:nk], in_=bs[:, 0:nk], func=AF.Exp, bias=sm_[:, 1:2], scale=1.0,
                                                   accum_out=sm_[:, 2:3]), reads=[bs, sm_], writes=[pp_, sm_])
                yield
                f.op("act", lambda e: e.activation(out=sm_[:, 3:4], in_=sm_[:, 1:2], func=AF.Exp,
                                                   bias=ec["sink"][:, head:head + 1], scale=1.0),
                     reads=[sm_, ec["sink"]], writes=[sm_])
                yield
                f.op("dve", lambda e: e.tensor_tensor(out=sm_[:, 4:5], in0=sm_[:, 2:3], in1=sm_[:, 3:4], op=ALU.add),
                     reads=[sm_], writes=[sm_])
                yield
                f.op("dve", lambda e: e.reciprocal(out=sm_[:, 5:6], in_=sm_[:, 4:5]), reads=[sm_], writes=[sm_])
                yield
                f.op("dve", lambda e: e.tensor_scalar(out=dg_[:], in0=self.identb[:], scalar1=sm_[:, 5:6], scalar2=None, op0=ALU.mult),
                     reads=[self.identb, sm_], writes=[dg_])
                yield
                for jb in range(3):
                    f.op("pe", lambda e, jb=jb: e.matmul(bt[:, jb * 128:(jb + 1) * 128], lhsT=pp_[:, jb * 128:(jb + 1) * 128], rhs=dg_[:],
                                                         start=True, stop=True), reads=[pp_, dg_], writes=[bt])
                yield
                f.op("act", lambda e: e.copy(out=pT_[:, 0:3, :], in_=bt[:, 0:nk].rearrange("p (a b) -> p a b", b=128)),
                     reads=[bt], writes=[pT_])
                yield
                V_ = Vlo if hh == 0 else Vhi
                for jb, kb in enumerate(kbs):
                    c = grp_cnt.get((p, qb), 0)
                    grp_cnt[(p, qb)] = c + 1
                    f.op("pe", lambda e, jb=jb, kb=kb, c=c: e.matmul(
                        bo[:, 0:128], lhsT=V_[:, kb + 1, :], rhs=pT_[:, jb, :], start=(c == 0), stop=(c == 5)),
                        reads=[V_, pT_], writes=[bo])
                yield
                if grp_cnt[(p, qb)] == 6:
                    f.op("act", lambda e: e.copy(out=oTa[p][:, qb * 128:(qb + 1) * 128], in_=bo[:, 0:128]),
                         reads=[bo], writes=[oTa[p]])
                    if qb == NB - 1:
                        f.dma("sp", self.oT[s][:, 2 * g + p, :], oTa[p][:], reads=[oTa[p]], writes=[self.oT[s]])

            def units():
                it = 0
                for p in range(2):
                    for qb in range(NB):
                        for hh in range(2):
                            yield unit(it, p, qb, hh)
                            it += 1
            lockstep(units(), AW)

    def ssd_F(self, l, s, hnT, H, g, ec):
        f, cfg = self.f, self.cfg
        S = cfg.S
        NT = S // 128
        j = l // 2
        B = self.banks
        win = self.W["ev_w_in"]
        TC = min(256, S)
        sp = self.spill[(s, g)]
        with f.scope():
            wB = f.sbuf("swB", [128, DC, 128], BF16)
            wC = f.sbuf("swC", [128, DC, 128], BF16)
            wdt = f.sbuf("swdt", [128, DC, 16], BF16)
            xs_tm = f.sbuf("sxs", [128, NT, 512], BF16)
            BT = f.sbuf("sBT", [128, S], BF16)
            CT = f.sbuf("sCT", [128, S], BF16)
            B_tm = f.sbuf("sBtm", [128, NT, 128], BF16)
            dt = f.sbuf("sdt", [128, NT, 16], F32)
            dtA = f.sbuf("sdtA", [128, NT, 16], F32)
            y_acc = f.sbuf("syacc", [128, NT, 512], F32)

            def wsl(c0, n):
                return win[j, :, c0:c0 + n].rearrange("(c p) n -> p c n", p=128)
            f.op("pool", lambda e: e.memset(wB[:], 0.0), writes=[wB])
            f.op("pool", lambda e: e.memset(wC[:], 0.0), writes=[wC])
            f.dma("pool", wB[:, :, 0:64], wsl(2816 + g * 64, 64), writes=[wB])
            f.dma("pool", wC[:, :, 0:64], wsl(2944 + g * 64, 64), writes=[wC])
            f.dma("pool", wdt[:, :, 0:8], wsl(3072 + g * 8, 8), writes=[wdt])
            f.dma("pool", wdt[:, :, 8:16], wsl(3088 + g * 8, 8), writes=[wdt])
            if True:
                wx = f.sbuf("swx", [128, DC, 512], BF16)
                f.dma("pool", wx[:], wsl(1792 + g * 512, 512), writes=[wx])
                NW = 3
                acc = [f.sbuf("sacc%d" % i, [128, TC], F32) for i in range(NW)]
                xsT = [f.sbuf("sxsT%d" % i, [128, TC], BF16) for i in range(NW)]
                xcnt = {}

                def cunit(it, c0, cc):
                    b = B[it % NW]
                    a_, xo_ = acc[it % NW], xsT[it % NW]
                    if cc < 4:
                        wblk = wx[:, :, cc * 128:(cc + 1) * 128]
                        wdep = wx
                    else:
                        wdep = wB if cc == 4 else wC
                        wblk = wdep[:, :, :]
                    for dc in range(DC):
                        f.op("pe", lambda e, dc=dc: e.matmul(
                            b[:, 0:TC + 4], lhsT=wblk[:, dc, :], rhs=hnT[:, dc, H + c0 - 2:H + c0 + TC + 2],
                            start=(dc == 0), stop=(dc == DC - 1)), reads=[wdep, hnT], writes=[b])
                    yield
                    if cc < 4:
                        def wcol(k):
                            c = (j * 5 + k) * 10 + g * 4 + cc
                            return self.cwT[:, c:c + 1]
                        bcol = self.cbT[:, j * 10 + g * 4 + cc: j * 10 + g * 4 + cc + 1]
                        cdeps = [self.cwT, self.cbT]
                    else:
                        def wcol(k):
                            return self.cwBC[:, j, cc - 4, g, k:k + 1]
                        bcol = self.cbBC[:, j, cc - 4, g:g + 1]
                        cdeps = [self.cwBC, self.cbBC]
                    f.op("dve", lambda e: e.tensor_scalar(
                        out=a_[:, :], in0=b[:, 2:2 + TC], scalar1=wcol(2), scalar2=bcol, op0=ALU.mult, op1=ALU.add),
                        reads=[b] + cdeps, writes=[a_])
                    yield
                    for k in (0, 1, 3, 4):
                        f.op("dve", lambda e, k=k: e.scalar_tensor_tensor(
                            out=a_[:, :], in0=b[:, k:k + TC], scalar=wcol(k), in1=a_[:, :], op0=ALU.mult, op1=ALU.add),
                            reads=[b, a_] + cdeps, writes=[a_])
                        yield
                    if cc < 4:
                        f.op("act", lambda e: e.activation(out=xo_[:], in_=a_[:], func=AF.Silu), reads=[a_], writes=[xo_])
                        yield
                        for tt in range(TC // 128):
                            t = c0 // 128 + tt
                            bt = B[4 + (t % 2)]
                            btb = bt[:, 0:256].bitcast(BF16)
                            f.op("pe", lambda e, tt=tt, btb=btb: e.transpose(
                                btb[:, cc * 128:(cc + 1) * 128], xo_[:, tt * 128:(tt + 1) * 128], self.identb[:]),
                                reads=[xo_, self.identb], writes=[bt])
                        yield
                        xcnt[c0] = xcnt.get(c0, 0) + 1
                        if xcnt[c0] == 4:
                            for tt in range(TC // 128):
                                t = c0 // 128 + tt
                                bt = B[4 + (t % 2)]
                                btb = bt[:, 0:256].bitcast(BF16)
                                f.op("act", lambda e, btb=btb, t=t: e.copy(out=xs_tm[:, t, :], in_=btb[:, 0:512]), reads=[bt], writes=[xs_tm])
                            yield
                    else:
                        dst = BT if cc == 4 else CT
                        f.op("act", lambda e: e.activation(out=dst[:, c0:c0 + TC], in_=a_[:, :], func=AF.Silu),
                             reads=[a_], writes=[dst])
                        yield
                        if cc == 4:
                            for tt in range(TC // 128):
                                t = c0 // 128 + tt
                                bt = B[6 + (t % 2)]
                                btb = bt[:, 0:64].bitcast(BF16)
                                f.op("pe", lambda e, btb=btb, t=t: e.transpose(btb[:, 0:128], BT[:, t * 128:(t + 1) * 128], self.identb[:]),
                                     reads=[BT, self.identb], writes=[bt])
                                f.op("dve", lambda e, btb=btb, t=t: e.tensor_copy(out=B_tm[:, t, :], in_=btb[:, 0:128]), reads=[bt], writes=[B_tm])
                            yield

                def cunits():
                    it = 0
                    for c0 in range(0, S, TC):
                        for cc in range(6):
                            yield cunit(it, c0, cc)
                            it += 1
                lockstep(cunits(), NW)
            if True:
                t1 = f.sbuf("sdt1", [128, NT, 16], F32)
                t2 = f.sbuf("sdt2", [128, NT, 16], F32)
                b = B[0]
                for t in range(NT):
                    self.proj_tm(wdt, (0, 16), hnT, H + t * 128, b, t * 16)
                bv = b[:, 0:NT * 16].rearrange("p (t x) -> p t x", x=16)
                dtb16 = f.sbuf("sdtb16", [128, 16], F32)
                a16 = f.sbuf("sa16", [128, 16], F32)
                for d in range(2):
                    f.op("pool", lambda e, d=d: e.tensor_copy(out=dtb16[:, d * 8:(d + 1) * 8], in_=ec["dtb"][:, d * 16 + g * 8:d * 16 + g * 8 + 8]),
                         reads=[ec["dtb"]], writes=[dtb16])
                    f.op("pool", lambda e, d=d: e.tensor_copy(out=a16[:, d * 8:(d + 1) * 8], in_=ec["a"][:, d * 16 + g * 8:d * 16 + g * 8 + 8]),
                         reads=[ec["a"]], writes=[a16])
                f.op("dve", lambda e: e.tensor_tensor(out=t1[:], in0=bv, in1=dtb16[:, :].unsqueeze(1).to_broadcast([128, NT, 16]), op=ALU.add),
                     reads=[b, dtb16], writes=[t1])
                f.op("act", lambda e: e.activation(out=t2[:], in_=t1[:], func=AF.Abs), reads=[t1], writes=[t2])
                f.op("act", lambda e: e.activation(out=t2[:], in_=t2[:], func=AF.Exp, scale=-1.0), reads=[t2], writes=[t2])
                f.op("act", lambda e: e.activation(out=t2[:], in_=t2[:], func=AF.Ln, bias=1.0, scale=1.0), reads=[t2], writes=[t2])
                f.op("dve", lambda e: e.tensor_scalar(out=t1[:], in0=t1[:], scalar1=0.0, scalar2=None, op0=ALU.max), reads=[t1], writes=[t1])
                f.op("dve", lambda e: e.tensor_tensor(out=dt[:], in0=t1[:], in1=t2[:], op=ALU.add), reads=[t1, t2], writes=[dt])
                f.op("dve", lambda e: e.tensor_tensor(out=dtA[:], in0=dt[:], in1=a16[:, :].unsqueeze(1).to_broadcast([128, NT, 16]), op=ALU.mult),
                     reads=[dt, a16], writes=[dtA])
            self.ssd_scan(s, g, 0, ec, xs_tm, BT, CT, B_tm, dt, dtA, y_acc)
            f.dma("sp", sp["xs"][:, :, :], xs_tm[:], reads=[xs_tm], writes=[sp["xs"]])
            f.dma("sp", sp["BT"][:, :], BT[:], reads=[BT], writes=[sp["BT"]])
            f.dma("sp", sp["CT"][:, :], CT[:], reads=[CT], writes=[sp["CT"]])
            f.dma("sp", sp["Btm"][:, :, :], B_tm[:], reads=[B_tm], writes=[sp["Btm"]])
            f.dma("sp", sp["dt"][:, :, :], dt[:], reads=[dt], writes=[sp["dt"]])
            f.dma("sp", sp["dtA"][:, :, :], dtA[:], reads=[dtA], writes=[sp["dtA"]])
            f.dma("sp", sp["y"][:, :, :], y_acc[:], reads=[y_acc], writes=[sp["y"]])

    def ssd_scan(self, s, g, d, ec, xs_tm, BT, CT, B_tm, dt, dtA, y_acc):
        f, cfg = self.f, self.cfg
        NT = cfg.S // 128
        B = self.banks
        carry = ec["carry"][g]
        flag = self.fl(d, s)
        if True:
            NP = 3
            cbm = [f.sbuf("scb%d" % i, [128, 128], F32) for i in range(NP)]
            R1s = [f.sbuf("sR1_%d" % i, [128, 8, 128], F32) for i in range(2)]
            arg = [f.sbuf("sarg%d" % i, [128, 8, 128], F32) for i in range(NP)]
            Wt = [f.sbuf("sWt%d" % i, [128, 8, 128], BF16) for i in range(NP)]
            csts = [f.sbuf("scst%d" % i, [128, 8], F32) for i in range(2)]
            od = [f.sbuf("sod%d" % i, [128, 8], F32) for i in range(NP)]
            dcb = [f.sbuf("sdcb%d" % i, [128, 8], F32) for i in range(NP)]
            xc = [f.sbuf("sxc%d" % i, [128, 8, 64], BF16) for i in range(NP)]
            xw = [f.sbuf("sxw%d" % i, [128, 8, 64], BF16) for i in range(NP)]
            ytmp = f.sbuf("sytmp", [128, 8, 64], F32)
            hT = f.sbuf("shT", [128, 8, 64], F32)
            hTb = f.sbuf("shTb", [128, 8, 64], BF16)
            f.op("dve", lambda e: e.tensor_scalar(out=hT[:], in0=carry[:], scalar1=flag, scalar2=None, op0=ALU.mult),
                 reads=[carry, self.flags], writes=[hT])
            f.op("dve", lambda e: e.tensor_scalar(out=hTb[:], in0=carry[:], scalar1=flag, scalar2=None, op0=ALU.mult),
                 reads=[carry, self.flags], writes=[hTb])
            ecol = 127 if d == 0 else 0
            tri = self.tri[d]
            bC, bP0, bP1, bYo, bU = B[0], B[1], B[2], B[6], B[7]
            bYd = (B[3], B[4], B[5])

            def pre(ti):
                t = ti if d == 0 else NT - 1 - ti
                c0 = t * 128
                i2 = ti % NP
                R1, cst = R1s[ti % 2], csts[ti % 2]
                cb_, arg_, Wt_, od_, dcb_, xc_, xw_ = cbm[i2], arg[i2], Wt[i2], od[i2], dcb[i2], xc[i2], xw[i2]
                f.op("pe", lambda e: e.matmul(bC[:, 0:128], lhsT=BT[:, c0:c0 + 128], rhs=CT[:, c0:c0 + 128], start=True, stop=True),
                     reads=[BT, CT], writes=[bC])
                dtA_d = dtA[:, t, d * 8:(d + 1) * 8]
                for h in range(8):
                    f.op("act", lambda e, h=h: e.mul(out=R1[:, h, :], in_=tri[:, :], mul=dtA[:, t, d * 8 + h:d * 8 + h + 1]),
                         reads=[tri, dtA], writes=[R1])
                yield
                f.op("pe", lambda e: e.matmul(bC[:, 128:136], lhsT=tri[:], rhs=dtA_d, start=True, stop=True),
                     reads=[tri, dtA], writes=[bC])
                f.op("dve", lambda e: e.tensor_tensor(out=cb_[:], in0=bC[:, 0:128], in1=tri[:], op=ALU.mult), reads=[bC, tri], writes=[cb_])
                yield
                f.op("dve", lambda e: e.tensor_scalar(out=cst[:], in0=bC[:, 128:136], scalar1=-1.0, scalar2=None, op0=ALU.mult),
                     reads=[bC], writes=[cst])
                for hf, bP in enumerate((bP0, bP1)):
                    f.op("pe", lambda e, hf=hf, bP=bP: e.matmul(bP[:, 0:512], lhsT=self.ones32[:],
                                                                rhs=R1[:, hf * 4:(hf + 1) * 4, :].rearrange("p a b -> p (a b)"),
                                                                start=True, stop=True), reads=[self.ones32, R1], writes=[bP])
                yield
                f.op("act", lambda e: e.activation(out=od_[:], in_=cst[:], func=AF.Exp, scale=-1.0), reads=[cst], writes=[od_])
                yield
                for hf, bP in enumerate((bP0, bP1)):
                    for h4 in range(4):
                        hh_ = hf * 4 + h4
                        f.op("act", lambda e, hh_=hh_, h4=h4, bP=bP: e.activation(
                            out=arg_[:, hh_, :], in_=bP[:, h4 * 128:(h4 + 1) * 128], func=AF.Abs, bias=cst[:, hh_:hh_ + 1], scale=1.0),
                            reads=[bP, cst], writes=[arg_])
                    f.op("act", lambda e, hf=hf, bP=bP: e.activation(
                        out=dcb_[:, hf * 4:(hf + 1) * 4], in_=bP[:, 0:512].rearrange("p (a b) -> p a b", b=128)[:, :, ecol], func=AF.Exp),
                        reads=[bP], writes=[dcb_])
                    yield
                f.op("act", lambda e: e.activation(out=arg_[:], in_=arg_[:], func=AF.Exp, scale=-1.0), reads=[arg_], writes=[arg_])
                yield
                f.op("dve", lambda e: e.tensor_tensor(out=Wt_[:], in0=arg_[:], in1=cb_[:, :].unsqueeze(1).to_broadcast([128, 8, 128]),
                                                      op=ALU.mult), reads=[arg_, cb_], writes=[Wt_])
                f.op("dve", lambda e: e.tensor_tensor(out=xc_[:], in0=xs_tm[:, t, :].rearrange("p (h x) -> p h x", x=64),
                                                      in1=dt[:, t, d * 8:(d + 1) * 8].unsqueeze(2).to_broadcast([128, 8, 64]), op=ALU.mult),
                     reads=[xs_tm, dt], writes=[xc_])
                yield
                f.op("dve", lambda e: e.tensor_tensor(out=xw_[:], in0=xc_[:], in1=arg_[:, :, ecol:ecol + 1].to_broadcast([128, 8, 64]),
                                                      op=ALU.mult), reads=[xc_, arg_], writes=[xw_])
                yield
                bY = bYd[i2]
                for h in range(8):
                    f.op("pe", lambda e, h=h: e.matmul(bY[:, h * 64:(h + 1) * 64], lhsT=Wt_[:, h, :], rhs=xc_[:, h, :],
                                                       start=True, stop=True), reads=[Wt_, xc_], writes=[bY])
                    if h % 2 == 1:
                        yield

            def chain(ti):
                t = ti if d == 0 else NT - 1 - ti
                c0 = t * 128
                i2 = ti % NP
                od_, dcb_, xw_, bY = od[i2], dcb[i2], xw[i2], bYd[i2]
                f.op("pe", lambda e: e.matmul(bYo[:, 0:512], lhsT=CT[:, c0:c0 + 128], rhs=hTb[:].rearrange("p a b -> p (a b)"),
                                              start=True, stop=True), reads=[CT, hTb], writes=[bYo])
                f.op("pe", lambda e: e.matmul(bU[:, 0:512], lhsT=B_tm[:, t, :], rhs=xw_[:].rearrange("p a b -> p (a b)"),
                                              start=True, stop=True), reads=[B_tm, xw_], writes=[bU])
                yield
                f.op("dve", lambda e: e.tensor_tensor(out=hT[:], in0=hT[:], in1=dcb_[:, :].unsqueeze(2).to_broadcast([128, 8, 64]),
                                                      op=ALU.mult), reads=[hT, dcb_], writes=[hT])
                yield
                f.op("dve", lambda e: e.tensor_tensor(out=hT[:], in0=hT[:], in1=bU[:, 0:512].rearrange("p (a b) -> p a b", b=64),
                                                      op=ALU.add), reads=[hT, bU], writes=[hT])
                yield
                f.op("act", lambda e: e.copy(out=hTb[:], in_=hT[:]), reads=[hT], writes=[hTb])
                yield
                f.op("dve", lambda e: e.tensor_tensor(out=ytmp[:], in0=bYo[:, 0:512].rearrange("p (h x) -> p h x", x=64),
                                                      in1=od_[:, :].unsqueeze(2).to_broadcast([128, 8, 64]), op=ALU.mult),
                     reads=[bYo, od_], writes=[ytmp])
                yield
                ya = y_acc[:, t, :].rearrange("p (h x) -> p h x", x=64)
                if d == 1:
                    f.op("dve", lambda e: e.tensor_tensor(out=ytmp[:], in0=ytmp[:], in1=ya, op=ALU.add),
                         reads=[ytmp, y_acc], writes=[ytmp])
                    yield
                f.op("dve", lambda e: e.tensor_tensor(out=ya, in0=bY[:, 0:512].rearrange("p (h x) -> p h x", x=64), in1=ytmp[:],
                                                      op=ALU.add), reads=[bY, ytmp], writes=[y_acc])
                yield

            def pres():
                for ti in range(NT):
                    yield from pre(ti)
                    yield ("done", ti)

            def driver():
                pg = pres()
                done = -1
                while done < 0:
                    if next(pg) is not None:
                        done = 0
                    yield
                for ti in range(NT):
                    cg = chain(ti)
                    alive = True
                    while alive:
                        try:
                            next(cg)
                            yield
                        except StopIteration:
                            alive = False
                        if done < min(ti + 2, NT - 1):
                            r = next(pg)
                            if r is not None:
                                done = r[1]
                            yield
                    while done < min(ti + 1, NT - 1):
                        r = next(pg)
                        if r is not None:
                            done = r[1]
                        yield
            lockstep([driver()], 1)
            f.op("pool", lambda e: e.tensor_copy(out=carry[:], in_=hT[:]), reads=[hT], writes=[carry])

    def ssd_B(self, l, s, hnT, H, g, ec):
        f, cfg = self.f, self.cfg
        S = cfg.S
        NT = S // 128
        j = l // 2
        B = self.banks
        win = self.W["ev_w_in"]
        sp = self.spill[(s, g)]
        with f.scope():
            wz = f.sbuf("swz", [128, DC, 512], BF16)
            xs_tm = f.sbuf("sxs", [128, NT, 512], BF16)
            BT = f.sbuf("sBT", [128, S], BF16)
            CT = f.sbuf("sCT", [128, S], BF16)
            B_tm = f.sbuf("sBtm", [128, NT, 128], BF16)
            dt = f.sbuf("sdt", [128, NT, 16], F32)
            dtA = f.sbuf("sdtA", [128, NT, 16], F32)
            y_acc = f.sbuf("syacc", [128, NT, 512], F32)
            f.dma("pool", wz[:], win[j, :, 768 + g * 512:768 + (g + 1) * 512].rearrange("(c p) n -> p c n", p=128), writes=[wz])
            f.dma("sp", xs_tm[:], sp["xs"][:, :, :], reads=[sp["xs"]], writes=[xs_tm])
            f.dma("sp", BT[:], sp["BT"][:, :], reads=[sp["BT"]], writes=[BT])
            f.dma("sp", CT[:], sp["CT"][:, :], reads=[sp["CT"]], writes=[CT])
            f.dma("sp", B_tm[:], sp["Btm"][:, :, :], reads=[sp["Btm"]], writes=[B_tm])
            f.dma("sp", dt[:], sp["dt"][:, :, :], reads=[sp["dt"]], writes=[dt])
            f.dma("sp", dtA[:], sp["dtA"][:, :, :], reads=[sp["dtA"]], writes=[dtA])
            f.dma("sp", y_acc[:], sp["y"][:, :, :], reads=[sp["y"]], writes=[y_acc])
            self.ssd_scan(s, g, 1, ec, xs_tm, BT, CT, B_tm, dt, dtA, y_acc)
            if True:
                zs = [f.sbuf("szs%d" % i, [128, 512], F32) for i in range(2)]
                yn = [f.sbuf("syn%d" % i, [128, 512], BF16) for i in range(2)]
                sq = [f.sbuf("ssq%d" % i, [128, 512], F32) for i in range(2)]
                ssum = f.sbuf("sssum", [128, NT], F32)
                rs = f.sbuf("srs", [128, NT], F32)
                oTs = [f.sbuf("soTs%d" % i, [128, 4, 512], BF16) for i in range(2)]
                D8 = ec["D"][:, g * 8:(g + 1) * 8]

                def gate(t):
                    z_ = zs[t % 2]
                    bz = B[t % 2]
                    self.proj_tm(wz, (0, 512), hnT, H + t * 128, bz, 0)
                    f.op("dve", lambda e: e.tensor_tensor(out=z_[:].rearrange("p (h x) -> p h x", x=64),
                                                          in0=xs_tm[:, t, :].rearrange("p (h x) -> p h x", x=64),
                                                          in1=D8.unsqueeze(2).to_broadcast([128, 8, 64]), op=ALU.mult),
                         reads=[xs_tm, ec["D"]], writes=[z_])
                    yield
                    f.op("dve", lambda e: e.tensor_tensor(out=y_acc[:, t, :], in0=y_acc[:, t, :], in1=z_[:], op=ALU.add),
                         reads=[y_acc, z_], writes=[y_acc])
                    yield
                    f.op("act", lambda e: e.activation(out=z_[:], in_=bz[:, 0:512], func=AF.Silu), reads=[bz], writes=[z_])
                    yield
                    f.op("dve", lambda e: e.tensor_tensor(out=y_acc[:, t, :], in0=y_acc[:, t, :], in1=z_[:], op=ALU.mult),
                         reads=[y_acc, z_], writes=[y_acc])
                    yield
                    f.op("act", lambda e: e.activation(out=sq[t % 2][:], in_=y_acc[:, t, :], func=AF.Square, accum_out=ssum[:, t:t + 1]),
                         reads=[y_acc], writes=[sq[t % 2], ssum])
                    yield
                lockstep((gate(t) for t in range(NT)), 2)
                f.op("act", lambda e: e.activation(out=rs[:], in_=ssum[:], func=AF.Ln, bias=self.eps[:, 0:1], scale=1.0 / 512),
                     reads=[ssum, self.eps], writes=[rs])
                f.op("act", lambda e: e.activation(out=rs[:], in_=rs[:], func=AF.Exp, scale=-0.5), reads=[rs], writes=[rs])

                def fin(t):
                    y_ = yn[t % 2]
                    f.op("dve", lambda e: e.scalar_tensor_tensor(out=y_[:], in0=y_acc[:, t, :], scalar=rs[:, t:t + 1],
                                                                 in1=ec["nw"][:, g * 512:(g + 1) * 512], op0=ALU.mult, op1=ALU.mult),
                         reads=[y_acc, rs, ec["nw"]], writes=[y_])
                    yield
                    bt = B[2 + t % 2]
                    btb = bt[:, 0:256].bitcast(BF16)
                    for c in range(4):
                        f.op("pe", lambda e, c=c: e.transpose(btb[:, c * 128:(c + 1) * 128], y_[:, c * 128:(c + 1) * 128], self.identb[:]),
                             reads=[y_, self.identb], writes=[bt])
                    yield
                    o_ = oTs[(t // 4) % 2]
                    f.op("act", lambda e: e.copy(out=o_[:, :, (t % 4) * 128:(t % 4 + 1) * 128],
                                                 in_=btb[:, 0:512].rearrange("p (c x) -> p c x", x=128)),
                         reads=[bt], writes=[o_])
                    yield
                for t0 in range(0, NT, 4):
                    n4 = min(4, NT - t0)
                    lockstep((fin(t) for t in range(t0, t0 + n4)), 2)
                    o_ = oTs[(t0 // 4) % 2]
                    f.dma("sp", self.oT[s][:, 4 + 4 * g:8 + 4 * g, t0 * 128:(t0 + n4) * 128], o_[:, :, 0:n4 * 128], reads=[o_], writes=[self.oT[s]])

    def phase_c(self, l, s, cur, nxt, last):
        f, cfg = self.f, self.cfg
        S, TS = cfg.S, cfg.TS
        j = l // 2
        even = (l % 2 == 0)
        nO = 12 if even else 8
        wout = self.W["ev_w_out"] if even else self.W["od_w_out"]
        B = self.banks
        with f.scope():
            wo = f.sbuf("wo", [128, nO, D], BF16)
            xt = f.sbuf("xt", [128, DC, TS], F32)
            ot = f.sbuf("ot", [128, nO, TS], BF16)
            mt = f.sbuf("mt", [128, DC, TS], F32)
            sq = f.sbuf("sq", [128, DC, TS], BF16)
            rstd = f.sbuf("rstd", [128, TS], F32)
            hn2 = f.sbuf("hn2", [128, DC, TS], BF16)
            act = f.sbuf("actT", [128, NFC, TS], BF16)
            sg = [f.sbuf("sg", [128, TS], BF16) for _ in range(2)]
            wg = [f.sbuf("wg", [128, DC, 128], BF16) for _ in range(3)]
            wu = [f.sbuf("wu", [128, DC, 128], BF16) for _ in range(3)]
            wd = [f.sbuf("wd", [128, NFC, 128], BF16) for _ in range(2)]
            yo = [f.sbuf("yo", [128, D], F32) for _ in range(2)] if last else None
            if cfg.mixers:
                f.dma("sp", wo[:], self.wo_t[:, 0:nO * D].rearrange("p (c d) -> p c d", d=D), reads=[self.wo_t], writes=[wo])
            wi = 0
            for st_i in range(S // TS):
                c0 = st_i * TS
                f.dma("sp", xt[:], cur[s][:, :, c0:c0 + TS], reads=[cur[s]], writes=[xt])
                if cfg.mixers:
                    f.dma("sp", ot[:], self.oT[s][:, 0:nO, c0:c0 + TS], reads=[self.oT[s]], writes=[ot])
                    for dc in range(DC):
                        b = B[dc % 2]
                        for oc in range(nO):
                            f.op("pe", lambda e, b=b, oc=oc, dc=dc: e.matmul(
                                b[:, 0:TS], lhsT=wo[:, oc, dc * 128:(dc + 1) * 128], rhs=ot[:, oc, :],
                                start=(oc == 0), stop=(oc == nO - 1)), reads=[wo, ot], writes=[b])
                        f.op("act", lambda e, b=b, dc=dc: e.copy(out=mt[:, dc, :], in_=b[:, 0:TS]),
                             reads=[b], writes=[mt])
                        f.op("act", lambda e, b=b, dc=dc: e.activation(out=sq[:, dc, :], in_=b[:, 0:TS], func=AF.Square),
                             reads=[b], writes=[sq])
                    self.rstd_from_sq(sq, rstd, B[2], TS)
                    for dc in range(DC):
                        f.op("dve", lambda e, dc=dc: e.scalar_tensor_tensor(
                            out=mt[:, dc, :], in0=mt[:, dc, :], scalar=self.gcol(l, 1, dc), in1=rstd[:, :],
                            op0=ALU.mult, op1=ALU.mult), reads=[mt, self.gall, rstd], writes=[mt])
                    for dc in range(DC):
                        f.op("dve", lambda e, dc=dc: e.tensor_tensor(out=xt[:, dc, :], in0=xt[:, dc, :], in1=mt[:, dc, :], op=ALU.add),
                             reads=[xt, mt], writes=[xt])
                f.op("act", lambda e: e.activation(out=sq[:], in_=xt[:], func=AF.Square), reads=[xt], writes=[sq])
                self.rstd_from_sq(sq, rstd, B[2], TS)
                for dc in range(DC):
                    f.op("dve", lambda e, dc=dc: e.scalar_tensor_tensor(
                        out=hn2[:, dc, :], in0=xt[:, dc, :], scalar=self.gcol(l, 2, dc), in1=rstd[:, :],
                        op0=ALU.mult, op1=ALU.mult), reads=[xt, self.gall, rstd], writes=[hn2])
                for fc in range(NFC):
                    g_, u_ = wg[wi % 3], wu[wi % 3]
                    wi += 1
                    f.dma("sp", g_[:], self.wg_t[fc].rearrange("p (c n) -> p c n", n=128), reads=[self.wg_t], writes=[g_])
                    f.dma("sp", u_[:], self.wu_t[fc].rearrange("p (c n) -> p c n", n=128), reads=[self.wu_t], writes=[u_])
                    bg, bu = B[4 + (fc % 2)], B[6 + (fc % 2)]
                    for dc in range(DC):
                        f.op("pe", lambda e, dc=dc, g_=g_, bg=bg: e.matmul(
                            bg[:, 0:TS], lhsT=g_[:, dc, :], rhs=hn2[:, dc, :], start=(dc == 0), stop=(dc == DC - 1)),
                            reads=[g_, hn2], writes=[bg])
                    for dc in range(DC):
                        f.op("pe", lambda e, dc=dc, u_=u_, bu=bu: e.matmul(
                            bu[:, 0:TS], lhsT=u_[:, dc, :], rhs=hn2[:, dc, :], start=(dc == 0), stop=(dc == DC - 1)),
                            reads=[u_, hn2], writes=[bu])
                    sgt = sg[fc % 2]
                    f.op("act", lambda e, sgt=sgt, bg=bg: e.activation(out=sgt[:], in_=bg[:, 0:TS], func=AF.Silu),
                         reads=[bg], writes=[sgt])
                    f.op("dve", lambda e, sgt=sgt, bu=bu, fc=fc: e.tensor_tensor(
                        out=act[:, fc, :], in0=sgt[:], in1=bu[:, 0:TS], op=ALU.mult),
                        reads=[sgt, bu], writes=[act])
                for dc in range(DC):
                    d_ = wd[dc % 2]
                    f.dma("sp", d_[:], self.wd_t[dc].rearrange("p (c n) -> p c n", n=128), reads=[self.wd_t], writes=[d_])
                    b = B[dc % 2]
                    for fc in range(NFC):
                        f.op("pe", lambda e, b=b, fc=fc, d_=d_: e.matmul(
                            b[:, 0:TS], lhsT=d_[:, fc, :], rhs=act[:, fc, :], start=(fc == 0), stop=(fc == NFC - 1)),
                            reads=[d_, act], writes=[b])
                    f.op("act", lambda e, b=b, dc=dc: e.copy(out=mt[:, dc, :], in_=b[:, 0:TS]), reads=[b], writes=[mt])
                    f.op("act", lambda e, b=b, dc=dc: e.activation(out=sq[:, dc, :], in_=b[:, 0:TS], func=AF.Square),
                         reads=[b], writes=[sq])
                self.rstd_from_sq(sq, rstd, B[2], TS)
                for dc in range(DC):
                    f.op("dve", lambda e, dc=dc: e.scalar_tensor_tensor(
                        out=mt[:, dc, :], in0=mt[:, dc, :], scalar=self.gcol(l, 3, dc), in1=rstd[:, :],
                        op0=ALU.mult, op1=ALU.mult), reads=[mt, self.gall, rstd], writes=[mt])
                for dc in range(DC):
                    f.op("dve", lambda e, dc=dc: e.tensor_tensor(out=xt[:, dc, :], in0=xt[:, dc, :], in1=mt[:, dc, :], op=ALU.add),
                         reads=[xt, mt], writes=[xt])
                if not last:
                    f.dma("sp", nxt[s][:, :, c0:c0 + TS], xt[:], reads=[xt], writes=[nxt[s]])
                else:
                    for tt in range(TS // 128):
                        y = yo[tt % 2]
                        for half in range(2):
                            b = B[2 + half]
                            for q in range(4):
                                dc = half * 4 + q
                                f.op("pe", lambda e, b=b, q=q, dc=dc, tt=tt: e.transpose(
                                    b[:, q * 128:(q + 1) * 128], xt[:, dc, tt * 128:(tt + 1) * 128], self.ident[:]),
                                    reads=[xt, self.ident], writes=[b])
                            if half == 0:
                                f.op("act", lambda e, b=b, y=y: e.copy(out=y[:, 0:512], in_=b[:]), reads=[b], writes=[y])
                            else:
                                f.op("dve", lambda e, b=b, y=y: e.tensor_copy(out=y[:, 512:1024], in_=b[:]),
                                     reads=[b], writes=[y])
                        r0 = c0 + tt * 128
                        f.dma("sp", self.y_out[s, r0:r0 + 128, :], y[:], reads=[y])


_CACHE = {}

NSEG = 8
ASSIGN = [[("s", i) for i in range(8)]]
_p = 0
for _c in range(1, 8):
    _n = 3 if _c <= 2 else 2
    ASSIGN.append([("p", _p + i) for i in range(_n)] + [None] * (NSEG - _n))
    _p += _n


def make_flags(chain_left):
    n = len(chain_left)
    cl = np.asarray(chain_left, dtype=np.float32)
    cr = np.concatenate([cl[1:], np.zeros(1, np.float32)])
    return np.concatenate([cl, cr, NEG * (1 - cl), NEG * (1 - cr)]).astype(np.float32)[None, :]


def _get_nc(cfg_key, cfg):
    if cfg_key not in _CACHE:
        _CACHE[cfg_key] = KB(cfg).build()
    return _CACHE[cfg_key]


def kernel(**inputs):
    cfg = Cfg()
    nc = _get_nc("full", cfg)
    xp = np.ascontiguousarray(inputs["x_prompt"], dtype=np.float32)
    xs = np.ascontiguousarray(inputs["x_sample"], dtype=np.float32)
    S = cfg.S
    in_maps = []
    hc = host_consts()
    wmap = {name: np.ascontiguousarray(inputs[name], dtype=np.float32) for name, _ in W_SPECS}
    for c in range(8):
        x_in = np.zeros((NSEG, S, D), np.float32)
        chain = [0] * NSEG
        for i, a in enumerate(ASSIGN[c]):
            if a is None:
                continue
            if a[0] == "s":
                x_in[i] = xs[0, a[1] * S:(a[1] + 1) * S]
                chain[i] = 1 if a[1] > 0 else 0
            else:
                x_in[i] = xp[a[1]]
        m = {"x_in": x_in, "flags": make_flags(chain)}
        m.update(hc)
        m.update(wmap)
        in_maps.append(m)
    res = run_bass_kernel_spmd(nc, in_maps, core_ids=list(range(8)))
    yp = np.empty_like(xp)
    ys = np.empty_like(xs)
    for c in range(8):
        y = np.asarray(res.results[c]["y_out"])
        for i, a in enumerate(ASSIGN[c]):
            if a is None:
                continue
            if a[0] == "s":
                ys[0, a[1] * S:(a[1] + 1) * S] = y[i]
            else:
                yp[a[1]] = y[i]
    return (yp, ys)
```

```python
import contextlib
import numpy as np
import concourse.bass as bass
import concourse.mybir as mybir
from concourse.bass_utils import run_bass_kernel_spmd

F32 = mybir.dt.float32
BF16 = mybir.dt.bfloat16
ALU = mybir.AluOpType
AF = mybir.ActivationFunctionType
AX = mybir.AxisListType

D = 1024
DC = 8
DFF = 2816
NFC = 22
EV_IN = 3104
OD_IN = 5120
EPS = 1e-6
NEG = -30000.0
LSUB = 64
NSB = 128 // LSUB


class Buf:
    __slots__ = ("t", "name", "w", "r", "excl")

    def __init__(self, t, name, excl=False):
        self.t = t
        self.name = name
        self.w = None
        self.r = {}
        self.excl = excl

    def __getitem__(self, idx):
        return self.t[idx]


class _Eng:
    def __init__(self, h, key, sem):
        self.h = h
        self.key = key
        self.sem = sem
        self.n = 0
        self.waited = {}


class FW:
    def __init__(self, nc, stack, same_engine_sync=("act", "dve", "pool")):
        self.nc = nc
        self.stack = stack
        self.sems = {}
        self.eng = {}
        for key, h in (("pe", nc.tensor), ("act", nc.scalar), ("dve", nc.vector),
                       ("pool", nc.gpsimd), ("sp", nc.sync)):
            s = stack.enter_context(nc.semaphore("s_" + key))
            self.sems[key] = s
            self.eng[key] = _Eng(h, key, s)
        self.same_engine_sync = set(same_engine_sync)
        n_slots = {"sp": 16, "act": 4, "pool": 12}
        self.slots = {}
        self.slot_rr = {}
        for q, n in n_slots.items():
            lst = []
            for i in range(n):
                k = "d_%s%d" % (q, i)
                s = stack.enter_context(nc.semaphore(k))
                self.sems[k] = s
                lst.append([k, 0])
            self.slots[q] = lst
            self.slot_rr[q] = 0
        self.n_wait = 0
        self.n_inst = 0
        self.uid = 0

    def sbuf(self, name, shape, dtype):
        self.uid += 1
        nm = "%s_%d" % (name, self.uid)
        t = self.stack.enter_context(self.nc.sbuf_tensor(nm, list(shape), dtype))
        return Buf(t, nm)

    def psum(self, name, shape, dtype):
        t = self.stack.enter_context(self.nc.psum_tensor(name, list(shape), dtype))
        return Buf(t, name, excl=True)

    def dram(self, name, shape, dtype, kind="Internal"):
        t = self.nc.dram_tensor(name, list(shape), dtype, kind=kind)
        return Buf(t, name)

    @contextlib.contextmanager
    def scope(self):
        old = self.stack
        self.stack = contextlib.ExitStack()
        try:
            yield
        finally:
            self.barrier()
            self.stack.close()
            self.stack = old

    def _wait(self, E, key, val):
        if E.waited.get(key, 0) >= val:
            return
        E.h.wait_ge(self.sems[key], val)
        E.waited[key] = val
        self.n_wait += 1

    def _deps(self, E, reads, writes):
        deps = {}
        for b in reads:
            if b.w is not None:
                k, v = b.w
                if deps.get(k, 0) < v:
                    deps[k] = v
            if b.excl:
                for k, v in b.r.items():
                    if k != E.key and deps.get(k, 0) < v:
                        deps[k] = v
        for b in writes:
            if b.w is not None:
                k, v = b.w
                if deps.get(k, 0) < v:
                    deps[k] = v
            for k, v in b.r.items():
                if deps.get(k, 0) < v:
                    deps[k] = v
        for k, v in deps.items():
            if k == E.key and k not in self.same_engine_sync:
                continue
            self._wait(E, k, v)

    def _mark(self, ev, reads, writes):
        k, v = ev
        for b in reads:
            if b.r.get(k, 0) < v:
                b.r[k] = v
        for b in writes:
            b.w = ev
            b.r = {}

    def op(self, eng, fn, reads=(), writes=()):
        E = self.eng[eng]
        self._deps(E, reads, writes)
        inst = fn(E.h)
        E.n += 1
        inst.then_inc(E.sem, 1)
        self.n_inst += 1
        self._mark((E.key, E.n), reads, writes)
        return inst

    def dma(self, q, out, in_, reads=(), writes=(), **kw):
        E = self.eng[q]
        self._deps(E, reads, writes)
        lst = self.slots[q]
        i = self.slot_rr[q]
        self.slot_rr[q] = (i + 1) % len(lst)
        slot = lst[i]
        if slot[1] > 0:
            self._wait(E, slot[0], 16 * slot[1])
        inst = E.h.dma_start(out=out, in_=in_, **kw)
        slot[1] += 1
        inst.then_inc(self.sems[slot[0]], 16)
        self.n_inst += 1
        self._mark((slot[0], 16 * slot[1]), reads, writes)
        return inst

    def collective(self, kind, op, rg, in_ap, out_ap, reads=(), writes=()):
        E = self.eng["pool"]
        self._deps(E, reads, writes)
        lst = self.slots["pool"]
        i = self.slot_rr["pool"]
        self.slot_rr["pool"] = (i + 1) % len(lst)
        slot = lst[i]
        if slot[1] > 0:
            self._wait(E, slot[0], 16 * slot[1])
        inst = E.h.collective_compute(kind, op, replica_groups=rg, ins=[in_ap], outs=[out_ap])
        slot[1] += 1
        inst.then_inc(self.sems[slot[0]], 16)
        self.n_inst += 1
        self._mark((slot[0], 16 * slot[1]), reads, writes)
        return inst

    def barrier(self):
        S = self.eng["sp"]
        for q, lst in self.slots.items():
            for k, c in lst:
                if c > 0:
                    self._wait(S, k, 16 * c)
        for key in ("pe", "act", "dve", "pool"):
            X = self.eng[key]
            if X.n > 0:
                self._wait(S, key, X.n)
        S.h.sem_inc(S.sem, 1)
        S.n += 1
        for key in ("pe", "act", "dve", "pool"):
            X = self.eng[key]
            self._wait(X, "sp", S.n)
            for k2 in ("pe", "act", "dve", "pool"):
                X.waited[k2] = max(X.waited.get(k2, 0), self.eng[k2].n)
            for q, lst in self.slots.items():
                for k, c in lst:
                    X.waited[k] = max(X.waited.get(k, 0), 16 * c)

    def finish(self):
        self.barrier()


def lockstep(gen_iter, width=2):
    it = iter(gen_iter)
    active = []
    done = False
    while True:
        while not done and len(active) < width:
            try:
                active.append(next(it))
            except StopIteration:
                done = True
        if not active:
            return
        for g in list(active):
            try:
                next(g)
            except StopIteration:
                active.remove(g)


def lockstep_gen(gens):
    active = list(gens)
    while active:
        for g in list(active):
            try:
                next(g)
                yield
            except StopIteration:
                active.remove(g)


class Cfg:
    def __init__(self, S=2048, nseg=8, layers=(0, 1, 2, 3), TS=512, mixers=True):
        self.S = S
        self.nseg = nseg
        self.layers = tuple(layers)
        self.TS = min(TS, S)
        self.mixers = mixers


W_SPECS = [
    ("norm_gains", [4, 4, 1024]), ("t5_bias", [32, 8]), ("ev_w_in", [2, 1024, 3104]),
    ("attn_sink", [2, 8]), ("ssd_conv_w", [2, 5, 1280]), ("ssd_conv_b", [2, 1280]),
    ("ssd_a_log", [2, 2, 16]), ("ssd_dt_bias", [2, 2, 16]), ("ssd_d", [2, 16]),
    ("ssd_norm_w", [2, 1024]), ("ev_w_out", [2, 1536, 1024]), ("od_w_in", [2, 1024, 5120]),
    ("hg_lower_bounds", [2, 1024]), ("hg_norm_w", [2, 128]), ("od_w_out", [2, 1024, 1024]),
    ("ffn_w_gate", [4, 1024, 2816]), ("ffn_w_up", [4, 1024, 2816]), ("ffn_w_down", [4, 2816, 1024]),
]


def t5_bucket_np(rel):
    half = 16
    max_exact = 8
    n = np.abs(rel)
    large = max_exact + (np.log(np.maximum(n, 1) / max_exact) / np.log(128 / max_exact) * (half - max_exact)).astype(np.int32)
    large = np.minimum(large, half - 1)
    return ((rel > 0).astype(np.int32) * half + np.where(n < max_exact, n, large)).astype(np.int32)


def host_consts():
    qi = np.arange(128)[:, None]
    kj = np.arange(384)[None, :] - 128
    rel = kj - qi
    valid = np.abs(rel) <= 128
    bucket = np.where(valid, t5_bucket_np(rel), -1).astype(np.float32)
    maskneg = np.where(valid, 0.0, NEG).astype(np.float32)
    return {"c_bucket": bucket, "c_maskneg": maskneg}

class KB:
    def __init__(self, cfg):
        self.cfg = cfg
        self.nc = bass.Bass("TRN2", target_bir_lowering=False)

    def build(self):
        cfg = self.cfg
        nc = self.nc
        S, nseg = cfg.S, cfg.nseg
        NT = S // 128
        self.x_in = nc.dram_tensor("x_in", [nseg, S, D], F32, kind="ExternalInput")
        self.y_out = nc.dram_tensor("y_out", [nseg, S, D], F32, kind="ExternalOutput")
        self.W = {}
        for name, shape in W_SPECS:
            self.W[name] = nc.dram_tensor(name, shape, F32, kind="ExternalInput")
        self.c_bucket = nc.dram_tensor("c_bucket", [128, 384], F32, kind="ExternalInput")
        self.c_maskneg = nc.dram_tensor("c_maskneg", [128, 384], F32, kind="ExternalInput")
        self.flags_in = nc.dram_tensor("flags", [1, 4 * nseg], F32, kind="ExternalInput")
        with contextlib.ExitStack() as st:
            import os as _os
            _ses = tuple(x for x in _os.environ.get("SES", "act,dve,pool").split(",") if x)
            f = FW(nc, st, same_engine_sync=_ses)
            self.f = f
            xa = [f.dram("xTa%d" % s, [128, DC, S], F32) for s in range(nseg)]
            xb = [f.dram("xTb%d" % s, [128, DC, S], F32) for s in range(nseg)]
            self.oT = [f.dram("oT%d" % s, [128, 12, S], BF16) for s in range(nseg)]
            self.ofwd = [f.dram("ofwd%d" % s, [128, 8, NT * 128], F32) for s in range(nseg)]
            self.qsp = [f.dram("qsp%d" % s, [128, 8, S], F32) for s in range(nseg)]
            self.hnsp = [f.dram("hnsp%d" % s, [128, DC, S], BF16) for s in range(nseg)]
            self.vsp = [f.dram("vsp%d" % s, [128, 8, NT * 128], BF16) for s in range(nseg)]
            self.spill = {}
            if cfg.mixers and any(l % 2 == 0 for l in cfg.layers):
                for s in range(nseg):
                    for g in range(2):
                        k = "%d_%d" % (s, g)
                        self.spill[(s, g)] = {
                            "xs": f.dram("sp_xs" + k, [128, NT, 512], BF16),
                            "BT": f.dram("sp_BT" + k, [128, S], BF16),
                            "CT": f.dram("sp_CT" + k, [128, S], BF16),
                            "Btm": f.dram("sp_Btm" + k, [128, NT, 128], BF16),
                            "dt": f.dram("sp_dt" + k, [128, NT, 16], F32),
                            "dtA": f.dram("sp_dtA" + k, [128, NT, 16], F32),
                            "y": f.dram("sp_y" + k, [128, NT, 512], F32),
                        }
            self.wg_t = f.dram("wg_t", [NFC, 128, DC * 128], BF16)
            self.wu_t = f.dram("wu_t", [NFC, 128, DC * 128], BF16)
            self.wd_t = f.dram("wd_t", [DC, 128, NFC * 128], BF16)
            self.wo_t = f.dram("wo_t", [128, 12 * D], BF16)
            self.banks = [f.psum("bank%d" % i, [128, 512], F32) for i in range(8)]
            self.consts()
            self.flags = f.sbuf("flags", [128, 4 * nseg], F32)
            f.dma("sp", self.flags[:], self.flags_in[0:1, :].partition_broadcast(128), writes=[self.flags])
            if cfg.mixers and any(l % 2 == 0 for l in cfg.layers):
                self.even_global_consts()
            self.phase_x0(xa)
            cur, nxt = xa, xb
            for li, l in enumerate(cfg.layers):
                last = li == len(cfg.layers) - 1
                self.convert_weights(l)
                with f.scope():
                    if cfg.mixers:
                        lc = self.layer_consts(l)
                        self.reset_carry(lc)
                        for s in range(nseg):
                            self.mixer_pass(l, s, 0, cur, lc)
                        self.reset_carry(lc)
                    for s in reversed(range(nseg)):
                        if cfg.mixers:
                            self.mixer_pass(l, s, 1, cur, lc)
                        self.phase_c(l, s, cur, nxt, last)
                cur, nxt = nxt, cur
            f.finish()
        return nc

    def convert_weights(self, l):
        f, cfg = self.f, self.cfg
        j = l // 2
        even = (l % 2 == 0)
        nO = 12 if even else 8
        wout = self.W["ev_w_out"] if even else self.W["od_w_out"]
        with f.scope():
            st8 = [f.sbuf("cv8_%d" % i, [128, DC, 128], BF16) for i in range(4)]
            st22 = [f.sbuf("cv22_%d" % i, [128, NFC, 128], BF16) for i in range(2)]
            sto = [f.sbuf("cvo_%d" % i, [128, 2, D], BF16) for i in range(2)]
            k = 0
            for fc in range(NFC):
                for src, dst in ((self.W["ffn_w_gate"], self.wg_t), (self.W["ffn_w_up"], self.wu_t)):
                    t_ = st8[k % 4]
                    k += 1
                    f.dma("pool", t_[:], src[l, :, fc * 128:(fc + 1) * 128].rearrange("(c p) n -> p c n", p=128), writes=[t_])
                    f.dma("sp", dst[fc].rearrange("p (c n) -> p c n", n=128), t_[:], reads=[t_], writes=[dst])
            for dc in range(DC):
                t_ = st22[dc % 2]
                f.dma("pool", t_[:], self.W["ffn_w_down"][l, :, dc * 128:(dc + 1) * 128].rearrange("(c p) n -> p c n", p=128), writes=[t_])
                f.dma("sp", self.wd_t[dc].rearrange("p (c n) -> p c n", n=128), t_[:], reads=[t_], writes=[self.wd_t])
            if cfg.mixers:
                for o2 in range(nO // 2):
                    t_ = sto[o2 % 2]
                    f.dma("pool", t_[:], wout[j, o2 * 256:(o2 + 1) * 256, :].rearrange("(c p) d -> p c d", p=128), writes=[t_])
                    f.dma("sp", self.wo_t[:, o2 * 2 * D:(o2 + 1) * 2 * D].rearrange("p (c d) -> p c d", d=D), t_[:], reads=[t_],
                          writes=[self.wo_t])

    def fl(self, kind, s):
        c = kind * self.cfg.nseg + s
        return self.flags[:, c:c + 1]

    def consts(self):
        f = self.f
        self.ident = f.sbuf("ident", [128, 128], F32)
        self.identb = f.sbuf("identb", [128, 128], BF16)
        self.onesb = f.sbuf("onesb", [128, 128], BF16)
        self.eps = f.sbuf("eps", [128, 1], F32)
        self.gall = f.sbuf("gall", [128, 128], F32)
        ident, identb = self.ident, self.identb
        f.op("pool", lambda e: e.memset(ident[:], 0.0), writes=[ident])
        f.op("pool", lambda e: e.affine_select(out=ident[:], in_=ident[:], pattern=[[-1, 128]],
                                               compare_op=ALU.not_equal, fill=1.0, base=0,
                                               channel_multiplier=1), reads=[ident], writes=[ident])
        f.op("dve", lambda e: e.tensor_copy(out=identb[:], in_=ident[:]), reads=[ident], writes=[identb])
        f.op("pool", lambda e: e.memset(self.onesb[:], 1.0), writes=[self.onesb])
        f.op("pool", lambda e: e.memset(self.eps[:], EPS), writes=[self.eps])
        with f.scope():
            tmp = f.sbuf("gtmp", [128, 128], F32)
            f.dma("sp", tmp[:], self.W["norm_gains"][:, :, :].rearrange("l k (c p) -> (l k c) p", p=128),
                  writes=[tmp])
            b = self.banks[0]
            f.op("pe", lambda e: e.transpose(b[:, 0:128], tmp[:], ident[:]), reads=[tmp, ident], writes=[b])
            f.op("dve", lambda e: e.tensor_copy(out=self.gall[:], in_=b[:, 0:128]), reads=[b], writes=[self.gall])

    def gcol(self, l, k, dc):
        c = (l * 4 + k) * 8 + dc
        return self.gall[:, c:c + 1]
    def phase_x0(self, xa):
        f, cfg = self.f, self.cfg
        S = cfg.S
        with f.scope():
            xin = [f.sbuf("xin", [128, D], F32) for _ in range(2)]
            xo = [f.sbuf("xo", [128, DC, 128], F32) for _ in range(2)]
            i = 0
            for s in range(cfg.nseg):
                for t in range(S // 128):
                    a, o = xin[i % 2], xo[i % 2]
                    f.dma("sp", a[:], self.x_in[s, t * 128:(t + 1) * 128, :], writes=[a])
                    for half in range(2):
                        b = self.banks[(2 * i + half) % 4]
                        for j in range(4):
                            dc = half * 4 + j
                            f.op("pe", lambda e, b=b, j=j, dc=dc, a=a: e.transpose(
                                b[:, j * 128:(j + 1) * 128], a[:, dc * 128:(dc + 1) * 128], self.ident[:]),
                                reads=[a, self.ident], writes=[b])
                        if half == 0:
                            f.op("act", lambda e, b=b, o=o, half=half: e.copy(
                                out=o[:, half * 4:half * 4 + 4, :], in_=b[:].rearrange("p (c t) -> p c t", t=128)),
                                reads=[b], writes=[o])
                        else:
                            f.op("dve", lambda e, b=b, o=o, half=half: e.tensor_copy(
                                out=o[:, half * 4:half * 4 + 4, :], in_=b[:].rearrange("p (c t) -> p c t", t=128)),
                                reads=[b], writes=[o])
                    f.dma("sp", xa[s][:, :, t * 128:(t + 1) * 128], o[:], reads=[o], writes=[xa[s]])
                    i += 1

    def rstd_from_sq(self, sq, rstd, bank, n):
        f = self.f
        for dc in range(DC):
            f.op("pe", lambda e, dc=dc: e.matmul(bank[:, 0:n], lhsT=self.onesb[:], rhs=sq[:, dc, 0:n],
                                                 start=(dc == 0), stop=(dc == DC - 1)),
                 reads=[self.onesb, sq], writes=[bank])
        f.op("act", lambda e: e.activation(out=rstd[:, 0:n], in_=bank[:, 0:n], func=AF.Ln,
                                           bias=self.eps[:, 0:1], scale=1.0 / D), reads=[bank, self.eps], writes=[rstd])
        f.op("act", lambda e: e.activation(out=rstd[:, 0:n], in_=rstd[:, 0:n], func=AF.Exp, scale=-0.5),
             reads=[rstd], writes=[rstd])
    def layer_consts(self, l):
        f = self.f
        j = l // 2
        if l % 2 == 0:
            lc = self.even_consts(j)
            lc["carry"] = [f.sbuf("carry%d" % g, [128, 8, 64], F32) for g in range(2)]
        else:
            lc = self.odd_consts(j)
            lc["carry"] = [f.sbuf("carry%d" % h, [128, 128], F32) for h in range(8)]
        return lc

    def reset_carry(self, lc):
        f = self.f
        for c in lc["carry"]:
            f.op("pool", lambda e, c=c: e.memset(c[:], 0.0), writes=[c])

    def mixer_pass(self, l, s, d, cur, lc):
        f, cfg = self.f, self.cfg
        S = cfg.S
        even = (l % 2 == 0)
        H = 128 if even else 0
        with f.scope():
            hnT = f.sbuf("hnT", [128, DC, S + 2 * H], BF16)
            if d == 0:
                self.phase_a0(l, s, hnT, H, cur, halo=even)
                f.dma("sp", self.hnsp[s][:, :, :], hnT[:, :, H:H + S], reads=[hnT], writes=[self.hnsp[s]])
            else:
                f.dma("sp", hnT[:, :, H:H + S], self.hnsp[s][:, :, :], reads=[self.hnsp[s]], writes=[hnT])
            if even:
                if d == 0:
                    for g in range(2):
                        self.attn_group(l, s, hnT, H, g, lc)
                    for g in range(2):
                        self.ssd_F(l, s, hnT, H, g, lc)
                else:
                    for g in range(2):
                        self.ssd_B(l, s, hnT, H, g, lc)
            else:
                self.odd_mixer(l, s, hnT, d, lc)

    def phase_a0(self, l, s, hnT, H, cur, halo):
        f, cfg = self.f, self.cfg
        S, TS = cfg.S, cfg.TS
        with f.scope():
            xt = [f.sbuf("a0x", [128, DC, TS], F32) for _ in range(2)]
            sq = f.sbuf("a0sq", [128, DC, TS], BF16)
            rstd = f.sbuf("a0rstd", [128, TS], F32)
            jobs = [(s, st_i * TS, TS, H + st_i * TS, None) for st_i in range(S // TS)]
            if halo:
                for side, nb in ((0, s - 1), (1, s + 1)):
                    dst = 0 if side == 0 else H + S
                    if 0 <= nb < cfg.nseg:
                        jobs.append((nb, (S - 128) if side == 0 else 0, 128, dst, self.fl(side, s)))
                    else:
                        f.op("pool", lambda e, dst=dst: e.memset(hnT[:, :, dst:dst + 128], 0.0), writes=[hnT])
            for ji, (src, c0, n, dst, flag) in enumerate(jobs):
                x_ = xt[ji % 2]
                f.dma("sp", x_[:, :, 0:n], cur[src][:, :, c0:c0 + n], reads=[cur[src]], writes=[x_])
                f.op("act", lambda e, x_=x_, n=n: e.activation(out=sq[:, :, 0:n], in_=x_[:, :, 0:n], func=AF.Square), reads=[x_], writes=[sq])
                self.rstd_from_sq(sq, rstd, self.banks[ji % 2], n)
                if flag is not None:
                    f.op("dve", lambda e, n=n, flag=flag: e.tensor_scalar(out=rstd[:, 0:n], in0=rstd[:, 0:n], scalar1=flag, scalar2=None,
                                                                          op0=ALU.mult), reads=[rstd, self.flags], writes=[rstd])
                for dc in range(DC):
                    f.op("dve", lambda e, dc=dc, x_=x_, n=n, dst=dst: e.scalar_tensor_tensor(
                        out=hnT[:, dc, dst:dst + n], in0=x_[:, dc, 0:n], scalar=self.gcol(l, 0, dc), in1=rstd[:, 0:n],
                        op0=ALU.mult, op1=ALU.mult), reads=[x_, self.gall, rstd], writes=[hnT])

    def proj_fm(self, w, hnT, c0, n, bank):
        f = self.f
        for dc in range(DC):
            f.op("pe", lambda e, dc=dc: e.matmul(bank[:, 0:n], lhsT=w[:, dc, :], rhs=hnT[:, dc, c0:c0 + n],
                                                 start=(dc == 0), stop=(dc == DC - 1)), reads=[w, hnT], writes=[bank])

    def proj_tm(self, w, wcols, hnT, c0, bank, o0):
        f = self.f
        lo, hi = wcols
        for dc in range(DC):
            f.op("pe", lambda e, dc=dc: e.matmul(bank[:, o0:o0 + (hi - lo)], lhsT=hnT[:, dc, c0:c0 + 128],
                                                 rhs=w[:, dc, lo:hi], start=(dc == 0), stop=(dc == DC - 1)),
                 reads=[w, hnT], writes=[bank])

    def odd_consts(self, j):
        f = self.f
        c = {}
        lbT = f.sbuf("lbT", [128, 16], F32)
        tmp = f.sbuf("lbtmp", [16, 128], F32)
        f.dma("sp", tmp[:], self.W["hg_lower_bounds"][:, :].rearrange("j (h p) -> (j h) p", p=128), writes=[tmp])
        b = self.banks[0]
        f.op("pe", lambda e: e.transpose(b[:, 0:16], tmp[:], self.ident[0:16, 0:16]), reads=[tmp, self.ident], writes=[b])
        f.op("dve", lambda e: e.tensor_copy(out=lbT[:], in_=b[:, 0:16]), reads=[b], writes=[lbT])
        lb = f.sbuf("lb", [128, 8], F32)
        oml = f.sbuf("oml", [128, 8], F32)
        noml = f.sbuf("noml", [128, 8], F32)
        if j == 0:
            f.op("dve", lambda e: e.memset(lb[:], 0.0), writes=[lb])
        else:
            m = f.sbuf("lbm", [128, 8], F32)
            e0 = f.sbuf("lbe0", [128, 8], F32)
            e1 = f.sbuf("lbe1", [128, 8], F32)
            f.op("dve", lambda e: e.tensor_tensor(out=m[:], in0=lbT[:, 0:8], in1=lbT[:, 8:16], op=ALU.max), reads=[lbT], writes=[m])
            f.op("dve", lambda e: e.tensor_tensor(out=e0[:], in0=lbT[:, 0:8], in1=m[:], op=ALU.subtract), reads=[lbT, m], writes=[e0])
            f.op("dve", lambda e: e.tensor_tensor(out=e1[:], in0=lbT[:, 8:16], in1=m[:], op=ALU.subtract), reads=[lbT, m], writes=[e1])
            f.op("act", lambda e: e.activation(out=e0[:], in_=e0[:], func=AF.Exp), reads=[e0], writes=[e0])
            f.op("act", lambda e: e.activation(out=e1[:], in_=e1[:], func=AF.Exp), reads=[e1], writes=[e1])
            f.op("dve", lambda e: e.tensor_tensor(out=e0[:], in0=e0[:], in1=e1[:], op=ALU.add), reads=[e0, e1], writes=[e0])
            f.op("dve", lambda e: e.reciprocal(out=e0[:], in_=e0[:]), reads=[e0], writes=[e0])
            f.op("dve", lambda e: e.tensor_tensor(out=lb[:], in0=e1[:], in1=e0[:], op=ALU.mult), reads=[e0, e1], writes=[lb])
        f.op("dve", lambda e: e.tensor_scalar(out=oml[:], in0=lb[:], scalar1=-1.0, scalar2=1.0, op0=ALU.mult, op1=ALU.add),
             reads=[lb], writes=[oml])
        f.op("dve", lambda e: e.tensor_scalar(out=noml[:], in0=lb[:], scalar1=1.0, scalar2=-1.0, op0=ALU.mult, op1=ALU.add),
             reads=[lb], writes=[noml])
        c["lb"], c["oml"], c["noml"] = lb, oml, noml
        mf = f.sbuf("hmf", [128, 128], F32)
        mb = f.sbuf("hmb", [128, 128], F32)
        blk = f.sbuf("hblk", [128, 128], F32)
        f.op("pool", lambda e: e.memset(blk[:], 0.0), writes=[blk])
        for c4 in range(NSB):
            f.op("pool", lambda e, c4=c4: e.memset(blk[c4 * LSUB:(c4 + 1) * LSUB, c4 * LSUB:(c4 + 1) * LSUB], 1.0), reads=[blk], writes=[blk])
        f.op("pool", lambda e: e.affine_select(out=mf[:], in_=blk[:], pattern=[[1, 128]], compare_op=ALU.is_ge, fill=0.0,
                                               base=0, channel_multiplier=-1), reads=[blk], writes=[mf])
        f.op("pool", lambda e: e.affine_select(out=mb[:], in_=blk[:], pattern=[[-1, 128]], compare_op=ALU.is_ge, fill=0.0,
                                               base=0, channel_multiplier=1), reads=[blk], writes=[mb])
        c["mf"], c["mb"] = mf, mb
        rm = f.sbuf("hrm", [128, NSB], F32)
        f.op("pool", lambda e: e.memset(rm[:], 0.0), writes=[rm])
        for c4 in range(NSB):
            f.op("pool", lambda e, c4=c4: e.memset(rm[c4 * LSUB:(c4 + 1) * LSUB, c4:c4 + 1], 1.0), reads=[rm], writes=[rm])
        c["rm"] = rm
        S = self.cfg.S
        sm = f.sbuf("hsm", [128, S], BF16)
        f.op("pool", lambda e: e.memset(sm[:], 1.0), writes=[sm])
        f.op("pool", lambda e: e.memset(sm[:].rearrange("p (a b) -> p a b", b=LSUB)[:, :, 0:1], 0.0), reads=[sm], writes=[sm])
        c["sm"] = sm
        nw = f.sbuf("hnw", [128, 128], F32)
        f.dma("sp", nw[:], self.W["hg_norm_w"][j:j + 1, :].partition_broadcast(128), writes=[nw])
        c["nw"] = nw
        return c
    def odd_mixer(self, l, s, hnT, d, oc):
        f, cfg = self.f, self.cfg
        S = cfg.S
        NT = S // 128
        NS = S // LSUB
        j = l // 2
        TS = cfg.TS
        B = self.banks
        win = self.W["od_w_in"]
        flag = self.fl(d, s)
        with f.scope():
            X = [f.sbuf("hX%d" % i, [128, S], F32) for i in range(7 if d == 1 else 6)]
            sigs = [X[0], X[0]]
            qss = [X[1], X[1]]
            sets = []
            for i in range(2):
                st_ = {"qdc": f.sbuf("hqdc%d" % i, [128, S], BF16), "egm": f.sbuf("hegm%d" % i, [128, NS], F32),
                       "kd": f.sbuf("hkd%d" % i, [128, S], BF16),
                       "kst": f.sbuf("hkst%d" % i, [128, S], BF16), "v": f.sbuf("hv%d" % i, [128, NT, 128], BF16),
                       "dcv": f.sbuf("hdcv%d" % i, [128, NS], F32), "o": f.sbuf("hoacc%d" % i, [128, NT, 128], F32)}
                if d == 1:
                    st_["gs"] = f.sbuf("hgs%d" % i, [128, NT, 128], BF16)
                sets.append(st_)
            wq = f.sbuf("hwq", [128, DC, 128], BF16)
            wf = f.sbuf("hwf", [128, DC, 128], BF16)
            wv = f.sbuf("hwv", [128, DC, 128], BF16)
            if d == 1:
                og = f.sbuf("hog", [128, NT, 128], BF16)
                oTh = f.sbuf("hoTh", [128, S], BF16)
                wgt = f.sbuf("hwg", [128, DC, 128], BF16)
                of_ = f.sbuf("hof", [128, NT, 128], F32)
                sqd = f.sbuf("hsqd", [128, 128], F32)
                ss = f.sbuf("hss", [128, NT], F32)
                rs = f.sbuf("hrs", [128, NT], F32)
            QW = 128 + LSUB
            gm = f.sbuf("hgm", [128, NS], F32)
            qd4 = [f.sbuf("hqd4_%d" % i, [128, NSB * QW], BF16) for i in range(2)]
            kstm = [f.sbuf("hkstm%d" % i, [128, NSB, 128], BF16) for i in range(2)]
            attm = [f.sbuf("hattm%d" % i, [128, 128], BF16) for i in range(2)]
            St = [f.sbuf("hS%d" % i, [128, 128], F32) for i in range(2)]
            Sb = [f.sbuf("hSb%d" % i, [128, 128], BF16) for i in range(2)]
            for i in range(2):
                f.op("pool", lambda e, t=qd4[i]: e.memset(t[:], 0.0), writes=[qd4[i]])

            def Pproj(h):
                sig, qs = sigs[h % 2], qss[h % 2]

                def wsl(off):
                    return win[j, :, off + h * 128: off + (h + 1) * 128].rearrange("(c p) n -> p c n", p=128)
                if d == 0:
                    f.dma("pool", wq[:], wsl(0), writes=[wq])
                else:
                    f.dma("sp", qs[:], self.qsp[s][:, h, :], reads=[self.qsp[s]], writes=[qs])
                f.dma("pool", wf[:], wsl(1024 + 1024 * d), writes=[wf])
                yield
                bi = 0
                for st_i in range(S // TS):
                    c0 = st_i * TS
                    if d == 0:
                        b = B[6 + bi % 2]; bi += 1
                        for dc in range(DC):
                            f.op("pe", lambda e, dc=dc, b=b: e.matmul(b[:, 0:TS], lhsT=wq[:, dc, :], rhs=hnT[:, dc, c0:c0 + TS],
                                                                      start=(dc == 0), stop=(dc == DC - 1)), reads=[wq, hnT], writes=[b])
                            yield
                        f.op("act", lambda e, b=b: e.activation(out=qs[:, c0:c0 + TS], in_=b[:, 0:TS], func=AF.Silu),
                             reads=[b], writes=[qs])
                        yield
                        if st_i == S // TS - 1:
                            f.dma("sp", self.qsp[s][:, h, :], qs[:], reads=[qs], writes=[self.qsp[s]])
                    b = B[6 + bi % 2]; bi += 1
                    for dc in range(DC):
                        f.op("pe", lambda e, dc=dc, b=b: e.matmul(b[:, 0:TS], lhsT=wf[:, dc, :], rhs=hnT[:, dc, c0:c0 + TS],
                                                                  start=(dc == 0), stop=(dc == DC - 1)), reads=[wf, hnT], writes=[b])
                        yield
                    f.op("act", lambda e, b=b: e.activation(out=sig[:, c0:c0 + TS], in_=b[:, 0:TS], func=AF.Sigmoid),
                         reads=[b], writes=[sig])
                    yield

            def Pelem(h):
                T = sets[h % 2]
                sig, qs, A1, A2, A3, A5 = sigs[h % 2], qss[h % 2], X[2], X[3], X[4], X[5]

                def wsl(off):
                    return win[j, :, off + h * 128: off + (h + 1) * 128].rearrange("(c p) n -> p c n", p=128)
                if d == 0:
                    f.dma("pool", wv[:], wsl(3072), writes=[wv])
                else:
                    f.dma("pool", wgt[:], wsl(4096), writes=[wgt])
                    f.dma("sp", T["v"][:].rearrange("p a b -> p (a b)"), self.vsp[s][:, h, :], reads=[self.vsp[s]], writes=[T["v"]])
                yield
                bi = 0
                for t4 in range(0, NT, 4):
                    nt = min(4, NT - t4)
                    if d == 0:
                        bv = B[6 + bi % 2]; bi += 1
                        for q in range(nt):
                            for dc in range(DC):
                                f.op("pe", lambda e, dc=dc, q=q, bv=bv: e.matmul(
                                    bv[:, q * 128:(q + 1) * 128], lhsT=hnT[:, dc, (t4 + q) * 128:(t4 + q + 1) * 128], rhs=wv[:, dc, :],
                                    start=(dc == 0), stop=(dc == DC - 1)), reads=[wv, hnT], writes=[bv])
                                yield
                        f.op("dve", lambda e, bv=bv: e.tensor_copy(
                            out=T["v"][:, t4:t4 + nt, :], in_=bv[:, 0:nt * 128].rearrange("p (a b) -> p a b", b=128)),
                            reads=[bv], writes=[T["v"]])
                        yield
                        if t4 + nt == NT:
                            f.dma("sp", self.vsp[s][:, h, :], T["v"][:].rearrange("p a b -> p (a b)"), reads=[T["v"]], writes=[self.vsp[s]])
                    if d == 1:
                        bg = B[6 + bi % 2]; bi += 1
                        for q in range(nt):
                            for dc in range(DC):
                                f.op("pe", lambda e, dc=dc, q=q, bg=bg: e.matmul(
                                    bg[:, q * 128:(q + 1) * 128], lhsT=hnT[:, dc, (t4 + q) * 128:(t4 + q + 1) * 128], rhs=wgt[:, dc, :],
                                    start=(dc == 0), stop=(dc == DC - 1)), reads=[wgt, hnT], writes=[bg])
                                yield
                        f.op("act", lambda e, bg=bg: e.activation(
                            out=T["gs"][:, t4:t4 + nt, :], in_=bg[:, 0:nt * 128].rearrange("p (a b) -> p a b", b=128), func=AF.Silu),
                            reads=[bg], writes=[T["gs"]])
                        yield
                lbc, omlc, nomlc = oc["lb"][:, h:h + 1], oc["oml"][:, h:h + 1], oc["noml"][:, h:h + 1]
                cdeps = [oc["lb"], oc["oml"], oc["noml"]]
                CW = min(512, S)
                pieces = [(c, c + CW) for c in range(0, S, CW)]
                A4 = sig
                EG = A2
                if d == 0:
                    Gd, Kx, En = A4, A1, A5
                else:
                    A6 = X[6]
                    Gd, Kx, En = A5, A6, A4
                dcol = (LSUB - 1) if d == 0 else 0
                for (a, b_) in pieces:
                    f.op("dve", lambda e: e.tensor_scalar(out=A1[:, a:b_], in0=sig[:, a:b_], scalar1=omlc, scalar2=lbc, op0=ALU.mult, op1=ALU.add),
                         reads=[sig] + cdeps, writes=[A1])
                    yield
                    f.op("act", lambda e: e.activation(out=A2[:, a:b_], in_=A1[:, a:b_], func=AF.Ln), reads=[A1], writes=[A2])
                    yield
                    f.op("pool", lambda e: e.tensor_scalar(out=A3[:, a:b_], in0=sig[:, a:b_], scalar1=nomlc, scalar2=omlc, op0=ALU.mult, op1=ALU.add),
                         reads=[sig] + cdeps, writes=[A3])
                    yield
                for (a, b_) in pieces:
                    f.op("dve", lambda e: e.tensor_tensor_scan(out=A4[:, a:b_], data0=oc["sm"][:, a:b_], data1=A2[:, a:b_], initial=0.0,
                                                               op0=ALU.mult, op1=ALU.add), reads=[oc["sm"], A2], writes=[A4])
                    yield
                    g3 = A4[:, a:b_].rearrange("p (a b) -> p a b", b=LSUB)
                    f.op("dve", lambda e: e.tensor_tensor(out=A1[:, a:b_].rearrange("p (a b) -> p a b", b=LSUB),
                                                          in0=g3[:, :, LSUB - 1:LSUB].to_broadcast([128, (b_ - a) // LSUB, LSUB]), in1=g3,
                                                          op=ALU.subtract), reads=[A4], writes=[A1])
                    yield
                    if d == 1:
                        f.op("dve", lambda e: e.tensor_tensor(out=A5[:, a:b_], in0=A1[:, a:b_], in1=A2[:, a:b_], op=ALU.add),
                             reads=[A1, A2], writes=[A5])
                        yield
                        f.op("pool", lambda e: e.tensor_tensor(out=A6[:, a:b_], in0=A4[:, a:b_], in1=A2[:, a:b_], op=ALU.subtract),
                             reads=[A4, A2], writes=[A6])
                        yield
                f.op("dve", lambda e: e.tensor_copy(out=T["dcv"][:], in_=Gd[:].rearrange("p (a b) -> p a b", b=LSUB)[:, :, dcol]),
                     reads=[Gd], writes=[T["dcv"]])
                f.op("dve", lambda e: e.tensor_copy(out=gm[:], in_=Gd[:].rearrange("p (a b) -> p a b", b=LSUB)[:, :, LSUB // 2]),
                     reads=[Gd], writes=[gm])
                yield
                f.op("act", lambda e: e.activation(out=T["dcv"][:], in_=T["dcv"][:], func=AF.Exp), reads=[T["dcv"]], writes=[T["dcv"]])
                f.op("act", lambda e: e.activation(out=T["egm"][:], in_=gm[:], func=AF.Exp), reads=[gm], writes=[T["egm"]])
                yield
                for (a, b_) in pieces:
                    ns_ = (b_ - a) // LSUB
                    f.op("dve", lambda e: e.tensor_tensor(out=Gd[:, a:b_].rearrange("p (a b) -> p a b", b=LSUB),
                                                          in0=Gd[:, a:b_].rearrange("p (a b) -> p a b", b=LSUB),
                                                          in1=gm[:, a // LSUB:a // LSUB + ns_].unsqueeze(2).to_broadcast([128, ns_, LSUB]),
                                                          op=ALU.subtract), reads=[Gd, gm], writes=[Gd])
                    yield
                    f.op("act", lambda e: e.activation(out=En[:, a:b_], in_=Gd[:, a:b_], func=AF.Exp), reads=[Gd], writes=[En])
                    yield
                    f.op("dve", lambda e: e.tensor_tensor(out=T["qdc"][:, a:b_], in0=qs[:, a:b_], in1=En[:, a:b_], op=ALU.mult),
                         reads=[qs, En], writes=[T["qdc"]])
                    yield
                    f.op("act", lambda e: e.activation(out=En[:, a:b_], in_=Gd[:, a:b_], func=AF.Exp, scale=-1.0), reads=[Gd], writes=[En])
                    yield
                    f.op("dve", lambda e: e.tensor_tensor(out=T["kd"][:, a:b_], in0=A3[:, a:b_], in1=En[:, a:b_], op=ALU.mult),
                         reads=[A3, En], writes=[T["kd"]])
                    yield
                    f.op("act", lambda e: e.activation(out=Kx[:, a:b_], in_=Kx[:, a:b_], func=AF.Exp), reads=[Kx], writes=[Kx])
                    yield
                    f.op("dve", lambda e: e.tensor_tensor(out=T["kst"][:, a:b_], in0=A3[:, a:b_], in1=Kx[:, a:b_], op=ALU.mult),
                         reads=[A3, Kx], writes=[T["kst"]])
                    yield

            def LE(h):
                T = sets[h % 2]
                carry = oc["carry"][h]
                qdc, egm, kd, kst, v_tm, dcv, o_acc = T["qdc"], T["egm"], T["kd"], T["kst"], T["v"], T["dcv"], T["o"]
                msk = oc["mf"] if d == 0 else oc["mb"]
                f.op("dve", lambda e: e.tensor_scalar(out=St[0][:], in0=carry[:], scalar1=flag, scalar2=None, op0=ALU.mult),
                     reads=[carry, self.flags], writes=[St[0]])
                f.op("dve", lambda e: e.tensor_scalar(out=Sb[0][:], in0=carry[:], scalar1=flag, scalar2=None, op0=ALU.mult),
                     reads=[carry, self.flags], writes=[Sb[0]])
                yield

                def pre(ti):
                    t = ti if d == 0 else NT - 1 - ti
                    c0 = t * 128
                    bA = B[0]
                    am, km, q4 = attm[ti % 2], kstm[ti % 2], qd4[ti % 2]
                    f.op("pe", lambda e: e.matmul(bA[:, 0:128], lhsT=kd[:, c0:c0 + 128], rhs=qdc[:, c0:c0 + 128],
                                                  start=True, stop=True), reads=[kd, qdc], writes=[bA])
                    bAb = bA[:, 128:256].bitcast(BF16)
                    f.op("pe", lambda e: e.transpose(bAb[:, 0:128], kst[:, c0:c0 + 128], self.identb[:]),
                         reads=[kst, self.identb], writes=[bA])
                    yield
                    f.op("dve", lambda e: e.tensor_tensor(out=am[:], in0=bA[:, 0:128], in1=msk[:], op=ALU.mult),
                         reads=[bA, msk], writes=[am])
                    yield
                    f.op("dve", lambda e: e.tensor_tensor(
                        out=km[:], in0=bAb[:, 0:128].unsqueeze(1).to_broadcast([128, NSB, 128]),
                        in1=oc["rm"][:, :].unsqueeze(2).to_broadcast([128, NSB, 128]), op=ALU.mult),
                        reads=[bA, oc["rm"]], writes=[km])
                    yield
                    for c4 in range(NSB):
                        f.op("act", lambda e, c4=c4: e.mul(out=q4[:, c4 * QW:c4 * QW + LSUB], in_=qdc[:, c0 + c4 * LSUB:c0 + (c4 + 1) * LSUB],
                                                           mul=egm[:, t * NSB + c4:t * NSB + c4 + 1]), reads=[qdc, egm], writes=[q4])
                    yield

                def chain(ti):
                    t = ti if d == 0 else NT - 1 - ti
                    bU, bO = (B[1], B[2]), B[3 + ti % 2]
                    am, km, q4 = attm[ti % 2], kstm[ti % 2], qd4[ti % 2]
                    order = range(NSB) if d == 0 else range(NSB - 1, -1, -1)
                    for n_i, c4 in enumerate(order):
                        k = ti * NSB + n_i
                        Sc, Sn = St[k % 2], St[(k + 1) % 2]
                        Sbc, Sbn = Sb[k % 2], Sb[(k + 1) % 2]
                        bu = bU[k % 2]
                        f.op("pe", lambda e: e.matmul(bO[:, 0:128], lhsT=q4[:, c4 * 128:c4 * 128 + 128], rhs=Sbc[:],
                                                      start=(n_i == 0), stop=False), reads=[q4, Sbc], writes=[bO])
                        f.op("pe", lambda e: e.matmul(bu[:, 0:128], lhsT=km[:, c4, :], rhs=v_tm[:, t, :], start=True, stop=True),
                             reads=[km, v_tm], writes=[bu])
                        yield
                        sc = t * NSB + c4
                        f.op("dve", lambda e: e.scalar_tensor_tensor(out=Sbn[:], in0=Sc[:], scalar=dcv[:, sc:sc + 1], in1=bu[:, 0:128],
                                                                     op0=ALU.mult, op1=ALU.add), reads=[Sc, dcv, bu], writes=[Sbn])
                        f.op("dve", lambda e: e.scalar_tensor_tensor(out=Sn[:], in0=Sc[:], scalar=dcv[:, sc:sc + 1], in1=bu[:, 0:128],
                                                                     op0=ALU.mult, op1=ALU.add), reads=[Sc, dcv, bu], writes=[Sn])
                        yield
                    f.op("pe", lambda e: e.matmul(bO[:, 0:128], lhsT=am[:], rhs=v_tm[:, t, :], start=False, stop=True),
                         reads=[am, v_tm], writes=[bO])
                    yield
                    f.op("act", lambda e: e.copy(out=o_acc[:, t, :], in_=bO[:, 0:128]), reads=[bO], writes=[o_acc])
                    yield

                yield from pre(0)
                for ti in range(NT):
                    gens = [chain(ti)] + ([pre(ti + 1)] if ti + 1 < NT else [])
                    yield from lockstep_gen(gens)
                Sfin = St[(NT * NSB) % 2]
                f.op("pool", lambda e: e.tensor_copy(out=carry[:], in_=Sfin[:]), reads=[Sfin], writes=[carry])
                yield
                if d == 0:
                    f.dma("sp", self.ofwd[s][:, h, :], o_acc[:].rearrange("p a b -> p (a b)"), reads=[o_acc], writes=[self.ofwd[s]])
                    return
                f.dma("sp", of_[:].rearrange("p a b -> p (a b)"), self.ofwd[s][:, h, :], reads=[self.ofwd[s]], writes=[of_])
                yield
                f.op("dve", lambda e: e.tensor_tensor(out=o_acc[:], in0=o_acc[:], in1=of_[:], op=ALU.add), reads=[o_acc, of_], writes=[o_acc])
                yield
                for t in range(NT):
                    f.op("act", lambda e, t=t: e.activation(out=sqd[:], in_=o_acc[:, t, :], func=AF.Square, accum_out=ss[:, t:t + 1]),
                         reads=[o_acc], writes=[sqd, ss])
                    yield
                f.op("act", lambda e: e.activation(out=rs[:], in_=ss[:], func=AF.Ln, bias=self.eps[:, 0:1], scale=1.0 / 128),
                     reads=[ss, self.eps], writes=[rs])
                f.op("act", lambda e: e.activation(out=rs[:], in_=rs[:], func=AF.Exp, scale=-0.5), reads=[rs], writes=[rs])
                yield
                f.op("dve", lambda e: e.tensor_tensor(out=o_acc[:], in0=o_acc[:], in1=rs[:, :].unsqueeze(2).to_broadcast([128, NT, 128]),
                                                      op=ALU.mult), reads=[o_acc, rs], writes=[o_acc])
                yield
                f.op("dve", lambda e: e.tensor_tensor(out=o_acc[:], in0=o_acc[:], in1=oc["nw"][:, :].unsqueeze(1).to_broadcast([128, NT, 128]),
                                                      op=ALU.mult), reads=[o_acc, oc["nw"]], writes=[o_acc])
                yield
                f.op("dve", lambda e: e.tensor_tensor(out=og[:], in0=o_acc[:], in1=T["gs"][:], op=ALU.mult), reads=[o_acc, T["gs"]], writes=[og])
                yield
                for t4 in range(0, NT, 4):
                    nt = min(4, NT - t4)
                    b = B[0]
                    bb = b[:, 0:256].bitcast(BF16)
                    for q in range(nt):
                        f.op("pe", lambda e, q=q, bb=bb: e.transpose(bb[:, q * 128:(q + 1) * 128], og[:, t4 + q, :], self.identb[:]),
                             reads=[og, self.identb], writes=[b])
                    yield
                    f.op("act", lambda e, bb=bb, nt=nt: e.copy(out=oTh[:, t4 * 128:(t4 + nt) * 128], in_=bb[:, 0:nt * 128]),
                         reads=[b], writes=[oTh])
                    yield
                f.dma("sp", self.oT[s][:, h, :], oTh[:], reads=[oTh], writes=[self.oT[s]])

            def P(h):
                yield from Pproj(h)
                yield from Pelem(h)
            lockstep([P(0)], 1)
            for h in range(8):
                gens = [LE(h)] + ([P(h + 1)] if h + 1 < 8 else [])
                lockstep(gens, 2)

    def even_global_consts(self):
        f = self.f
        B = self.banks
        self.build_abias()
        self.cwT = f.sbuf("cwT", [128, 100], F32)
        self.cbT = f.sbuf("cbT", [128, 20], F32)
        self.cwBC = f.sbuf("cwBC", [128, 2, 2, 2, 5], F32)
        self.cbBC = f.sbuf("cbBC", [128, 2, 2, 2], F32)
        f.op("pool", lambda e: e.memset(self.cwBC[:], 0.0), writes=[self.cwBC])
        f.op("pool", lambda e: e.memset(self.cbBC[:], 0.0), writes=[self.cbBC])
        self.ones32 = f.sbuf("ones32", [128, 128], F32)
        self.tri = [f.sbuf("tri%d" % d, [128, 128], F32) for d in range(2)]
        self.negm = [f.sbuf("negm%d" % d, [128, 128], F32) for d in range(2)]
        f.op("pool", lambda e: e.memset(self.ones32[:], 1.0), writes=[self.ones32])
        f.op("pool", lambda e: e.affine_select(out=self.tri[0][:], in_=self.ones32[:], pattern=[[1, 128]], compare_op=ALU.is_ge,
                                               fill=0.0, base=0, channel_multiplier=-1), reads=[self.ones32], writes=[self.tri[0]])
        f.op("pool", lambda e: e.affine_select(out=self.tri[1][:], in_=self.ones32[:], pattern=[[-1, 128]], compare_op=ALU.is_ge,
                                               fill=0.0, base=0, channel_multiplier=1), reads=[self.ones32], writes=[self.tri[1]])
        for d in range(2):
            f.op("dve", lambda e, d=d: e.tensor_scalar(out=self.negm[d][:], in0=self.tri[d][:], scalar1=-1.0, scalar2=-NEG,
                                                       op0=ALU.add, op1=ALU.mult), reads=[self.tri[d]], writes=[self.negm[d]])
        with f.scope():
            tmp = f.sbuf("cwtmp", [100, 128], F32)
            f.dma("sp", tmp[:], self.W["ssd_conv_w"][:, :, :].rearrange("j k (c p) -> (j k c) p", p=128), writes=[tmp])
            f.op("pe", lambda e: e.transpose(B[0][:, 0:100], tmp[:], self.ident[0:100, 0:100]), reads=[tmp, self.ident], writes=[B[0]])
            f.op("dve", lambda e: e.tensor_copy(out=self.cwT[:], in_=B[0][:, 0:100]), reads=[B[0]], writes=[self.cwT])
            tmp2 = f.sbuf("cbtmp", [20, 128], F32)
            f.dma("sp", tmp2[:], self.W["ssd_conv_b"][:, :].rearrange("j (c p) -> (j c) p", p=128), writes=[tmp2])
            f.op("pe", lambda e: e.transpose(B[1][:, 0:20], tmp2[:], self.ident[0:20, 0:20]), reads=[tmp2, self.ident], writes=[B[1]])
            f.op("dve", lambda e: e.tensor_copy(out=self.cbT[:], in_=B[1][:, 0:20]), reads=[B[1]], writes=[self.cbT])
            for j in range(2):
                for wh in range(2):
                    for g in range(2):
                        c0 = 1024 + wh * 128 + g * 64
                        f.dma("sp", self.cwBC[0:64, j, wh, g, :], self.W["ssd_conv_w"][j, :, c0:c0 + 64].rearrange("k n -> n k"),
                              writes=[self.cwBC], allow_slow_non_contiguous=True)
                        f.dma("sp", self.cbBC[0:64, j, wh, g:g + 1], self.W["ssd_conv_b"][j:j + 1, c0:c0 + 64].rearrange("o n -> n o"),
                              writes=[self.cbBC], allow_slow_non_contiguous=True)

    def build_abias(self):
        f = self.f
        self.bhi = f.sbuf("abhi", [128, 8, 384], BF16)
        self.blo = f.sbuf("ablo", [128, 8, 384], BF16)
        with f.scope():
            self.abias = f.sbuf("abias", [128, 8, 384], F32)
            bk = f.sbuf("bk", [128, 384], F32)
            mk = f.sbuf("mk", [128, 384], F32)
            t5b = f.sbuf("t5b", [128, 256], F32)
            f.dma("sp", bk[:], self.c_bucket[:, :], writes=[bk])
            f.dma("sp", t5b[:], self.W["t5_bias"][:, :].rearrange("b h -> (b h)").unsqueeze(0).partition_broadcast(128)
                  if False else self.W["t5_bias"][:, :].rearrange("(o b) h -> o (b h)", o=1).partition_broadcast(128), writes=[t5b])
            for h in range(8):
                f.dma("sp", self.abias[:, h, :], self.c_maskneg[:, :], writes=[self.abias])
            for b in range(32):
                f.op("pool", lambda e, b=b: e.tensor_scalar(out=mk[:], in0=bk[:], scalar1=float(b), scalar2=None, op0=ALU.is_equal),
                     reads=[bk], writes=[mk])
                for h in range(8):
                    f.op("dve", lambda e, b=b, h=h: e.scalar_tensor_tensor(
                        out=self.abias[:, h, :], in0=mk[:], scalar=t5b[:, b * 8 + h:b * 8 + h + 1], in1=self.abias[:, h, :],
                        op0=ALU.mult, op1=ALU.add), reads=[mk, t5b, self.abias], writes=[self.abias])
            f.op("dve", lambda e: e.tensor_copy(out=self.bhi[:], in_=self.abias[:]), reads=[self.abias], writes=[self.bhi])
            f.op("dve", lambda e: e.tensor_tensor(out=self.abias[:], in0=self.abias[:], in1=self.bhi[:], op=ALU.subtract),
                 reads=[self.abias, self.bhi], writes=[self.abias])
            f.op("dve", lambda e: e.tensor_copy(out=self.blo[:], in_=self.abias[:]), reads=[self.abias], writes=[self.blo])

    def even_consts(self, j):
        f = self.f
        c = {}
        c["sink"] = f.sbuf("sink", [128, 8], F32)
        f.dma("sp", c["sink"][:], self.W["attn_sink"][j:j + 1, :].partition_broadcast(128), writes=[c["sink"]])
        c["a"] = f.sbuf("ssa", [128, 32], F32)
        f.dma("sp", c["a"][:], self.W["ssd_a_log"][j:j + 1, :, :].rearrange("o d h -> o (d h)").partition_broadcast(128), writes=[c["a"]])
        f.op("act", lambda e: e.activation(out=c["a"][:], in_=c["a"][:], func=AF.Exp), reads=[c["a"]], writes=[c["a"]])
        f.op("dve", lambda e: e.tensor_scalar(out=c["a"][:], in0=c["a"][:], scalar1=-1.0, scalar2=None, op0=ALU.mult),
             reads=[c["a"]], writes=[c["a"]])
        c["dtb"] = f.sbuf("ssdtb", [128, 32], F32)
        f.dma("sp", c["dtb"][:], self.W["ssd_dt_bias"][j:j + 1, :, :].rearrange("o d h -> o (d h)").partition_broadcast(128), writes=[c["dtb"]])
        c["D"] = f.sbuf("ssD", [128, 16], F32)
        f.dma("sp", c["D"][:], self.W["ssd_d"][j:j + 1, :].partition_broadcast(128), writes=[c["D"]])
        c["nw"] = f.sbuf("ssnw", [128, 1024], F32)
        f.dma("sp", c["nw"][:], self.W["ssd_norm_w"][j:j + 1, :].partition_broadcast(128), writes=[c["nw"]])
        return c
    def attn_group(self, l, s, hnT, H, g, ec):
        f, cfg = self.f, self.cfg
        S, TS = cfg.S, cfg.TS
        NB = S // 128
        j = l // 2
        B = self.banks
        win = self.W["ev_w_in"]
        with f.scope():
            wq = f.sbuf("awq", [128, DC, 256], BF16)
            wk = f.sbuf("awk", [128, DC, 128], BF16)
            wv = f.sbuf("awv", [128, DC, 64], BF16)
            qT = f.sbuf("aqT", [128, 2, S], BF16)
            kTlo = f.sbuf("akTlo", [128, S + 2 * H], BF16)
            kThi = f.sbuf("akThi", [128, S + 2 * H], BF16)
            Vlo = f.sbuf("aVlo", [128, NB + 2, 128], BF16)
            Vhi = f.sbuf("aVhi", [128, NB + 2, 128], BF16)
            oTa = [f.sbuf("aoT%d" % i, [128, S], BF16) for i in range(2)]
            AW = 3
            s_sb = [f.sbuf("as%d" % i, [128, 384], F32) for i in range(AW)]
            p_sb = [f.sbuf("ap%d" % i, [128, 384], F32) for i in range(AW)]
            pn = [f.sbuf("apn%d" % i, [128, 384], BF16) for i in range(AW)]
            pT = [f.sbuf("apT%d" % i, [128, 3, 128], BF16) for i in range(AW)]
            sm = [f.sbuf("asm%d" % i, [128, 8], F32) for i in range(AW)]
            dgs = [f.sbuf("adg%d" % i, [128, 128], BF16) for i in range(AW)]

            def wsl(c0, n):
                return win[j, :, c0:c0 + n].rearrange("(c p) n -> p c n", p=128)
            f.dma("pool", wq[:], wsl(g * 256, 256), writes=[wq])
            f.dma("pool", wk[:, :, 0:64], wsl(512 + g * 64, 64), writes=[wk])
            f.dma("pool", wk[:, :, 64:128], wsl(512 + g * 64, 64), writes=[wk])
            f.dma("pool", wv[:], wsl(640 + g * 64, 64), writes=[wv])
            f.op("pool", lambda e: e.memset(Vlo[:], 0.0), writes=[Vlo])
            f.op("pool", lambda e: e.memset(Vhi[:], 0.0), writes=[Vhi])
            f.op("pool", lambda e: e.memset(kTlo[:], 0.0), writes=[kTlo])
            f.op("pool", lambda e: e.memset(kThi[:], 0.0), writes=[kThi])
            ranges = [(H + i * TS, TS) for i in range(S // TS)] + [(0, H), (H + S, H)]
            bi = 0
            for (c0, n) in ranges:
                main = (H <= c0 < H + S)
                if main:
                    for p in range(2):
                        b = B[6 + bi % 2]; bi += 1
                        for dc in range(DC):
                            f.op("pe", lambda e, dc=dc, b=b, p=p: e.matmul(b[:, 0:n], lhsT=wq[:, dc, p * 128:(p + 1) * 128],
                                                                            rhs=hnT[:, dc, c0:c0 + n], start=(dc == 0), stop=(dc == DC - 1)),
                                 reads=[wq, hnT], writes=[b])
                        f.op("act", lambda e, b=b, p=p: e.mul(out=qT[:, p, c0 - H:c0 - H + n], in_=b[:, 0:n], mul=0.125),
                             reads=[b], writes=[qT])
                b = B[6 + bi % 2]; bi += 1
                self.proj_fm(wk, hnT, c0, n, b)
                f.op("dve", lambda e, b=b: e.tensor_copy(out=kTlo[0:64, c0:c0 + n], in_=b[0:64, 0:n]), reads=[b], writes=[kTlo])
                f.op("dve", lambda e, b=b: e.tensor_copy(out=kThi[64:128, c0:c0 + n], in_=b[64:128, 0:n]), reads=[b], writes=[kThi])
            blocks = list(range(-1, NB + 1))
            for i0 in range(0, len(blocks), 8):
                grp = blocks[i0:i0 + 8]
                b = B[4 + (i0 // 8) % 2]
                for q, blk in enumerate(grp):
                    self.proj_tm(wv, (0, 64), hnT, H + blk * 128, b, q * 64)
                s0 = grp[0] + 1
                f.op("act", lambda e, b=b, s0=s0, n=len(grp): e.copy(out=Vlo[:, s0:s0 + n, 0:64],
                                                                      in_=b[:, 0:n * 64].rearrange("p (a c) -> p a c", c=64)),
                     reads=[b], writes=[Vlo])
                f.op("act", lambda e, b=b, s0=s0, n=len(grp): e.copy(out=Vhi[:, s0:s0 + n, 64:128],
                                                                      in_=b[:, 0:n * 64].rearrange("p (a c) -> p a c", c=64)),
                     reads=[b], writes=[Vhi])
            grp_cnt = {}

            def unit(it, p, qb, hh):
                kbs = (qb - 1, qb, qb + 1)
                nk = 384
                k0 = H + kbs[0] * 128
                bo = B[6 + qb % 2]
                head = 4 * g + 2 * p + hh
                bs = B[it % AW]
                bt = B[AW + it % AW]
                pp_, pT_, sm_, dg_ = pn[it % AW], pT[it % AW], sm[it % AW], dgs[it % AW]
                kT_ = kTlo if hh == 0 else kThi
                f.op("pe", lambda e: e.matmul(bs[:, 0:nk], lhsT=self.identb[:], rhs=self.bhi[:, head, :], start=True, stop=False),
                     reads=[self.identb, self.bhi], writes=[bs])
                f.op("pe", lambda e: e.matmul(bs[:, 0:nk], lhsT=self.identb[:], rhs=self.blo[:, head, :], start=False, stop=False),
                     reads=[self.identb, self.blo], writes=[bs])
                f.op("pe", lambda e: e.matmul(bs[:, 0:nk], lhsT=qT[:, p, qb * 128:(qb + 1) * 128],
                                              rhs=kT_[:, k0:k0 + nk], start=False, stop=True),
                     reads=[qT, kT_], writes=[bs])
                yield
                if qb == 0:
                    f.op("dve", lambda e: e.tensor_scalar(out=bs[:, 0:128], in0=bs[:, 0:128], scalar1=self.fl(2, s), scalar2=None,
                                                          op0=ALU.add), reads=[bs, self.flags], writes=[bs])
                    yield
                if qb == NB - 1:
                    f.op("dve", lambda e: e.tensor_scalar(out=bs[:, 256:384], in0=bs[:, 256:384], scalar1=self.fl(3, s), scalar2=None,
                                                          op0=ALU.add), reads=[bs, self.flags], writes=[bs])
                    yield
                f.op("dve", lambda e: e.tensor_reduce(out=sm_[:, 0:1], in_=bs[:, 0:nk], axis=AX.X, op=ALU.max),
                     reads=[bs], writes=[sm_])
                yield
                f.op("dve", lambda e: e.tensor_scalar(out=sm_[:, 1:2], in0=sm_[:, 0:1], scalar1=ec["sink"][:, head:head + 1],
                                                      scalar2=-1.0, op0=ALU.max, op1=ALU.mult),
                     reads=[sm_, ec["sink"]], writes=[sm_])
                yield
                f.op("act", lambda e: e.activation(out=pp_[:, 0:nk], in_=bs[:, 0:nk], func=AF.Exp, bias=sm_[:, 1:2], scale=1.0,
                                                   accum_out=sm_[:, 2:3]), reads=[bs, sm_], writes=[pp_, sm_])
                yield
                f.op("act", lambda e: e.activation(out=sm_[:, 3:4], in_=sm_[:, 1:2], func=AF.Exp,
                                                   bias=ec["sink"][:, head:head + 1], scale=1.0),
                     reads=[sm_, ec["sink"]], writes=[sm_])
                yield
                f.op("dve", lambda e: e.tensor_tensor(out=sm_[:, 4:5], in0=sm_[:, 2:3], in1=sm_[:, 3:4], op=ALU.add),
                     reads=[sm_], writes=[sm_])
                yield
                f.op("dve", lambda e: e.reciprocal(out=sm_[:, 5:6], in_=sm_[:, 4:5]), reads=[sm_], writes=[sm_])
                yield
                f.op("dve", lambda e: e.tensor_scalar(out=dg_[:], in0=self.identb[:], scalar1=sm_[:, 5:6], scalar2=None, op0=ALU.mult),
                     reads=[self.identb, sm_], writes=[dg_])
                yield
                for jb in range(3):
                    f.op("pe", lambda e, jb=jb: e.matmul(bt[:, jb * 128:(jb + 1) * 128], lhsT=pp_[:, jb * 128:(jb + 1) * 128], rhs=dg_[:],
                                                         start=True, stop=True), reads=[pp_, dg_], writes=[bt])
                yield
                f.op("act", lambda e: e.copy(out=pT_[:, 0:3, :], in_=bt[:, 0:nk].rearrange("p (a b) -> p a b", b=128)),
                     reads=[bt], writes=[pT_])
                yield
                V_ = Vlo if hh == 0 else Vhi
                for jb, kb in enumerate(kbs):
                    c = grp_cnt.get((p, qb), 0)
                    grp_cnt[(p, qb)] = c + 1
                    f.op("pe", lambda e, jb=jb, kb=kb, c=c: e.matmul(
                        bo[:, 0:128], lhsT=V_[:, kb + 1, :], rhs=pT_[:, jb, :], start=(c == 0), stop=(c == 5)),
                        reads=[V_, pT_], writes=[bo])
                yield
                if grp_cnt[(p, qb)] == 6:
                    f.op("act", lambda e: e.copy(out=oTa[p][:, qb * 128:(qb + 1) * 128], in_=bo[:, 0:128]),
                         reads=[bo], writes=[oTa[p]])
                    if qb == NB - 1:
                        f.dma("sp", self.oT[s][:, 2 * g + p, :], oTa[p][:], reads=[oTa[p]], writes=[self.oT[s]])

            def units():
                it = 0
                for p in range(2):
                    for qb in range(NB):
                        for hh in range(2):
                            yield unit(it, p, qb, hh)
                            it += 1
            lockstep(units(), AW)

    def ssd_F(self, l, s, hnT, H, g, ec):
        f, cfg = self.f, self.cfg
        S = cfg.S
        NT = S // 128
        j = l // 2
        B = self.banks
        win = self.W["ev_w_in"]
        TC = min(256, S)
        sp = self.spill[(s, g)]
        with f.scope():
            wB = f.sbuf("swB", [128, DC, 128], BF16)
            wC = f.sbuf("swC", [128, DC, 128], BF16)
            wdt = f.sbuf("swdt", [128, DC, 16], BF16)
            xs_tm = f.sbuf("sxs", [128, NT, 512], BF16)
            BT = f.sbuf("sBT", [128, S], BF16)
            CT = f.sbuf("sCT", [128, S], BF16)
            B_tm = f.sbuf("sBtm", [128, NT, 128], BF16)
            dt = f.sbuf("sdt", [128, NT, 16], F32)
            dtA = f.sbuf("sdtA", [128, NT, 16], F32)
            y_acc = f.sbuf("syacc", [128, NT, 512], F32)

            def wsl(c0, n):
                return win[j, :, c0:c0 + n].rearrange("(c p) n -> p c n", p=128)
            f.op("pool", lambda e: e.memset(wB[:], 0.0), writes=[wB])
            f.op("pool", lambda e: e.memset(wC[:], 0.0), writes=[wC])
            f.dma("pool", wB[:, :, 0:64], wsl(2816 + g * 64, 64), writes=[wB])
            f.dma("pool", wC[:, :, 0:64], wsl(2944 + g * 64, 64), writes=[wC])
            f.dma("pool", wdt[:, :, 0:8], wsl(3072 + g * 8, 8), writes=[wdt])
            f.dma("pool", wdt[:, :, 8:16], wsl(3088 + g * 8, 8), writes=[wdt])
            if True:
                wx = f.sbuf("swx", [128, DC, 512], BF16)
                f.dma("pool", wx[:], wsl(1792 + g * 512, 512), writes=[wx])
                NW = 3
                acc = [f.sbuf("sacc%d" % i, [128, TC], F32) for i in range(NW)]
                xsT = [f.sbuf("sxsT%d" % i, [128, TC], BF16) for i in range(NW)]
                xcnt = {}

                def cunit(it, c0, cc):
                    b = B[it % NW]
                    a_, xo_ = acc[it % NW], xsT[it % NW]
                    if cc < 4:
                        wblk = wx[:, :, cc * 128:(cc + 1) * 128]
                        wdep = wx
                    else:
                        wdep = wB if cc == 4 else wC
                        wblk = wdep[:, :, :]
                    for dc in range(DC):
                        f.op("pe", lambda e, dc=dc: e.matmul(
                            b[:, 0:TC + 4], lhsT=wblk[:, dc, :], rhs=hnT[:, dc, H + c0 - 2:H + c0 + TC + 2],
                            start=(dc == 0), stop=(dc == DC - 1)), reads=[wdep, hnT], writes=[b])
                    yield
                    if cc < 4:
                        def wcol(k):
                            c = (j * 5 + k) * 10 + g * 4 + cc
                            return self.cwT[:, c:c + 1]
                        bcol = self.cbT[:, j * 10 + g * 4 + cc: j * 10 + g * 4 + cc + 1]
                        cdeps = [self.cwT, self.cbT]
                    else:
                        def wcol(k):
                            return self.cwBC[:, j, cc - 4, g, k:k + 1]
                        bcol = self.cbBC[:, j, cc - 4, g:g + 1]
                        cdeps = [self.cwBC, self.cbBC]
                    f.op("dve", lambda e: e.tensor_scalar(
                        out=a_[:, :], in0=b[:, 2:2 + TC], scalar1=wcol(2), scalar2=bcol, op0=ALU.mult, op1=ALU.add),
                        reads=[b] + cdeps, writes=[a_])
                    yield
                    for k in (0, 1, 3, 4):
                        f.op("dve", lambda e, k=k: e.scalar_tensor_tensor(
                            out=a_[:, :], in0=b[:, k:k + TC], scalar=wcol(k), in1=a_[:, :], op0=ALU.mult, op1=ALU.add),
                            reads=[b, a_] + cdeps, writes=[a_])
                        yield
                    if cc < 4:
                        f.op("act", lambda e: e.activation(out=xo_[:], in_=a_[:], func=AF.Silu), reads=[a_], writes=[xo_])
                        yield
                        for tt in range(TC // 128):
                            t = c0 // 128 + tt
                            bt = B[4 + (t % 2)]
                            btb = bt[:, 0:256].bitcast(BF16)
                            f.op("pe", lambda e, tt=tt, btb=btb: e.transpose(
                                btb[:, cc * 128:(cc + 1) * 128], xo_[:, tt * 128:(tt + 1) * 128], self.identb[:]),
                                reads=[xo_, self.identb], writes=[bt])
                        yield
                        xcnt[c0] = xcnt.get(c0, 0) + 1
                        if xcnt[c0] == 4:
                            for tt in range(TC // 128):
                                t = c0 // 128 + tt
                                bt = B[4 + (t % 2)]
                                btb = bt[:, 0:256].bitcast(BF16)
                                f.op("act", lambda e, btb=btb, t=t: e.copy(out=xs_tm[:, t, :], in_=btb[:, 0:512]), reads=[bt], writes=[xs_tm])
                            yield
                    else:
                        dst = BT if cc == 4 else CT
                        f.op("act", lambda e: e.activation(out=dst[:, c0:c0 + TC], in_=a_[:, :], func=AF.Silu),
                             reads=[a_], writes=[dst])
                        yield
                        if cc == 4:
                            for tt in range(TC // 128):
                                t = c0 // 128 + tt
                                bt = B[6 + (t % 2)]
                                btb = bt[:, 0:64].bitcast(BF16)
                                f.op("pe", lambda e, btb=btb, t=t: e.transpose(btb[:, 0:128], BT[:, t * 128:(t + 1) * 128], self.identb[:]),
                                     reads=[BT, self.identb], writes=[bt])
                                f.op("dve", lambda e, btb=btb, t=t: e.tensor_copy(out=B_tm[:, t, :], in_=btb[:, 0:128]), reads=[bt], writes=[B_tm])
                            yield

                def cunits():
                    it = 0
                    for c0 in range(0, S, TC):
                        for cc in range(6):
                            yield cunit(it, c0, cc)
                            it += 1
                lockstep(cunits(), NW)
            if True:
                t1 = f.sbuf("sdt1", [128, NT, 16], F32)
                t2 = f.sbuf("sdt2", [128, NT, 16], F32)
                b = B[0]
                for t in range(NT):
                    self.proj_tm(wdt, (0, 16), hnT, H + t * 128, b, t * 16)
                bv = b[:, 0:NT * 16].rearrange("p (t x) -> p t x", x=16)
                dtb16 = f.sbuf("sdtb16", [128, 16], F32)
                a16 = f.sbuf("sa16", [128, 16], F32)
                for d in range(2):
                    f.op("pool", lambda e, d=d: e.tensor_copy(out=dtb16[:, d * 8:(d + 1) * 8], in_=ec["dtb"][:, d * 16 + g * 8:d * 16 + g * 8 + 8]),
                         reads=[ec["dtb"]], writes=[dtb16])
                    f.op("pool", lambda e, d=d: e.tensor_copy(out=a16[:, d * 8:(d + 1) * 8], in_=ec["a"][:, d * 16 + g * 8:d * 16 + g * 8 + 8]),
                         reads=[ec["a"]], writes=[a16])
                f.op("dve", lambda e: e.tensor_tensor(out=t1[:], in0=bv, in1=dtb16[:, :].unsqueeze(1).to_broadcast([128, NT, 16]), op=ALU.add),
                     reads=[b, dtb16], writes=[t1])
                f.op("act", lambda e: e.activation(out=t2[:], in_=t1[:], func=AF.Abs), reads=[t1], writes=[t2])
                f.op("act", lambda e: e.activation(out=t2[:], in_=t2[:], func=AF.Exp, scale=-1.0), reads=[t2], writes=[t2])
                f.op("act", lambda e: e.activation(out=t2[:], in_=t2[:], func=AF.Ln, bias=1.0, scale=1.0), reads=[t2], writes=[t2])
                f.op("dve", lambda e: e.tensor_scalar(out=t1[:], in0=t1[:], scalar1=0.0, scalar2=None, op0=ALU.max), reads=[t1], writes=[t1])
                f.op("dve", lambda e: e.tensor_tensor(out=dt[:], in0=t1[:], in1=t2[:], op=ALU.add), reads=[t1, t2], writes=[dt])
                f.op("dve", lambda e: e.tensor_tensor(out=dtA[:], in0=dt[:], in1=a16[:, :].unsqueeze(1).to_broadcast([128, NT, 16]), op=ALU.mult),
                     reads=[dt, a16], writes=[dtA])
            self.ssd_scan(s, g, 0, ec, xs_tm, BT, CT, B_tm, dt, dtA, y_acc)
            f.dma("sp", sp["xs"][:, :, :], xs_tm[:], reads=[xs_tm], writes=[sp["xs"]])
            f.dma("sp", sp["BT"][:, :], BT[:], reads=[BT], writes=[sp["BT"]])
            f.dma("sp", sp["CT"][:, :], CT[:], reads=[CT], writes=[sp["CT"]])
            f.dma("sp", sp["Btm"][:, :, :], B_tm[:], reads=[B_tm], writes=[sp["Btm"]])
            f.dma("sp", sp["dt"][:, :, :], dt[:], reads=[dt], writes=[sp["dt"]])
            f.dma("sp", sp["dtA"][:, :, :], dtA[:], reads=[dtA], writes=[sp["dtA"]])
            f.dma("sp", sp["y"][:, :, :], y_acc[:], reads=[y_acc], writes=[sp["y"]])

    def ssd_scan(self, s, g, d, ec, xs_tm, BT, CT, B_tm, dt, dtA, y_acc):
        f, cfg = self.f, self.cfg
        NT = cfg.S // 128
        B = self.banks
        carry = ec["carry"][g]
        flag = self.fl(d, s)
        if True:
            NP = 3
            cbm = [f.sbuf("scb%d" % i, [128, 128], F32) for i in range(NP)]
            R1s = [f.sbuf("sR1_%d" % i, [128, 8, 128], F32) for i in range(2)]
            arg = [f.sbuf("sarg%d" % i, [128, 8, 128], F32) for i in range(NP)]
            Wt = [f.sbuf("sWt%d" % i, [128, 8, 128], BF16) for i in range(NP)]
            csts = [f.sbuf("scst%d" % i, [128, 8], F32) for i in range(2)]
            od = [f.sbuf("sod%d" % i, [128, 8], F32) for i in range(NP)]
            dcb = [f.sbuf("sdcb%d" % i, [128, 8], F32) for i in range(NP)]
            xc = [f.sbuf("sxc%d" % i, [128, 8, 64], BF16) for i in range(NP)]
            xw = [f.sbuf("sxw%d" % i, [128, 8, 64], BF16) for i in range(NP)]
            ytmp = f.sbuf("sytmp", [128, 8, 64], F32)
            hT = f.sbuf("shT", [128, 8, 64], F32)
            hTb = f.sbuf("shTb", [128, 8, 64], BF16)
            f.op("dve", lambda e: e.tensor_scalar(out=hT[:], in0=carry[:], scalar1=flag, scalar2=None, op0=ALU.mult),
                 reads=[carry, self.flags], writes=[hT])
            f.op("dve", lambda e: e.tensor_scalar(out=hTb[:], in0=carry[:], scalar1=flag, scalar2=None, op0=ALU.mult),
                 reads=[carry, self.flags], writes=[hTb])
            ecol = 127 if d == 0 else 0
            tri = self.tri[d]
            bC, bP0, bP1, bYo, bU = B[0], B[1], B[2], B[6], B[7]
            bYd = (B[3], B[4], B[5])

            def pre(ti):
                t = ti if d == 0 else NT - 1 - ti
                c0 = t * 128
                i2 = ti % NP
                R1, cst = R1s[ti % 2], csts[ti % 2]
                cb_, arg_, Wt_, od_, dcb_, xc_, xw_ = cbm[i2], arg[i2], Wt[i2], od[i2], dcb[i2], xc[i2], xw[i2]
                f.op("pe", lambda e: e.matmul(bC[:, 0:128], lhsT=BT[:, c0:c0 + 128], rhs=CT[:, c0:c0 + 128], start=True, stop=True),
                     reads=[BT, CT], writes=[bC])
                dtA_d = dtA[:, t, d * 8:(d + 1) * 8]
                for h in range(8):
                    f.op("act", lambda e, h=h: e.mul(out=R1[:, h, :], in_=tri[:, :], mul=dtA[:, t, d * 8 + h:d * 8 + h + 1]),
                         reads=[tri, dtA], writes=[R1])
                yield
                f.op("pe", lambda e: e.matmul(bC[:, 128:136], lhsT=tri[:], rhs=dtA_d, start=True, stop=True),
                     reads=[tri, dtA], writes=[bC])
                f.op("dve", lambda e: e.tensor_tensor(out=cb_[:], in0=bC[:, 0:128], in1=tri[:], op=ALU.mult), reads=[bC, tri], writes=[cb_])
                yield
                f.op("dve", lambda e: e.tensor_scalar(out=cst[:], in0=bC[:, 128:136], scalar1=-1.0, scalar2=None, op0=ALU.mult),
                     reads=[bC], writes=[cst])
                for hf, bP in enumerate((bP0, bP1)):
                    f.op("pe", lambda e, hf=hf, bP=bP: e.matmul(bP[:, 0:512], lhsT=self.ones32[:],
                                                                rhs=R1[:, hf * 4:(hf + 1) * 4, :].rearrange("p a b -> p (a b)"),
                                                                start=True, stop=True), reads=[self.ones32, R1], writes=[bP])
                yield
                f.op("act", lambda e: e.activation(out=od_[:], in_=cst[:], func=AF.Exp, scale=-1.0), reads=[cst], writes=[od_])
                yield
                for hf, bP in enumerate((bP0, bP1)):
                    for h4 in range(4):
                        hh_ = hf * 4 + h4
                        f.op("act", lambda e, hh_=hh_, h4=h4, bP=bP: e.activation(
                            out=arg_[:, hh_, :], in_=bP[:, h4 * 128:(h4 + 1) * 128], func=AF.Abs, bias=cst[:, hh_:hh_ + 1], scale=1.0),
                            reads=[bP, cst], writes=[arg_])
                    f.op("act", lambda e, hf=hf, bP=bP: e.activation(
                        out=dcb_[:, hf * 4:(hf + 1) * 4], in_=bP[:, 0:512].rearrange("p (a b) -> p a b", b=128)[:, :, ecol], func=AF.Exp),
                        reads=[bP], writes=[dcb_])
                    yield
                f.op("act", lambda e: e.activation(out=arg_[:], in_=arg_[:], func=AF.Exp, scale=-1.0), reads=[arg_], writes=[arg_])
                yield
                f.op("dve", lambda e: e.tensor_tensor(out=Wt_[:], in0=arg_[:], in1=cb_[:, :].unsqueeze(1).to_broadcast([128, 8, 128]),
                                                      op=ALU.mult), reads=[arg_, cb_], writes=[Wt_])
                f.op("dve", lambda e: e.tensor_tensor(out=xc_[:], in0=xs_tm[:, t, :].rearrange("p (h x) -> p h x", x=64),
                                                      in1=dt[:, t, d * 8:(d + 1) * 8].unsqueeze(2).to_broadcast([128, 8, 64]), op=ALU.mult),
                     reads=[xs_tm, dt], writes=[xc_])
                yield
                f.op("dve", lambda e: e.tensor_tensor(out=xw_[:], in0=xc_[:], in1=arg_[:, :, ecol:ecol + 1].to_broadcast([128, 8, 64]),
                                                      op=ALU.mult), reads=[xc_, arg_], writes=[xw_])
                yield
                bY = bYd[i2]
                for h in range(8):
                    f.op("pe", lambda e, h=h: e.matmul(bY[:, h * 64:(h + 1) * 64], lhsT=Wt_[:, h, :], rhs=xc_[:, h, :],
                                                       start=True, stop=True), reads=[Wt_, xc_], writes=[bY])
                    if h % 2 == 1:
                        yield

            def chain(ti):
                t = ti if d == 0 else NT - 1 - ti
                c0 = t * 128
                i2 = ti % NP
                od_, dcb_, xw_, bY = od[i2], dcb[i2], xw[i2], bYd[i2]
                f.op("pe", lambda e: e.matmul(bYo[:, 0:512], lhsT=CT[:, c0:c0 + 128], rhs=hTb[:].rearrange("p a b -> p (a b)"),
                                              start=True, stop=True), reads=[CT, hTb], writes=[bYo])
                f.op("pe", lambda e: e.matmul(bU[:, 0:512], lhsT=B_tm[:, t, :], rhs=xw_[:].rearrange("p a b -> p (a b)"),
                                              start=True, stop=True), reads=[B_tm, xw_], writes=[bU])
                yield
                f.op("dve", lambda e: e.tensor_tensor(out=hT[:], in0=hT[:], in1=dcb_[:, :].unsqueeze(2).to_broadcast([128, 8, 64]),
                                                      op=ALU.mult), reads=[hT, dcb_], writes=[hT])
                yield
                f.op("dve", lambda e: e.tensor_tensor(out=hT[:], in0=hT[:], in1=bU[:, 0:512].rearrange("p (a b) -> p a b", b=64),
                                                      op=ALU.add), reads=[hT, bU], writes=[hT])
                yield
                f.op("act", lambda e: e.copy(out=hTb[:], in_=hT[:]), reads=[hT], writes=[hTb])
                yield
                f.op("dve", lambda e: e.tensor_tensor(out=ytmp[:], in0=bYo[:, 0:512].rearrange("p (h x) -> p h x", x=64),
                                                      in1=od_[:, :].unsqueeze(2).to_broadcast([128, 8, 64]), op=ALU.mult),
                     reads=[bYo, od_], writes=[ytmp])
                yield
                ya = y_acc[:, t, :].rearrange("p (h x) -> p h x", x=64)
                if d == 1:
                    f.op("dve", lambda e: e.tensor_tensor(out=ytmp[:], in0=ytmp[:], in1=ya, op=ALU.add),
                         reads=[ytmp, y_acc], writes=[ytmp])
                    yield
                f.op("dve", lambda e: e.tensor_tensor(out=ya, in0=bY[:, 0:512].rearrange("p (h x) -> p h x", x=64), in1=ytmp[:],
                                                      op=ALU.add), reads=[bY, ytmp], writes=[y_acc])
                yield

            def pres():
                for ti in range(NT):
                    yield from pre(ti)
                    yield ("done", ti)

            def driver():
                pg = pres()
                done = -1
                while done < 0:
                    if next(pg) is not None:
                        done = 0
                    yield
                for ti in range(NT):
                    cg = chain(ti)
                    alive = True
                    while alive:
                        try:
                            next(cg)
                            yield
                        except StopIteration:
                            alive = False
                        if done < min(ti + 2, NT - 1):
                            r = next(pg)
                            if r is not None:
                                done = r[1]
                            yield
                    while done < min(ti + 1, NT - 1):
                        r = next(pg)
                        if r is not None:
                            done = r[1]
                        yield
            lockstep([driver()], 1)
            f.op("pool", lambda e: e.tensor_copy(out=carry[:], in_=hT[:]), reads=[hT], writes=[carry])

    def ssd_B(self, l, s, hnT, H, g, ec):
        f, cfg = self.f, self.cfg
        S = cfg.S
        NT = S // 128
        j = l // 2
        B = self.banks
        win = self.W["ev_w_in"]
        sp = self.spill[(s, g)]
        with f.scope():
            wz = f.sbuf("swz", [128, DC, 512], BF16)
            xs_tm = f.sbuf("sxs", [128, NT, 512], BF16)
            BT = f.sbuf("sBT", [128, S], BF16)
            CT = f.sbuf("sCT", [128, S], BF16)
            B_tm = f.sbuf("sBtm", [128, NT, 128], BF16)
            dt = f.sbuf("sdt", [128, NT, 16], F32)
            dtA = f.sbuf("sdtA", [128, NT, 16], F32)
            y_acc = f.sbuf("syacc", [128, NT, 512], F32)
            f.dma("pool", wz[:], win[j, :, 768 + g * 512:768 + (g + 1) * 512].rearrange("(c p) n -> p c n", p=128), writes=[wz])
            f.dma("sp", xs_tm[:], sp["xs"][:, :, :], reads=[sp["xs"]], writes=[xs_tm])
            f.dma("sp", BT[:], sp["BT"][:, :], reads=[sp["BT"]], writes=[BT])
            f.dma("sp", CT[:], sp["CT"][:, :], reads=[sp["CT"]], writes=[CT])
            f.dma("sp", B_tm[:], sp["Btm"][:, :, :], reads=[sp["Btm"]], writes=[B_tm])
            f.dma("sp", dt[:], sp["dt"][:, :, :], reads=[sp["dt"]], writes=[dt])
            f.dma("sp", dtA[:], sp["dtA"][:, :, :], reads=[sp["dtA"]], writes=[dtA])
            f.dma("sp", y_acc[:], sp["y"][:, :, :], reads=[sp["y"]], writes=[y_acc])
            self.ssd_scan(s, g, 1, ec, xs_tm, BT, CT, B_tm, dt, dtA, y_acc)
            if True:
                zs = [f.sbuf("szs%d" % i, [128, 512], F32) for i in range(2)]
                yn = [f.sbuf("syn%d" % i, [128, 512], BF16) for i in range(2)]
                sq = [f.sbuf("ssq%d" % i, [128, 512], F32) for i in range(2)]
                ssum = f.sbuf("sssum", [128, NT], F32)
                rs = f.sbuf("srs", [128, NT], F32)
                oTs = [f.sbuf("soTs%d" % i, [128, 4, 512], BF16) for i in range(2)]
                D8 = ec["D"][:, g * 8:(g + 1) * 8]

                def gate(t):
                    z_ = zs[t % 2]
                    bz = B[t % 2]
                    self.proj_tm(wz, (0, 512), hnT, H + t * 128, bz, 0)
                    f.op("dve", lambda e: e.tensor_tensor(out=z_[:].rearrange("p (h x) -> p h x", x=64),
                                                          in0=xs_tm[:, t, :].rearrange("p (h x) -> p h x", x=64),
                                                          in1=D8.unsqueeze(2).to_broadcast([128, 8, 64]), op=ALU.mult),
                         reads=[xs_tm, ec["D"]], writes=[z_])
                    yield
                    f.op("dve", lambda e: e.tensor_tensor(out=y_acc[:, t, :], in0=y_acc[:, t, :], in1=z_[:], op=ALU.add),
                         reads=[y_acc, z_], writes=[y_acc])
                    yield
                    f.op("act", lambda e: e.activation(out=z_[:], in_=bz[:, 0:512], func=AF.Silu), reads=[bz], writes=[z_])
                    yield
                    f.op("dve", lambda e: e.tensor_tensor(out=y_acc[:, t, :], in0=y_acc[:, t, :], in1=z_[:], op=ALU.mult),
                         reads=[y_acc, z_], writes=[y_acc])
                    yield
                    f.op("act", lambda e: e.activation(out=sq[t % 2][:], in_=y_acc[:, t, :], func=AF.Square, accum_out=ssum[:, t:t + 1]),
                         reads=[y_acc], writes=[sq[t % 2], ssum])
                    yield
                lockstep((gate(t) for t in range(NT)), 2)
                f.op("act", lambda e: e.activation(out=rs[:], in_=ssum[:], func=AF.Ln, bias=self.eps[:, 0:1], scale=1.0 / 512),
                     reads=[ssum, self.eps], writes=[rs])
                f.op("act", lambda e: e.activation(out=rs[:], in_=rs[:], func=AF.Exp, scale=-0.5), reads=[rs], writes=[rs])

                def fin(t):
                    y_ = yn[t % 2]
                    f.op("dve", lambda e: e.scalar_tensor_tensor(out=y_[:], in0=y_acc[:, t, :], scalar=rs[:, t:t + 1],
                                                                 in1=ec["nw"][:, g * 512:(g + 1) * 512], op0=ALU.mult, op1=ALU.mult),
                         reads=[y_acc, rs, ec["nw"]], writes=[y_])
                    yield
                    bt = B[2 + t % 2]
                    btb = bt[:, 0:256].bitcast(BF16)
                    for c in range(4):
                        f.op("pe", lambda e, c=c: e.transpose(btb[:, c * 128:(c + 1) * 128], y_[:, c * 128:(c + 1) * 128], self.identb[:]),
                             reads=[y_, self.identb], writes=[bt])
                    yield
                    o_ = oTs[(t // 4) % 2]
                    f.op("act", lambda e: e.copy(out=o_[:, :, (t % 4) * 128:(t % 4 + 1) * 128],
                                                 in_=btb[:, 0:512].rearrange("p (c x) -> p c x", x=128)),
                         reads=[bt], writes=[o_])
                    yield
                for t0 in range(0, NT, 4):
                    n4 = min(4, NT - t0)
                    lockstep((fin(t) for t in range(t0, t0 + n4)), 2)
                    o_ = oTs[(t0 // 4) % 2]
                    f.dma("sp", self.oT[s][:, 4 + 4 * g:8 + 4 * g, t0 * 128:(t0 + n4) * 128], o_[:, :, 0:n4 * 128], reads=[o_], writes=[self.oT[s]])

    def phase_c(self, l, s, cur, nxt, last):
        f, cfg = self.f, self.cfg
        S, TS = cfg.S, cfg.TS
        j = l // 2
        even = (l % 2 == 0)
        nO = 12 if even else 8
        wout = self.W["ev_w_out"] if even else self.W["od_w_out"]
        B = self.banks
        with f.scope():
            wo = f.sbuf("wo", [128, nO, D], BF16)
            xt = f.sbuf("xt", [128, DC, TS], F32)
            ot = f.sbuf("ot", [128, nO, TS], BF16)
            mt = f.sbuf("mt", [128, DC, TS], F32)
            sq = f.sbuf("sq", [128, DC, TS], BF16)
            rstd = f.sbuf("rstd", [128, TS], F32)
            hn2 = f.sbuf("hn2", [128, DC, TS], BF16)
            act = f.sbuf("actT", [128, NFC, TS], BF16)
            sg = [f.sbuf("sg", [128, TS], BF16) for _ in range(2)]
            wg = [f.sbuf("wg", [128, DC, 128], BF16) for _ in range(3)]
            wu = [f.sbuf("wu", [128, DC, 128], BF16) for _ in range(3)]
            wd = [f.sbuf("wd", [128, NFC, 128], BF16) for _ in range(2)]
            yo = [f.sbuf("yo", [128, D], F32) for _ in range(2)] if last else None
            if cfg.mixers:
                f.dma("sp", wo[:], self.wo_t[:, 0:nO * D].rearrange("p (c d) -> p c d", d=D), reads=[self.wo_t], writes=[wo])
            wi = 0
            for st_i in range(S // TS):
                c0 = st_i * TS
                f.dma("sp", xt[:], cur[s][:, :, c0:c0 + TS], reads=[cur[s]], writes=[xt])
                if cfg.mixers:
                    f.dma("sp", ot[:], self.oT[s][:, 0:nO, c0:c0 + TS], reads=[self.oT[s]], writes=[ot])
                    for dc in range(DC):
                        b = B[dc % 2]
                        for oc in range(nO):
                            f.op("pe", lambda e, b=b, oc=oc, dc=dc: e.matmul(
                                b[:, 0:TS], lhsT=wo[:, oc, dc * 128:(dc + 1) * 128], rhs=ot[:, oc, :],
                                start=(oc == 0), stop=(oc == nO - 1)), reads=[wo, ot], writes=[b])
                        f.op("act", lambda e, b=b, dc=dc: e.copy(out=mt[:, dc, :], in_=b[:, 0:TS]),
                             reads=[b], writes=[mt])
                        f.op("act", lambda e, b=b, dc=dc: e.activation(out=sq[:, dc, :], in_=b[:, 0:TS], func=AF.Square),
                             reads=[b], writes=[sq])
                    self.rstd_from_sq(sq, rstd, B[2], TS)
                    for dc in range(DC):
                        f.op("dve", lambda e, dc=dc: e.scalar_tensor_tensor(
                            out=mt[:, dc, :], in0=mt[:, dc, :], scalar=self.gcol(l, 1, dc), in1=rstd[:, :],
                            op0=ALU.mult, op1=ALU.mult), reads=[mt, self.gall, rstd], writes=[mt])
                    for dc in range(DC):
                        f.op("dve", lambda e, dc=dc: e.tensor_tensor(out=xt[:, dc, :], in0=xt[:, dc, :], in1=mt[:, dc, :], op=ALU.add),
                             reads=[xt, mt], writes=[xt])
                f.op("act", lambda e: e.activation(out=sq[:], in_=xt[:], func=AF.Square), reads=[xt], writes=[sq])
                self.rstd_from_sq(sq, rstd, B[2], TS)
                for dc in range(DC):
                    f.op("dve", lambda e, dc=dc: e.scalar_tensor_tensor(
                        out=hn2[:, dc, :], in0=xt[:, dc, :], scalar=self.gcol(l, 2, dc), in1=rstd[:, :],
                        op0=ALU.mult, op1=ALU.mult), reads=[xt, self.gall, rstd], writes=[hn2])
                for fc in range(NFC):
                    g_, u_ = wg[wi % 3], wu[wi % 3]
                    wi += 1
                    f.dma("sp", g_[:], self.wg_t[fc].rearrange("p (c n) -> p c n", n=128), reads=[self.wg_t], writes=[g_])
                    f.dma("sp", u_[:], self.wu_t[fc].rearrange("p (c n) -> p c n", n=128), reads=[self.wu_t], writes=[u_])
                    bg, bu = B[4 + (fc % 2)], B[6 + (fc % 2)]
                    for dc in range(DC):
                        f.op("pe", lambda e, dc=dc, g_=g_, bg=bg: e.matmul(
                            bg[:, 0:TS], lhsT=g_[:, dc, :], rhs=hn2[:, dc, :], start=(dc == 0), stop=(dc == DC - 1)),
                            reads=[g_, hn2], writes=[bg])
                    for dc in range(DC):
                        f.op("pe", lambda e, dc=dc, u_=u_, bu=bu: e.matmul(
                            bu[:, 0:TS], lhsT=u_[:, dc, :], rhs=hn2[:, dc, :], start=(dc == 0), stop=(dc == DC - 1)),
                            reads=[u_, hn2], writes=[bu])
                    sgt = sg[fc % 2]
                    f.op("act", lambda e, sgt=sgt, bg=bg: e.activation(out=sgt[:], in_=bg[:, 0:TS], func=AF.Silu),
                         reads=[bg], writes=[sgt])
                    f.op("dve", lambda e, sgt=sgt, bu=bu, fc=fc: e.tensor_tensor(
                        out=act[:, fc, :], in0=sgt[:], in1=bu[:, 0:TS], op=ALU.mult),
                        reads=[sgt, bu], writes=[act])
                for dc in range(DC):
                    d_ = wd[dc % 2]
                    f.dma("sp", d_[:], self.wd_t[dc].rearrange("p (c n) -> p c n", n=128), reads=[self.wd_t], writes=[d_])
                    b = B[dc % 2]
                    for fc in range(NFC):
                        f.op("pe", lambda e, b=b, fc=fc, d_=d_: e.matmul(
                            b[:, 0:TS], lhsT=d_[:, fc, :], rhs=act[:, fc, :], start=(fc == 0), stop=(fc == NFC - 1)),
                            reads=[d_, act], writes=[b])
                    f.op("act", lambda e, b=b, dc=dc: e.copy(out=mt[:, dc, :], in_=b[:, 0:TS]), reads=[b], writes=[mt])
                    f.op("act", lambda e, b=b, dc=dc: e.activation(out=sq[:, dc, :], in_=b[:, 0:TS], func=AF.Square),
                         reads=[b], writes=[sq])
                self.rstd_from_sq(sq, rstd, B[2], TS)
                for dc in range(DC):
                    f.op("dve", lambda e, dc=dc: e.scalar_tensor_tensor(
                        out=mt[:, dc, :], in0=mt[:, dc, :], scalar=self.gcol(l, 3, dc), in1=rstd[:, :],
                        op0=ALU.mult, op1=ALU.mult), reads=[mt, self.gall, rstd], writes=[mt])
                for dc in range(DC):
                    f.op("dve", lambda e, dc=dc: e.tensor_tensor(out=xt[:, dc, :], in0=xt[:, dc, :], in1=mt[:, dc, :], op=ALU.add),
                         reads=[xt, mt], writes=[xt])
                if not last:
                    f.dma("sp", nxt[s][:, :, c0:c0 + TS], xt[:], reads=[xt], writes=[nxt[s]])
                else:
                    for tt in range(TS // 128):
                        y = yo[tt % 2]
                        for half in range(2):
                            b = B[2 + half]
                            for q in range(4):
                                dc = half * 4 + q
                                f.op("pe", lambda e, b=b, q=q, dc=dc, tt=tt: e.transpose(
                                    b[:, q * 128:(q + 1) * 128], xt[:, dc, tt * 128:(tt + 1) * 128], self.ident[:]),
                                    reads=[xt, self.ident], writes=[b])
                            if half == 0:
                                f.op("act", lambda e, b=b, y=y: e.copy(out=y[:, 0:512], in_=b[:]), reads=[b], writes=[y])
                            else:
                                f.op("dve", lambda e, b=b, y=y: e.tensor_copy(out=y[:, 512:1024], in_=b[:]),
                                     reads=[b], writes=[y])
                        r0 = c0 + tt * 128
                        f.dma("sp", self.y_out[s, r0:r0 + 128, :], y[:], reads=[y])


_CACHE = {}

NSEG = 8
ASSIGN = [[("s", i) for i in range(8)]]
_p = 0
for _c in range(1, 8):
    _n = 3 if _c <= 2 else 2
    ASSIGN.append([("p", _p + i) for i in range(_n)] + [None] * (NSEG - _n))
    _p += _n


def make_flags(chain_left):
    n = len(chain_left)
    cl = np.asarray(chain_left, dtype=np.float32)
    cr = np.concatenate([cl[1:], np.zeros(1, np.float32)])
    return np.concatenate([cl, cr, NEG * (1 - cl), NEG * (1 - cr)]).astype(np.float32)[None, :]


def _get_nc(cfg_key, cfg):
    if cfg_key not in _CACHE:
        _CACHE[cfg_key] = KB(cfg).build()
    return _CACHE[cfg_key]


def kernel(**inputs):
    cfg = Cfg()
    nc = _get_nc("full", cfg)
    xp = np.ascontiguousarray(inputs["x_prompt"], dtype=np.float32)
    xs = np.ascontiguousarray(inputs["x_sample"], dtype=np.float32)
    S = cfg.S
    in_maps = []
    hc = host_consts()
    wmap = {name: np.ascontiguousarray(inputs[name], dtype=np.float32) for name, _ in W_SPECS}
    for c in range(8):
        x_in = np.zeros((NSEG, S, D), np.float32)
        chain = [0] * NSEG
        for i, a in enumerate(ASSIGN[c]):
            if a is None:
                continue
            if a[0] == "s":
                x_in[i] = xs[0, a[1] * S:(a[1] + 1) * S]
                chain[i] = 1 if a[1] > 0 else 0
            else:
                x_in[i] = xp[a[1]]
        m = {"x_in": x_in, "flags": make_flags(chain)}
        m.update(hc)
        m.update(wmap)
        in_maps.append(m)
    res = run_bass_kernel_spmd(nc, in_maps, core_ids=list(range(8)))
    yp = np.empty_like(xp)
    ys = np.empty_like(xs)
    for c in range(8):
        y = np.asarray(res.results[c]["y_out"])
        for i, a in enumerate(ASSIGN[c]):
            if a is None:
                continue
            if a[0] == "s":
                ys[0, a[1] * S:(a[1] + 1) * S] = y[i]
            else:
                yp[a[1]] = y[i]
    return (yp, ys)
```

```python
import contextlib
import numpy as np
import concourse.bass as bass
import concourse.mybir as mybir
from concourse.bass_utils import run_bass_kernel_spmd

F32 = mybir.dt.float32
BF16 = mybir.dt.bfloat16
ALU = mybir.AluOpType
AF = mybir.ActivationFunctionType
AX = mybir.AxisListType

D = 1024
DC = 8
DFF = 2816
NFC = 22
EV_IN = 3104
OD_IN = 5120
EPS = 1e-6
NEG = -30000.0
LSUB = 64
NSB = 128 // LSUB


class Buf:
    __slots__ = ("t", "name", "w", "r", "excl")

    def __init__(self, t, name, excl=False):
        self.t = t
        self.name = name
        self.w = None
        self.r = {}
        self.excl = excl

    def __getitem__(self, idx):
        return self.t[idx]


class _Eng:
    def __init__(self, h, key, sem):
        self.h = h
        self.key = key
        self.sem = sem
        self.n = 0
        self.waited = {}


class FW:
    def __init__(self, nc, stack, same_engine_sync=("act", "dve", "pool")):
        self.nc = nc
        self.stack = stack
        self.sems = {}
        self.eng = {}
        for key, h in (("pe", nc.tensor), ("act", nc.scalar), ("dve", nc.vector),
                       ("pool", nc.gpsimd), ("sp", nc.sync)):
            s = stack.enter_context(nc.semaphore("s_" + key))
            self.sems[key] = s
            self.eng[key] = _Eng(h, key, s)
        self.same_engine_sync = set(same_engine_sync)
        n_slots = {"sp": 16, "act": 4, "pool": 12}
        self.slots = {}
        self.slot_rr = {}
        for q, n in n_slots.items():
            lst = []
            for i in range(n):
                k = "d_%s%d" % (q, i)
                s = stack.enter_context(nc.semaphore(k))
                self.sems[k] = s
                lst.append([k, 0])
            self.slots[q] = lst
            self.slot_rr[q] = 0
        self.n_wait = 0
        self.n_inst = 0
        self.uid = 0

    def sbuf(self, name, shape, dtype):
        self.uid += 1
        nm = "%s_%d" % (name, self.uid)
        t = self.stack.enter_context(self.nc.sbuf_tensor(nm, list(shape), dtype))
        return Buf(t, nm)

    def psum(self, name, shape, dtype):
        t = self.stack.enter_context(self.nc.psum_tensor(name, list(shape), dtype))
        return Buf(t, name, excl=True)

    def dram(self, name, shape, dtype, kind="Internal"):
        t = self.nc.dram_tensor(name, list(shape), dtype, kind=kind)
        return Buf(t, name)

    @contextlib.contextmanager
    def scope(self):
        old = self.stack
        self.stack = contextlib.ExitStack()
        try:
            yield
        finally:
            self.barrier()
            self.stack.close()
            self.stack = old

    def _wait(self, E, key, val):
        if E.waited.get(key, 0) >= val:
            return
        E.h.wait_ge(self.sems[key], val)
        E.waited[key] = val
        self.n_wait += 1

    def _deps(self, E, reads, writes):
        deps = {}
        for b in reads:
            if b.w is not None:
                k, v = b.w
                if deps.get(k, 0) < v:
                    deps[k] = v
            if b.excl:
                for k, v in b.r.items():
                    if k != E.key and deps.get(k, 0) < v:
                        deps[k] = v
        for b in writes:
            if b.w is not None:
                k, v = b.w
                if deps.get(k, 0) < v:
                    deps[k] = v
            for k, v in b.r.items():
                if deps.get(k, 0) < v:
                    deps[k] = v
        for k, v in deps.items():
            if k == E.key and k not in self.same_engine_sync:
                continue
            self._wait(E, k, v)

    def _mark(self, ev, reads, writes):
        k, v = ev
        for b in reads:
            if b.r.get(k, 0) < v:
                b.r[k] = v
        for b in writes:
            b.w = ev
            b.r = {}

    def op(self, eng, fn, reads=(), writes=()):
        E = self.eng[eng]
        self._deps(E, reads, writes)
        inst = fn(E.h)
        E.n += 1
        inst.then_inc(E.sem, 1)
        self.n_inst += 1
        self._mark((E.key, E.n), reads, writes)
        return inst

    def dma(self, q, out, in_, reads=(), writes=(), **kw):
        E = self.eng[q]
        self._deps(E, reads, writes)
        lst = self.slots[q]
        i = self.slot_rr[q]
        self.slot_rr[q] = (i + 1) % len(lst)
        slot = lst[i]
        if slot[1] > 0:
            self._wait(E, slot[0], 16 * slot[1])
        inst = E.h.dma_start(out=out, in_=in_, **kw)
        slot[1] += 1
        inst.then_inc(self.sems[slot[0]], 16)
        self.n_inst += 1
        self._mark((slot[0], 16 * slot[1]), reads, writes)
        return inst

    def collective(self, kind, op, rg, in_ap, out_ap, reads=(), writes=()):
        E = self.eng["pool"]
        self._deps(E, reads, writes)
        lst = self.slots["pool"]
        i = self.slot_rr["pool"]
        self.slot_rr["pool"] = (i + 1) % len(lst)
        slot = lst[i]
        if slot[1] > 0:
            self._wait(E, slot[0], 16 * slot[1])
        inst = E.h.collective_compute(kind, op, replica_groups=rg, ins=[in_ap], outs=[out_ap])
        slot[1] += 1
        inst.then_inc(self.sems[slot[0]], 16)
        self.n_inst += 1
        self._mark((slot[0], 16 * slot[1]), reads, writes)
        return inst

    def barrier(self):
        S = self.eng["sp"]
        for q, lst in self.slots.items():
            for k, c in lst:
                if c > 0:
                    self._wait(S, k, 16 * c)
        for key in ("pe", "act", "dve", "pool"):
            X = self.eng[key]
            if X.n > 0:
                self._wait(S, key, X.n)
        S.h.sem_inc(S.sem, 1)
        S.n += 1
        for key in ("pe", "act", "dve", "pool"):
            X = self.eng[key]
            self._wait(X, "sp", S.n)
            for k2 in ("pe", "act", "dve", "pool"):
                X.waited[k2] = max(X.waited.get(k2, 0), self.eng[k2].n)
            for q, lst in self.slots.items():
                for k, c in lst:
                    X.waited[k] = max(X.waited.get(k, 0), 16 * c)

    def finish(self):
        self.barrier()


def lockstep(gen_iter, width=2):
    it = iter(gen_iter)
    active = []
    done = False
    while True:
        while not done and len(active) < width:
            try:
                active.append(next(it))
            except StopIteration:
                done = True
        if not active:
            return
        for g in list(active):
            try:
                next(g)
            except StopIteration:
                active.remove(g)


def lockstep_gen(gens):
    active = list(gens)
    while active:
        for g in list(active):
            try:
                next(g)
                yield
            except StopIteration:
                active.remove(g)


class Cfg:
    def __init__(self, S=2048, nseg=8, layers=(0, 1, 2, 3), TS=512, mixers=True):
        self.S = S
        self.nseg = nseg
        self.layers = tuple(layers)
        self.TS = min(TS, S)
        self.mixers = mixers


W_SPECS = [
    ("norm_gains", [4, 4, 1024]), ("t5_bias", [32, 8]), ("ev_w_in", [2, 1024, 3104]),
    ("attn_sink", [2, 8]), ("ssd_conv_w", [2, 5, 1280]), ("ssd_conv_b", [2, 1280]),
    ("ssd_a_log", [2, 2, 16]), ("ssd_dt_bias", [2, 2, 16]), ("ssd_d", [2, 16]),
    ("ssd_norm_w", [2, 1024]), ("ev_w_out", [2, 1536, 1024]), ("od_w_in", [2, 1024, 5120]),
    ("hg_lower_bounds", [2, 1024]), ("hg_norm_w", [2, 128]), ("od_w_out", [2, 1024, 1024]),
    ("ffn_w_gate", [4, 1024, 2816]), ("ffn_w_up", [4, 1024, 2816]), ("ffn_w_down", [4, 2816, 1024]),
]


def t5_bucket_np(rel):
    half = 16
    max_exact = 8
    n = np.abs(rel)
    large = max_exact + (np.log(np.maximum(n, 1) / max_exact) / np.log(128 / max_exact) * (half - max_exact)).astype(np.int32)
    large = np.minimum(large, half - 1)
    return ((rel > 0).astype(np.int32) * half + np.where(n < max_exact, n, large)).astype(np.int32)


def host_consts():
    qi = np.arange(128)[:, None]
    kj = np.arange(384)[None, :] - 128
    rel = kj - qi
    valid = np.abs(rel) <= 128
    bucket = np.where(valid, t5_bucket_np(rel), -1).astype(np.float32)
    maskneg = np.where(valid, 0.0, NEG).astype(np.float32)
    return {"c_bucket": bucket, "c_maskneg": maskneg}

class KB:
    def __init__(self, cfg):
        self.cfg = cfg
        self.nc = bass.Bass("TRN2", target_bir_lowering=False)

    def build(self):
        cfg = self.cfg
        nc = self.nc
        S, nseg = cfg.S, cfg.nseg
        NT = S // 128
        self.x_in = nc.dram_tensor("x_in", [nseg, S, D], F32, kind="ExternalInput")
        self.y_out = nc.dram_tensor("y_out", [nseg, S, D], F32, kind="ExternalOutput")
        self.W = {}
        for name, shape in W_SPECS:
            self.W[name] = nc.dram_tensor(name, shape, F32, kind="ExternalInput")
        self.c_bucket = nc.dram_tensor("c_bucket", [128, 384], F32, kind="ExternalInput")
        self.c_maskneg = nc.dram_tensor("c_maskneg", [128, 384], F32, kind="ExternalInput")
        self.flags_in = nc.dram_tensor("flags", [1, 4 * nseg], F32, kind="ExternalInput")
        with contextlib.ExitStack() as st:
            import os as _os
            _ses = tuple(x for x in _os.environ.get("SES", "act,dve,pool").split(",") if x)
            f = FW(nc, st, same_engine_sync=_ses)
            self.f = f
            xa = [f.dram("xTa%d" % s, [128, DC, S], F32) for s in range(nseg)]
            xb = [f.dram("xTb%d" % s, [128, DC, S], F32) for s in range(nseg)]
            self.oT = [f.dram("oT%d" % s, [128, 12, S], BF16) for s in range(nseg)]
            self.ofwd = [f.dram("ofwd%d" % s, [128, 8, NT * 128], F32) for s in range(nseg)]
            self.qsp = [f.dram("qsp%d" % s, [128, 8, S], F32) for s in range(nseg)]
            self.hnsp = [f.dram("hnsp%d" % s, [128, DC, S], BF16) for s in range(nseg)]
            self.vsp = [f.dram("vsp%d" % s, [128, 8, NT * 128], BF16) for s in range(nseg)]
            self.spill = {}
            if cfg.mixers and any(l % 2 == 0 for l in cfg.layers):
                for s in range(nseg):
                    for g in range(2):
                        k = "%d_%d" % (s, g)
                        self.spill[(s, g)] = {
                            "xs": f.dram("sp_xs" + k, [128, NT, 512], BF16),
                            "BT": f.dram("sp_BT" + k, [128, S], BF16),
                            "CT": f.dram("sp_CT" + k, [128, S], BF16),
                            "Btm": f.dram("sp_Btm" + k, [128, NT, 128], BF16),
                            "dt": f.dram("sp_dt" + k, [128, NT, 16], F32),
                            "dtA": f.dram("sp_dtA" + k, [128, NT, 16], F32),
                            "y": f.dram("sp_y" + k, [128, NT, 512], F32),
                        }
            self.wg_t = f.dram("wg_t", [NFC, 128, DC * 128], BF16)
            self.wu_t = f.dram("wu_t", [NFC, 128, DC * 128], BF16)
            self.wd_t = f.dram("wd_t", [DC, 128, NFC * 128], BF16)
            self.wo_t = f.dram("wo_t", [128, 12 * D], BF16)
            self.banks = [f.psum("bank%d" % i, [128, 512], F32) for i in range(8)]
            self.consts()
            self.flags = f.sbuf("flags", [128, 4 * nseg], F32)
            f.dma("sp", self.flags[:], self.flags_in[0:1, :].partition_broadcast(128), writes=[self.flags])
            if cfg.mixers and any(l % 2 == 0 for l in cfg.layers):
                self.even_global_consts()
            self.phase_x0(xa)
            cur, nxt = xa, xb
            for li, l in enumerate(cfg.layers):
                last = li == len(cfg.layers) - 1
                self.convert_weights(l)
                with f.scope():
                    if cfg.mixers:
                        lc = self.layer_consts(l)
                        self.reset_carry(lc)
                        for s in range(nseg):
                            self.mixer_pass(l, s, 0, cur, lc)
                        self.reset_carry(lc)
                    for s in reversed(range(nseg)):
                        if cfg.mixers:
                            self.mixer_pass(l, s, 1, cur, lc)
                        self.phase_c(l, s, cur, nxt, last)
                cur, nxt = nxt, cur
            f.finish()
        return nc

    def convert_weights(self, l):
        f, cfg = self.f, self.cfg
        j = l // 2
        even = (l % 2 == 0)
        nO = 12 if even else 8
        wout = self.W["ev_w_out"] if even else self.W["od_w_out"]
        with f.scope():
            st8 = [f.sbuf("cv8_%d" % i, [128, DC, 128], BF16) for i in range(4)]
            st22 = [f.sbuf("cv22_%d" % i, [128, NFC, 128], BF16) for i in range(2)]
            sto = [f.sbuf("cvo_%d" % i, [128, 2, D], BF16) for i in range(2)]
            k = 0
            for fc in range(NFC):
                for src, dst in ((self.W["ffn_w_gate"], self.wg_t), (self.W["ffn_w_up"], self.wu_t)):
                    t_ = st8[k % 4]
                    k += 1
                    f.dma("pool", t_[:], src[l, :, fc * 128:(fc + 1) * 128].rearrange("(c p) n -> p c n", p=128), writes=[t_])
                    f.dma("sp", dst[fc].rearrange("p (c n) -> p c n", n=128), t_[:], reads=[t_], writes=[dst])
            for dc in range(DC):
                t_ = st22[dc % 2]
                f.dma("pool", t_[:], self.W["ffn_w_down"][l, :, dc * 128:(dc + 1) * 128].rearrange("(c p) n -> p c n", p=128), writes=[t_])
                f.dma("sp", self.wd_t[dc].rearrange("p (c n) -> p c n", n=128), t_[:], reads=[t_], writes=[self.wd_t])
            if cfg.mixers:
                for o2 in range(nO // 2):
                    t_ = sto[o2 % 2]
                    f.dma("pool", t_[:], wout[j, o2 * 256:(o2 + 1) * 256, :].rearrange("(c p) d -> p c d", p=128), writes=[t_])
                    f.dma("sp", self.wo_t[:, o2 * 2 * D:(o2 + 1) * 2 * D].rearrange("p (c d) -> p c d", d=D), t_[:], reads=[t_],
                          writes=[self.wo_t])

    def fl(self, kind, s):
        c = kind * self.cfg.nseg + s
        return self.flags[:, c:c + 1]

    def consts(self):
        f = self.f
        self.ident = f.sbuf("ident", [128, 128], F32)
        self.identb = f.sbuf("identb", [128, 128], BF16)
        self.onesb = f.sbuf("onesb", [128, 128], BF16)
        self.eps = f.sbuf("eps", [128, 1], F32)
        self.gall = f.sbuf("gall", [128, 128], F32)
        ident, identb = self.ident, self.identb
        f.op("pool", lambda e: e.memset(ident[:], 0.0), writes=[ident])
        f.op("pool", lambda e: e.affine_select(out=ident[:], in_=ident[:], pattern=[[-1, 128]],
                                               compare_op=ALU.not_equal, fill=1.0, base=0,
                                               channel_multiplier=1), reads=[ident], writes=[ident])
        f.op("dve", lambda e: e.tensor_copy(out=identb[:], in_=ident[:]), reads=[ident], writes=[identb])
        f.op("pool", lambda e: e.memset(self.onesb[:], 1.0), writes=[self.onesb])
        f.op("pool", lambda e: e.memset(self.eps[:], EPS), writes=[self.eps])
        with f.scope():
            tmp = f.sbuf("gtmp", [128, 128], F32)
            f.dma("sp", tmp[:], self.W["norm_gains"][:, :, :].rearrange("l k (c p) -> (l k c) p", p=128),
                  writes=[tmp])
            b = self.banks[0]
            f.op("pe", lambda e: e.transpose(b[:, 0:128], tmp[:], ident[:]), reads=[tmp, ident], writes=[b])
            f.op("dve", lambda e: e.tensor_copy(out=self.gall[:], in_=b[:, 0:128]), reads=[b], writes=[self.gall])

    def gcol(self, l, k, dc):
        c = (l * 4 + k) * 8 + dc
        return self.gall[:, c:c + 1]
    def phase_x0(self, xa):
        f, cfg = self.f, self.cfg
        S = cfg.S
        with f.scope():
            xin = [f.sbuf("xin", [128, D], F32) for _ in range(2)]
            xo = [f.sbuf("xo", [128, DC, 128], F32) for _ in range(2)]
            i = 0
            for s in range(cfg.nseg):
                for t in range(S // 128):
                    a, o = xin[i % 2], xo[i % 2]
                    f.dma("sp", a[:], self.x_in[s, t * 128:(t + 1) * 128, :], writes=[a])
                    for half in range(2):
                        b = self.banks[(2 * i + half) % 4]
                        for j in range(4):
                            dc = half * 4 + j
                            f.op("pe", lambda e, b=b, j=j, dc=dc, a=a: e.transpose(
                                b[:, j * 128:(j + 1) * 128], a[:, dc * 128:(dc + 1) * 128], self.ident[:]),
                                reads=[a, self.ident], writes=[b])
                        if half == 0:
                            f.op("act", lambda e, b=b, o=o, half=half: e.copy(
                                out=o[:, half * 4:half * 4 + 4, :], in_=b[:].rearrange("p (c t) -> p c t", t=128)),
                                reads=[b], writes=[o])
                        else:
                            f.op("dve", lambda e, b=b, o=o, half=half: e.tensor_copy(
                                out=o[:, half * 4:half * 4 + 4, :], in_=b[:].rearrange("p (c t) -> p c t", t=128)),
                                reads=[b], writes=[o])
                    f.dma("sp", xa[s][:, :, t * 128:(t + 1) * 128], o[:], reads=[o], writes=[xa[s]])
                    i += 1

    def rstd_from_sq(self, sq, rstd, bank, n):
        f = self.f
        for dc in range(DC):
            f.op("pe", lambda e, dc=dc: e.matmul(bank[:, 0:n], lhsT=self.onesb[:], rhs=sq[:, dc, 0:n],
                                                 start=(dc == 0), stop=(dc == DC - 1)),
                 reads=[self.onesb, sq], writes=[bank])
        f.op("act", lambda e: e.activation(out=rstd[:, 0:n], in_=bank[:, 0:n], func=AF.Ln,
                                           bias=self.eps[:, 0:1], scale=1.0 / D), reads=[bank, self.eps], writes=[rstd])
        f.op("act", lambda e: e.activation(out=rstd[:, 0:n], in_=rstd[:, 0:n], func=AF.Exp, scale=-0.5),
             reads=[rstd], writes=[rstd])
    def layer_consts(self, l):
        f = self.f
        j = l // 2
        if l % 2 == 0:
            lc = self.even_consts(j)
            lc["carry"] = [f.sbuf("carry%d" % g, [128, 8, 64], F32) for g in range(2)]
        else:
            lc = self.odd_consts(j)
            lc["carry"] = [f.sbuf("carry%d" % h, [128, 128], F32) for h in range(8)]
        return lc

    def reset_carry(self, lc):
        f = self.f
        for c in lc["carry"]:
            f.op("pool", lambda e, c=c: e.memset(c[:], 0.0), writes=[c])

    def mixer_pass(self, l, s, d, cur, lc):
        f, cfg = self.f, self.cfg
        S = cfg.S
        even = (l % 2 == 0)
        H = 128 if even else 0
        with f.scope():
            hnT = f.sbuf("hnT", [128, DC, S + 2 * H], BF16)
            if d == 0:
                self.phase_a0(l, s, hnT, H, cur, halo=even)
                f.dma("sp", self.hnsp[s][:, :, :], hnT[:, :, H:H + S], reads=[hnT], writes=[self.hnsp[s]])
            else:
                f.dma("sp", hnT[:, :, H:H + S], self.hnsp[s][:, :, :], reads=[self.hnsp[s]], writes=[hnT])
            if even:
                if d == 0:
                    for g in range(2):
                        self.attn_group(l, s, hnT, H, g, lc)
                    for g in range(2):
                        self.ssd_F(l, s, hnT, H, g, lc)
                else:
                    for g in range(2):
                        self.ssd_B(l, s, hnT, H, g, lc)
            else:
                self.odd_mixer(l, s, hnT, d, lc)

    def phase_a0(self, l, s, hnT, H, cur, halo):
        f, cfg = self.f, self.cfg
        S, TS = cfg.S, cfg.TS
        with f.scope():
            xt = [f.sbuf("a0x", [128, DC, TS], F32) for _ in range(2)]
            sq = f.sbuf("a0sq", [128, DC, TS], BF16)
            rstd = f.sbuf("a0rstd", [128, TS], F32)
            jobs = [(s, st_i * TS, TS, H + st_i * TS, None) for st_i in range(S // TS)]
            if halo:
                for side, nb in ((0, s - 1), (1, s + 1)):
                    dst = 0 if side == 0 else H + S
                    if 0 <= nb < cfg.nseg:
                        jobs.append((nb, (S - 128) if side == 0 else 0, 128, dst, self.fl(side, s)))
                    else:
                        f.op("pool", lambda e, dst=dst: e.memset(hnT[:, :, dst:dst + 128], 0.0), writes=[hnT])
            for ji, (src, c0, n, dst, flag) in enumerate(jobs):
                x_ = xt[ji % 2]
                f.dma("sp", x_[:, :, 0:n], cur[src][:, :, c0:c0 + n], reads=[cur[src]], writes=[x_])
                f.op("act", lambda e, x_=x_, n=n: e.activation(out=sq[:, :, 0:n], in_=x_[:, :, 0:n], func=AF.Square), reads=[x_], writes=[sq])
                self.rstd_from_sq(sq, rstd, self.banks[ji % 2], n)
                if flag is not None:
                    f.op("dve", lambda e, n=n, flag=flag: e.tensor_scalar(out=rstd[:, 0:n], in0=rstd[:, 0:n], scalar1=flag, scalar2=None,
                                                                          op0=ALU.mult), reads=[rstd, self.flags], writes=[rstd])
                for dc in range(DC):
                    f.op("dve", lambda e, dc=dc, x_=x_, n=n, dst=dst: e.scalar_tensor_tensor(
                        out=hnT[:, dc, dst:dst + n], in0=x_[:, dc, 0:n], scalar=self.gcol(l, 0, dc), in1=rstd[:, 0:n],
                        op0=ALU.mult, op1=ALU.mult), reads=[x_, self.gall, rstd], writes=[hnT])

    def proj_fm(self, w, hnT, c0, n, bank):
        f = self.f
        for dc in range(DC):
            f.op("pe", lambda e, dc=dc: e.matmul(bank[:, 0:n], lhsT=w[:, dc, :], rhs=hnT[:, dc, c0:c0 + n],
                                                 start=(dc == 0), stop=(dc == DC - 1)), reads=[w, hnT], writes=[bank])

    def proj_tm(self, w, wcols, hnT, c0, bank, o0):
        f = self.f
        lo, hi = wcols
        for dc in range(DC):
            f.op("pe", lambda e, dc=dc: e.matmul(bank[:, o0:o0 + (hi - lo)], lhsT=hnT[:, dc, c0:c0 + 128],
                                                 rhs=w[:, dc, lo:hi], start=(dc == 0), stop=(dc == DC - 1)),
                 reads=[w, hnT], writes=[bank])

    def odd_consts(self, j):
        f = self.f
        c = {}
        lbT = f.sbuf("lbT", [128, 16], F32)
        tmp = f.sbuf("lbtmp", [16, 128], F32)
        f.dma("sp", tmp[:], self.W["hg_lower_bounds"][:, :].rearrange("j (h p) -> (j h) p", p=128), writes=[tmp])
        b = self.banks[0]
        f.op("pe", lambda e: e.transpose(b[:, 0:16], tmp[:], self.ident[0:16, 0:16]), reads=[tmp, self.ident], writes=[b])
        f.op("dve", lambda e: e.tensor_copy(out=lbT[:], in_=b[:, 0:16]), reads=[b], writes=[lbT])
        lb = f.sbuf("lb", [128, 8], F32)
        oml = f.sbuf("oml", [128, 8], F32)
        noml = f.sbuf("noml", [128, 8], F32)
        if j == 0:
            f.op("dve", lambda e: e.memset(lb[:], 0.0), writes=[lb])
        else:
            m = f.sbuf("lbm", [128, 8], F32)
            e0 = f.sbuf("lbe0", [128, 8], F32)
            e1 = f.sbuf("lbe1", [128, 8], F32)
            f.op("dve", lambda e: e.tensor_tensor(out=m[:], in0=lbT[:, 0:8], in1=lbT[:, 8:16], op=ALU.max), reads=[lbT], writes=[m])
            f.op("dve", lambda e: e.tensor_tensor(out=e0[:], in0=lbT[:, 0:8], in1=m[:], op=ALU.subtract), reads=[lbT, m], writes=[e0])
            f.op("dve", lambda e: e.tensor_tensor(out=e1[:], in0=lbT[:, 8:16], in1=m[:], op=ALU.subtract), reads=[lbT, m], writes=[e1])
            f.op("act", lambda e: e.activation(out=e0[:], in_=e0[:], func=AF.Exp), reads=[e0], writes=[e0])
            f.op("act", lambda e: e.activation(out=e1[:], in_=e1[:], func=AF.Exp), reads=[e1], writes=[e1])
            f.op("dve", lambda e: e.tensor_tensor(out=e0[:], in0=e0[:], in1=e1[:], op=ALU.add), reads=[e0, e1], writes=[e0])
            f.op("dve", lambda e: e.reciprocal(out=e0[:], in_=e0[:]), reads=[e0], writes=[e0])
            f.op("dve", lambda e: e.tensor_tensor(out=lb[:], in0=e1[:], in1=e0[:], op=ALU.mult), reads=[e0, e1], writes=[lb])
        f.op("dve", lambda e: e.tensor_scalar(out=oml[:], in0=lb[:], scalar1=-1.0, scalar2=1.0, op0=ALU.mult, op1=ALU.add),
             reads=[lb], writes=[oml])
        f.op("dve", lambda e: e.tensor_scalar(out=noml[:], in0=lb[:], scalar1=1.0, scalar2=-1.0, op0=ALU.mult, op1=ALU.add),
             reads=[lb], writes=[noml])
        c["lb"], c["oml"], c["noml"] = lb, oml, noml
        mf = f.sbuf("hmf", [128, 128], F32)
        mb = f.sbuf("hmb", [128, 128], F32)
        blk = f.sbuf("hblk", [128, 128], F32)
        f.op("pool", lambda e: e.memset(blk[:], 0.0), writes=[blk])
        for c4 in range(NSB):
            f.op("pool", lambda e, c4=c4: e.memset(blk[c4 * LSUB:(c4 + 1) * LSUB, c4 * LSUB:(c4 + 1) * LSUB], 1.0), reads=[blk], writes=[blk])
        f.op("pool", lambda e: e.affine_select(out=mf[:], in_=blk[:], pattern=[[1, 128]], compare_op=ALU.is_ge, fill=0.0,
                                               base=0, channel_multiplier=-1), reads=[blk], writes=[mf])
        f.op("pool", lambda e: e.affine_select(out=mb[:], in_=blk[:], pattern=[[-1, 128]], compare_op=ALU.is_ge, fill=0.0,
                                               base=0, channel_multiplier=1), reads=[blk], writes=[mb])
        c["mf"], c["mb"] = mf, mb
        rm = f.sbuf("hrm", [128, NSB], F32)
        f.op("pool", lambda e: e.memset(rm[:], 0.0), writes=[rm])
        for c4 in range(NSB):
            f.op("pool", lambda e, c4=c4: e.memset(rm[c4 * LSUB:(c4 + 1) * LSUB, c4:c4 + 1], 1.0), reads=[rm], writes=[rm])
        c["rm"] = rm
        S = self.cfg.S
        sm = f.sbuf("hsm", [128, S], BF16)
        f.op("pool", lambda e: e.memset(sm[:], 1.0), writes=[sm])
        f.op("pool", lambda e: e.memset(sm[:].rearrange("p (a b) -> p a b", b=LSUB)[:, :, 0:1], 0.0), reads=[sm], writes=[sm])
        c["sm"] = sm
        nw = f.sbuf("hnw", [128, 128], F32)
        f.dma("sp", nw[:], self.W["hg_norm_w"][j:j + 1, :].partition_broadcast(128), writes=[nw])
        c["nw"] = nw
        return c
    def odd_mixer(self, l, s, hnT, d, oc):
        f, cfg = self.f, self.cfg
        S = cfg.S
        NT = S // 128
        NS = S // LSUB
        j = l // 2
        TS = cfg.TS
        B = self.banks
        win = self.W["od_w_in"]
        flag = self.fl(d, s)
        with f.scope():
            X = [f.sbuf("hX%d" % i, [128, S], F32) for i in range(7 if d == 1 else 6)]
            sigs = [X[0], X[0]]
            qss = [X[1], X[1]]
            sets = []
            for i in range(2):
                st_ = {"qdc": f.sbuf("hqdc%d" % i, [128, S], BF16), "egm": f.sbuf("hegm%d" % i, [128, NS], F32),
                       "kd": f.sbuf("hkd%d" % i, [128, S], BF16),
                       "kst": f.sbuf("hkst%d" % i, [128, S], BF16), "v": f.sbuf("hv%d" % i, [128, NT, 128], BF16),
                       "dcv": f.sbuf("hdcv%d" % i, [128, NS], F32), "o": f.sbuf("hoacc%d" % i, [128, NT, 128], F32)}
                if d == 1:
                    st_["gs"] = f.sbuf("hgs%d" % i, [128, NT, 128], BF16)
                sets.append(st_)
            wq = f.sbuf("hwq", [128, DC, 128], BF16)
            wf = f.sbuf("hwf", [128, DC, 128], BF16)
            wv = f.sbuf("hwv", [128, DC, 128], BF16)
            if d == 1:
                og = f.sbuf("hog", [128, NT, 128], BF16)
                oTh = f.sbuf("hoTh", [128, S], BF16)
                wgt = f.sbuf("hwg", [128, DC, 128], BF16)
                of_ = f.sbuf("hof", [128, NT, 128], F32)
                sqd = f.sbuf("hsqd", [128, 128], F32)
                ss = f.sbuf("hss", [128, NT], F32)
                rs = f.sbuf("hrs", [128, NT], F32)
            QW = 128 + LSUB
            gm = f.sbuf("hgm", [128, NS], F32)
            qd4 = [f.sbuf("hqd4_%d" % i, [128, NSB * QW], BF16) for i in range(2)]
            kstm = [f.sbuf("hkstm%d" % i, [128, NSB, 128], BF16) for i in range(2)]
            attm = [f.sbuf("hattm%d" % i, [128, 128], BF16) for i in range(2)]
            St = [f.sbuf("hS%d" % i, [128, 128], F32) for i in range(2)]
            Sb = [f.sbuf("hSb%d" % i, [128, 128], BF16) for i in range(2)]
            for i in range(2):
                f.op("pool", lambda e, t=qd4[i]: e.memset(t[:], 0.0), writes=[qd4[i]])

            def Pproj(h):
                sig, qs = sigs[h % 2], qss[h % 2]

                def wsl(off):
                    return win[j, :, off + h * 128: off + (h + 1) * 128].rearrange("(c p) n -> p c n", p=128)
                if d == 0:
                    f.dma("pool", wq[:], wsl(0), writes=[wq])
                else:
                    f.dma("sp", qs[:], self.qsp[s][:, h, :], reads=[self.qsp[s]], writes=[qs])
                f.dma("pool", wf[:], wsl(1024 + 1024 * d), writes=[wf])
                yield
                bi = 0
                for st_i in range(S // TS):
                    c0 = st_i * TS
                    if d == 0:
                        b = B[6 + bi % 2]; bi += 1
                        for dc in range(DC):
                            f.op("pe", lambda e, dc=dc, b=b: e.matmul(b[:, 0:TS], lhsT=wq[:, dc, :], rhs=hnT[:, dc, c0:c0 + TS],
                                                                      start=(dc == 0), stop=(dc == DC - 1)), reads=[wq, hnT], writes=[b])
                            yield
                        f.op("act", lambda e, b=b: e.activation(out=qs[:, c0:c0 + TS], in_=b[:, 0:TS], func=AF.Silu),
                             reads=[b], writes=[qs])
                        yield
                        if st_i == S // TS - 1:
                            f.dma("sp", self.qsp[s][:, h, :], qs[:], reads=[qs], writes=[self.qsp[s]])
                    b = B[6 + bi % 2]; bi += 1
                    for dc in range(DC):
                        f.op("pe", lambda e, dc=dc, b=b: e.matmul(b[:, 0:TS], lhsT=wf[:, dc, :], rhs=hnT[:, dc, c0:c0 + TS],
                                                                  start=(dc == 0), stop=(dc == DC - 1)), reads=[wf, hnT], writes=[b])
                        yield
                    f.op("act", lambda e, b=b: e.activation(out=sig[:, c0:c0 + TS], in_=b[:, 0:TS], func=AF.Sigmoid),
                         reads=[b], writes=[sig])
                    yield

            def Pelem(h):
                T = sets[h % 2]
                sig, qs, A1, A2, A3, A5 = sigs[h % 2], qss[h % 2], X[2], X[3], X[4], X[5]

                def wsl(off):
                    return win[j, :, off + h * 128: off + (h + 1) * 128].rearrange("(c p) n -> p c n", p=128)
                if d == 0:
                    f.dma("pool", wv[:], wsl(3072), writes=[wv])
                else:
                    f.dma("pool", wgt[:], wsl(4096), writes=[wgt])
                    f.dma("sp", T["v"][:].rearrange("p a b -> p (a b)"), self.vsp[s][:, h, :], reads=[self.vsp[s]], writes=[T["v"]])
                yield
                bi = 0
                for t4 in range(0, NT, 4):
                    nt = min(4, NT - t4)
                    if d == 0:
                        bv = B[6 + bi % 2]; bi += 1
                        for q in range(nt):
                            for dc in range(DC):
                                f.op("pe", lambda e, dc=dc, q=q, bv=bv: e.matmul(
                                    bv[:, q * 128:(q + 1) * 128], lhsT=hnT[:, dc, (t4 + q) * 128:(t4 + q + 1) * 128], rhs=wv[:, dc, :],
                                    start=(dc == 0), stop=(dc == DC - 1)), reads=[wv, hnT], writes=[bv])
                                yield
                        f.op("dve", lambda e, bv=bv: e.tensor_copy(
                            out=T["v"][:, t4:t4 + nt, :], in_=bv[:, 0:nt * 128].rearrange("p (a b) -> p a b", b=128)),
                            reads=[bv], writes=[T["v"]])
                        yield
                        if t4 + nt == NT:
                            f.dma("sp", self.vsp[s][:, h, :], T["v"][:].rearrange("p a b -> p (a b)"), reads=[T["v"]], writes=[self.vsp[s]])
                    if d == 1:
                        bg = B[6 + bi % 2]; bi += 1
                        for q in range(nt):
                            for dc in range(DC):
                                f.op("pe", lambda e, dc=dc, q=q, bg=bg: e.matmul(
                                    bg[:, q * 128:(q + 1) * 128], lhsT=hnT[:, dc, (t4 + q) * 128:(t4 + q + 1) * 128], rhs=wgt[:, dc, :],
                                    start=(dc == 0), stop=(dc == DC - 1)), reads=[wgt, hnT], writes=[bg])
                                yield
                        f.op("act", lambda e, bg=bg: e.activation(
                            out=T["gs"][:, t4:t4 + nt, :], in_=bg[:, 0:nt * 128].rearrange("p (a b) -> p a b", b=128), func=AF.Silu),
                            reads=[bg], writes=[T["gs"]])
                        yield
                lbc, omlc, nomlc = oc["lb"][:, h:h + 1], oc["oml"][:, h:h + 1], oc["noml"][:, h:h + 1]
                cdeps = [oc["lb"], oc["oml"], oc["noml"]]
                CW = min(512, S)
                pieces = [(c, c + CW) for c in range(0, S, CW)]
                A4 = sig
                EG = A2
                if d == 0:
                    Gd, Kx, En = A4, A1, A5
                else:
                    A6 = X[6]
                    Gd, Kx, En = A5, A6, A4
                dcol = (LSUB - 1) if d == 0 else 0
                for (a, b_) in pieces:
                    f.op("dve", lambda e: e.tensor_scalar(out=A1[:, a:b_], in0=sig[:, a:b_], scalar1=omlc, scalar2=lbc, op0=ALU.mult, op1=ALU.add),
                         reads=[sig] + cdeps, writes=[A1])
                    yield
                    f.op("act", lambda e: e.activation(out=A2[:, a:b_], in_=A1[:, a:b_], func=AF.Ln), reads=[A1], writes=[A2])
                    yield
                    f.op("pool", lambda e: e.tensor_scalar(out=A3[:, a:b_], in0=sig[:, a:b_], scalar1=nomlc, scalar2=omlc, op0=ALU.mult, op1=ALU.add),
                         reads=[sig] + cdeps, writes=[A3])
                    yield
                for (a, b_) in pieces:
                    f.op("dve", lambda e: e.tensor_tensor_scan(out=A4[:, a:b_], data0=oc["sm"][:, a:b_], data1=A2[:, a:b_], initial=0.0,
                                                               op0=ALU.mult, op1=ALU.add), reads=[oc["sm"], A2], writes=[A4])
                    yield
                    g3 = A4[:, a:b_].rearrange("p (a b) -> p a b", b=LSUB)
                    f.op("dve", lambda e: e.tensor_tensor(out=A1[:, a:b_].rearrange("p (a b) -> p a b", b=LSUB),
                                                          in0=g3[:, :, LSUB - 1:LSUB].to_broadcast([128, (b_ - a) // LSUB, LSUB]), in1=g3,
                                                          op=ALU.subtract), reads=[A4], writes=[A1])
                    yield
                    if d == 1:
                        f.op("dve", lambda e: e.tensor_tensor(out=A5[:, a:b_], in0=A1[:, a:b_], in1=A2[:, a:b_], op=ALU.add),
                             reads=[A1, A2], writes=[A5])
                        yield
                        f.op("pool", lambda e: e.tensor_tensor(out=A6[:, a:b_], in0=A4[:, a:b_], in1=A2[:, a:b_], op=ALU.subtract),
                             reads=[A4, A2], writes=[A6])
                        yield
                f.op("dve", lambda e: e.tensor_copy(out=T["dcv"][:], in_=Gd[:].rearrange("p (a b) -> p a b", b=LSUB)[:, :, dcol]),
                     reads=[Gd], writes=[T["dcv"]])
                f.op("dve", lambda e: e.tensor_copy(out=gm[:], in_=Gd[:].rearrange("p (a b) -> p a b", b=LSUB)[:, :, LSUB // 2]),
                     reads=[Gd], writes=[gm])
                yield
                f.op("act", lambda e: e.activation(out=T["dcv"][:], in_=T["dcv"][:], func=AF.Exp), reads=[T["dcv"]], writes=[T["dcv"]])
                f.op("act", lambda e: e.activation(out=T["egm"][:], in_=gm[:], func=AF.Exp), reads=[gm], writes=[T["egm"]])
                yield
                for (a, b_) in pieces:
                    ns_ = (b_ - a) // LSUB
                    f.op("dve", lambda e: e.tensor_tensor(out=Gd[:, a:b_].rearrange("p (a b) -> p a b", b=LSUB),
                                                          in0=Gd[:, a:b_].rearrange("p (a b) -> p a b", b=LSUB),
                                                          in1=gm[:, a // LSUB:a // LSUB + ns_].unsqueeze(2).to_broadcast([128, ns_, LSUB]),
                                                          op=ALU.subtract), reads=[Gd, gm], writes=[Gd])
                    yield
                    f.op("act", lambda e: e.activation(out=En[:, a:b_], in_=Gd[:, a:b_], func=AF.Exp), reads=[Gd], writes=[En])
                    yield
                    f.op("dve", lambda e: e.tensor_tensor(out=T["qdc"][:, a:b_], in0=qs[:, a:b_], in1=En[:, a:b_], op=ALU.mult),
                         reads=[qs, En], writes=[T["qdc"]])
                    yield
                    f.op("act", lambda e: e.activation(out=En[:, a:b_], in_=Gd[:, a:b_], func=AF.Exp, scale=-1.0), reads=[Gd], writes=[En])
                    yield
                    f.op("dve", lambda e: e.tensor_tensor(out=T["kd"][:, a:b_], in0=A3[:, a:b_], in1=En[:, a:b_], op=ALU.mult),
                         reads=[A3, En], writes=[T["kd"]])
                    yield
                    f.op("act", lambda e: e.activation(out=Kx[:, a:b_], in_=Kx[:, a:b_], func=AF.Exp), reads=[Kx], writes=[Kx])
                    yield
                    f.op("dve", lambda e: e.tensor_tensor(out=T["kst"][:, a:b_], in0=A3[:, a:b_], in1=Kx[:, a:b_], op=ALU.mult),
                         reads=[A3, Kx], writes=[T["kst"]])
                    yield

            def LE(h):
                T = sets[h % 2]
                carry = oc["carry"][h]
                qdc, egm, kd, kst, v_tm, dcv, o_acc = T["qdc"], T["egm"], T["kd"], T["kst"], T["v"], T["dcv"], T["o"]
                msk = oc["mf"] if d == 0 else oc["mb"]
                f.op("dve", lambda e: e.tensor_scalar(out=St[0][:], in0=carry[:], scalar1=flag, scalar2=None, op0=ALU.mult),
                     reads=[carry, self.flags], writes=[St[0]])
                f.op("dve", lambda e: e.tensor_scalar(out=Sb[0][:], in0=carry[:], scalar1=flag, scalar2=None, op0=ALU.mult),
                     reads=[carry, self.flags], writes=[Sb[0]])
                yield

                def pre(ti):
                    t = ti if d == 0 else NT - 1 - ti
                    c0 = t * 128
                    bA = B[0]
                    am, km, q4 = attm[ti % 2], kstm[ti % 2], qd4[ti % 2]
                    f.op("pe", lambda e: e.matmul(bA[:, 0:128], lhsT=kd[:, c0:c0 + 128], rhs=qdc[:, c0:c0 + 128],
                                                  start=True, stop=True), reads=[kd, qdc], writes=[bA])
                    bAb = bA[:, 128:256].bitcast(BF16)
                    f.op("pe", lambda e: e.transpose(bAb[:, 0:128], kst[:, c0:c0 + 128], self.identb[:]),
                         reads=[kst, self.identb], writes=[bA])
                    yield
                    f.op("dve", lambda e: e.tensor_tensor(out=am[:], in0=bA[:, 0:128], in1=msk[:], op=ALU.mult),
                         reads=[bA, msk], writes=[am])
                    yield
                    f.op("dve", lambda e: e.tensor_tensor(
                        out=km[:], in0=bAb[:, 0:128].unsqueeze(1).to_broadcast([128, NSB, 128]),
                        in1=oc["rm"][:, :].unsqueeze(2).to_broadcast([128, NSB, 128]), op=ALU.mult),
                        reads=[bA, oc["rm"]], writes=[km])
                    yield
                    for c4 in range(NSB):
                        f.op("act", lambda e, c4=c4: e.mul(out=q4[:, c4 * QW:c4 * QW + LSUB], in_=qdc[:, c0 + c4 * LSUB:c0 + (c4 + 1) * LSUB],
                                                           mul=egm[:, t * NSB + c4:t * NSB + c4 + 1]), reads=[qdc, egm], writes=[q4])
                    yield

                def chain(ti):
                    t = ti if d == 0 else NT - 1 - ti
                    bU, bO = (B[1], B[2]), B[3 + ti % 2]
                    am, km, q4 = attm[ti % 2], kstm[ti % 2], qd4[ti % 2]
                    order = range(NSB) if d == 0 else range(NSB - 1, -1, -1)
                    for n_i, c4 in enumerate(order):
                        k = ti * NSB + n_i
                        Sc, Sn = St[k % 2], St[(k + 1) % 2]
                        Sbc, Sbn = Sb[k % 2], Sb[(k + 1) % 2]
                        bu = bU[k % 2]
                        f.op("pe", lambda e: e.matmul(bO[:, 0:128], lhsT=q4[:, c4 * 128:c4 * 128 + 128], rhs=Sbc[:],
                                                      start=(n_i == 0), stop=False), reads=[q4, Sbc], writes=[bO])
                        f.op("pe", lambda e: e.matmul(bu[:, 0:128], lhsT=km[:, c4, :], rhs=v_tm[:, t, :], start=True, stop=True),
                             reads=[km, v_tm], writes=[bu])
                        yield
                        sc = t * NSB + c4
                        f.op("dve", lambda e: e.scalar_tensor_tensor(out=Sbn[:], in0=Sc[:], scalar=dcv[:, sc:sc + 1], in1=bu[:, 0:128],
                                                                     op0=ALU.mult, op1=ALU.add), reads=[Sc, dcv, bu], writes=[Sbn])
                        f.op("dve", lambda e: e.scalar_tensor_tensor(out=Sn[:], in0=Sc[:], scalar=dcv[:, sc:sc + 1], in1=bu[:, 0:128],
                                                                     op0=ALU.mult, op1=ALU.add), reads=[Sc, dcv, bu], writes=[Sn])
                        yield
                    f.op("pe", lambda e: e.matmul(bO[:, 0:128], lhsT=am[:], rhs=v_tm[:, t, :], start=False, stop=True),
                         reads=[am, v_tm], writes=[bO])
                    yield
                    f.op("act", lambda e: e.copy(out=o_acc[:, t, :], in_=bO[:, 0:128]), reads=[bO], writes=[o_acc])
                    yield

                yield from pre(0)
                for ti in range(NT):
                    gens = [chain(ti)] + ([pre(ti + 1)] if ti + 1 < NT else [])
                    yield from lockstep_gen(gens)
                Sfin = St[(NT * NSB) % 2]
                f.op("pool", lambda e: e.tensor_copy(out=carry[:], in_=Sfin[:]), reads=[Sfin], writes=[carry])
                yield
                if d == 0:
                    f.dma("sp", self.ofwd[s][:, h, :], o_acc[:].rearrange("p a b -> p (a b)"), reads=[o_acc], writes=[self.ofwd[s]])
                    return
                f.dma("sp", of_[:].rearrange("p a b -> p (a b)"), self.ofwd[s][:, h, :], reads=[self.ofwd[s]], writes=[of_])
                yield
                f.op("dve", lambda e: e.tensor_tensor(out=o_acc[:], in0=o_acc[:], in1=of_[:], op=ALU.add), reads=[o_acc, of_], writes=[o_acc])
                yield
                for t in range(NT):
                    f.op("act", lambda e, t=t: e.activation(out=sqd[:], in_=o_acc[:, t, :], func=AF.Square, accum_out=ss[:, t:t + 1]),
                         reads=[o_acc], writes=[sqd, ss])
                    yield
                f.op("act", lambda e: e.activation(out=rs[:], in_=ss[:], func=AF.Ln, bias=self.eps[:, 0:1], scale=1.0 / 128),
                     reads=[ss, self.eps], writes=[rs])
                f.op("act", lambda e: e.activation(out=rs[:], in_=rs[:], func=AF.Exp, scale=-0.5), reads=[rs], writes=[rs])
                yield
                f.op("dve", lambda e: e.tensor_tensor(out=o_acc[:], in0=o_acc[:], in1=rs[:, :].unsqueeze(2).to_broadcast([128, NT, 128]),
                                                      op=ALU.mult), reads=[o_acc, rs], writes=[o_acc])
                yield
                f.op("dve", lambda e: e.tensor_tensor(out=o_acc[:], in0=o_acc[:], in1=oc["nw"][:, :].unsqueeze(1).to_broadcast([128, NT, 128]),
                                                      op=ALU.mult), reads=[o_acc, oc["nw"]], writes=[o_acc])
                yield
                f.op("dve", lambda e: e.tensor_tensor(out=og[:], in0=o_acc[:], in1=T["gs"][:], op=ALU.mult), reads=[o_acc, T["gs"]], writes=[og])
                yield
                for t4 in range(0, NT, 4):
                    nt = min(4, NT - t4)
                    b = B[0]
                    bb = b[:, 0:256].bitcast(BF16)
                    for q in range(nt):
                        f.op("pe", lambda e, q=q, bb=bb: e.transpose(bb[:, q * 128:(q + 1) * 128], og[:, t4 + q, :], self.identb[:]),
                             reads=[og, self.identb], writes=[b])
                    yield
                    f.op("act", lambda e, bb=bb, nt=nt: e.copy(out=oTh[:, t4 * 128:(t4 + nt) * 128], in_=bb[:, 0:nt * 128]),
                         reads=[b], writes=[oTh])
                    yield
                f.dma("sp", self.oT[s][:, h, :], oTh[:], reads=[oTh], writes=[self.oT[s]])

            def P(h):
                yield from Pproj(h)
                yield from Pelem(h)
            lockstep([P(0)], 1)
            for h in range(8):
                gens = [LE(h)] + ([P(h + 1)] if h + 1 < 8 else [])
                lockstep(gens, 2)

    def even_global_consts(self):
        f = self.f
        B = self.banks
        self.build_abias()
        self.cwT = f.sbuf("cwT", [128, 100], F32)
        self.cbT = f.sbuf("cbT", [128, 20], F32)
        self.cwBC = f.sbuf("cwBC", [128, 2, 2, 2, 5], F32)
        self.cbBC = f.sbuf("cbBC", [128, 2, 2, 2], F32)
        f.op("pool", lambda e: e.memset(self.cwBC[:], 0.0), writes=[self.cwBC])
        f.op("pool", lambda e: e.memset(self.cbBC[:], 0.0), writes=[self.cbBC])
        self.ones32 = f.sbuf("ones32", [128, 128], F32)
        self.tri = [f.sbuf("tri%d" % d, [128, 128], F32) for d in range(2)]
        self.negm = [f.sbuf("negm%d" % d, [128, 128], F32) for d in range(2)]
        f.op("pool", lambda e: e.memset(self.ones32[:], 1.0), writes=[self.ones32])
        f.op("pool", lambda e: e.affine_select(out=self.tri[0][:], in_=self.ones32[:], pattern=[[1, 128]], compare_op=ALU.is_ge,
                                               fill=0.0, base=0, channel_multiplier=-1), reads=[self.ones32], writes=[self.tri[0]])
        f.op("pool", lambda e: e.affine_select(out=self.tri[1][:], in_=self.ones32[:], pattern=[[-1, 128]], compare_op=ALU.is_ge,
                                               fill=0.0, base=0, channel_multiplier=1), reads=[self.ones32], writes=[self.tri[1]])
        for d in range(2):
            f.op("dve", lambda e, d=d: e.tensor_scalar(out=self.negm[d][:], in0=self.tri[d][:], scalar1=-1.0, scalar2=-NEG,
                                                       op0=ALU.add, op1=ALU.mult), reads=[self.tri[d]], writes=[self.negm[d]])
        with f.scope():
            tmp = f.sbuf("cwtmp", [100, 128], F32)
            f.dma("sp", tmp[:], self.W["ssd_conv_w"][:, :, :].rearrange("j k (c p) -> (j k c) p", p=128), writes=[tmp])
            f.op("pe", lambda e: e.transpose(B[0][:, 0:100], tmp[:], self.ident[0:100, 0:100]), reads=[tmp, self.ident], writes=[B[0]])
            f.op("dve", lambda e: e.tensor_copy(out=self.cwT[:], in_=B[0][:, 0:100]), reads=[B[0]], writes=[self.cwT])
            tmp2 = f.sbuf("cbtmp", [20, 128], F32)
            f.dma("sp", tmp2[:], self.W["ssd_conv_b"][:, :].rearrange("j (c p) -> (j c) p", p=128), writes=[tmp2])
            f.op("pe", lambda e: e.transpose(B[1][:, 0:20], tmp2[:], self.ident[0:20, 0:20]), reads=[tmp2, self.ident], writes=[B[1]])
            f.op("dve", lambda e: e.tensor_copy(out=self.cbT[:], in_=B[1][:, 0:20]), reads=[B[1]], writes=[self.cbT])
            for j in range(2):
                for wh in range(2):
                    for g in range(2):
                        c0 = 1024 + wh * 128 + g * 64
                        f.dma("sp", self.cwBC[0:64, j, wh, g, :], self.W["ssd_conv_w"][j, :, c0:c0 + 64].rearrange("k n -> n k"),
                              writes=[self.cwBC], allow_slow_non_contiguous=True)
                        f.dma("sp", self.cbBC[0:64, j, wh, g:g + 1], self.W["ssd_conv_b"][j:j + 1, c0:c0 + 64].rearrange("o n -> n o"),
                              writes=[self.cbBC], allow_slow_non_contiguous=True)

    def build_abias(self):
        f = self.f
        self.bhi = f.sbuf("abhi", [128, 8, 384], BF16)
        self.blo = f.sbuf("ablo", [128, 8, 384], BF16)
        with f.scope():
            self.abias = f.sbuf("abias", [128, 8, 384], F32)
            bk = f.sbuf("bk", [128, 384], F32)
            mk = f.sbuf("mk", [128, 384], F32)
            t5b = f.sbuf("t5b", [128, 256], F32)
            f.dma("sp", bk[:], self.c_bucket[:, :], writes=[bk])
            f.dma("sp", t5b[:], self.W["t5_bias"][:, :].rearrange("b h -> (b h)").unsqueeze(0).partition_broadcast(128)
                  if False else self.W["t5_bias"][:, :].rearrange("(o b) h -> o (b h)", o=1).partition_broadcast(128), writes=[t5b])
            for h in range(8):
                f.dma("sp", self.abias[:, h, :], self.c_maskneg[:, :], writes=[self.abias])
            for b in range(32):
                f.op("pool", lambda e, b=b: e.tensor_scalar(out=mk[:], in0=bk[:], scalar1=float(b), scalar2=None, op0=ALU.is_equal),
                     reads=[bk], writes=[mk])
                for h in range(8):
                    f.op("dve", lambda e, b=b, h=h: e.scalar_tensor_tensor(
                        out=self.abias[:, h, :], in0=mk[:], scalar=t5b[:, b * 8 + h:b * 8 + h + 1], in1=self.abias[:, h, :],
                        op0=ALU.mult, op1=ALU.add), reads=[mk, t5b, self.abias], writes=[self.abias])
            f.op("dve", lambda e: e.tensor_copy(out=self.bhi[:], in_=self.abias[:]), reads=[self.abias], writes=[self.bhi])
            f.op("dve", lambda e: e.tensor_tensor(out=self.abias[:], in0=self.abias[:], in1=self.bhi[:], op=ALU.subtract),
                 reads=[self.abias, self.bhi], writes=[self.abias])
            f.op("dve", lambda e: e.tensor_copy(out=self.blo[:], in_=self.abias[:]), reads=[self.abias], writes=[self.blo])

    def even_consts(self, j):
        f = self.f
        c = {}
        c["sink"] = f.sbuf("sink", [128, 8], F32)
        f.dma("sp", c["sink"][:], self.W["attn_sink"][j:j + 1, :].partition_broadcast(128), writes=[c["sink"]])
        c["a"] = f.sbuf("ssa", [128, 32], F32)
        f.dma("sp", c["a"][:], self.W["ssd_a_log"][j:j + 1, :, :].rearrange("o d h -> o (d h)").partition_broadcast(128), writes=[c["a"]])
        f.op("act", lambda e: e.activation(out=c["a"][:], in_=c["a"][:], func=AF.Exp), reads=[c["a"]], writes=[c["a"]])
        f.op("dve", lambda e: e.tensor_scalar(out=c["a"][:], in0=c["a"][:], scalar1=-1.0, scalar2=None, op0=ALU.mult),
             reads=[c["a"]], writes=[c["a"]])
        c["dtb"] = f.sbuf("ssdtb", [128, 32], F32)
        f.dma("sp", c["dtb"][:], self.W["ssd_dt_bias"][j:j + 1, :, :].rearrange("o d h -> o (d h)").partition_broadcast(128), writes=[c["dtb"]])
        c["D"] = f.sbuf("ssD", [128, 16], F32)
        f.dma("sp", c["D"][:], self.W["ssd_d"][j:j + 1, :].partition_broadcast(128), writes=[c["D"]])
        c["nw"] = f.sbuf("ssnw", [128, 1024], F32)
        f.dma("sp", c["nw"][:], self.W["ssd_norm_w"][j:j + 1, :].partition_broadcast(128), writes=[c["nw"]])
        return c
    def attn_group(self, l, s, hnT, H, g, ec):
        f, cfg = self.f, self.cfg
        S, TS = cfg.S, cfg.TS
        NB = S // 128
        j = l // 2
        B = self.banks
        win = self.W["ev_w_in"]
        with f.scope():
            wq = f.sbuf("awq", [128, DC, 256], BF16)
            wk = f.sbuf("awk", [128, DC, 128], BF16)
            wv = f.sbuf("awv", [128, DC, 64], BF16)
            qT = f.sbuf("aqT", [128, 2, S], BF16)
            kTlo = f.sbuf("akTlo", [128, S + 2 * H], BF16)
            kThi = f.sbuf("akThi", [128, S + 2 * H], BF16)
            Vlo = f.sbuf("aVlo", [128, NB + 2, 128], BF16)
            Vhi = f.sbuf("aVhi", [128, NB + 2, 128], BF16)
            oTa = [f.sbuf("aoT%d" % i, [128, S], BF16) for i in range(2)]
            AW = 3
            s_sb = [f.sbuf("as%d" % i, [128, 384], F32) for i in range(AW)]
            p_sb = [f.sbuf("ap%d" % i, [128, 384], F32) for i in range(AW)]
            pn = [f.sbuf("apn%d" % i, [128, 384], BF16) for i in range(AW)]
            pT = [f.sbuf("apT%d" % i, [128, 3, 128], BF16) for i in range(AW)]
            sm = [f.sbuf("asm%d" % i, [128, 8], F32) for i in range(AW)]
            dgs = [f.sbuf("adg%d" % i, [128, 128], BF16) for i in range(AW)]

            def wsl(c0, n):
                return win[j, :, c0:c0 + n].rearrange("(c p) n -> p c n", p=128)
            f.dma("pool", wq[:], wsl(g * 256, 256), writes=[wq])
            f.dma("pool", wk[:, :, 0:64], wsl(512 + g * 64, 64), writes=[wk])
            f.dma("pool", wk[:, :, 64:128], wsl(512 + g * 64, 64), writes=[wk])
            f.dma("pool", wv[:], wsl(640 + g * 64, 64), writes=[wv])
            f.op("pool", lambda e: e.memset(Vlo[:], 0.0), writes=[Vlo])
            f.op("pool", lambda e: e.memset(Vhi[:], 0.0), writes=[Vhi])
            f.op("pool", lambda e: e.memset(kTlo[:], 0.0), writes=[kTlo])
            f.op("pool", lambda e: e.memset(kThi[:], 0.0), writes=[kThi])
            ranges = [(H + i * TS, TS) for i in range(S // TS)] + [(0, H), (H + S, H)]
            bi = 0
            for (c0, n) in ranges:
                main = (H <= c0 < H + S)
                if main:
                    for p in range(2):
                        b = B[6 + bi % 2]; bi += 1
                        for dc in range(DC):
                            f.op("pe", lambda e, dc=dc, b=b, p=p: e.matmul(b[:, 0:n], lhsT=wq[:, dc, p * 128:(p + 1) * 128],
                                                                            rhs=hnT[:, dc, c0:c0 + n], start=(dc == 0), stop=(dc == DC - 1)),
                                 reads=[wq, hnT], writes=[b])
                        f.op("act", lambda e, b=b, p=p: e.mul(out=qT[:, p, c0 - H:c0 - H + n], in_=b[:, 0:n], mul=0.125),
                             reads=[b], writes=[qT])
                b = B[6 + bi % 2]; bi += 1
                self.proj_fm(wk, hnT, c0, n, b)
                f.op("dve", lambda e, b=b: e.tensor_copy(out=kTlo[0:64, c0:c0 + n], in_=b[0:64, 0:n]), reads=[b], writes=[kTlo])
                f.op("dve", lambda e, b=b: e.tensor_copy(out=kThi[64:128, c0:c0 + n], in_=b[64:128, 0:n]), reads=[b], writes=[kThi])
            blocks = list(range(-1, NB + 1))
            for i0 in range(0, len(blocks), 8):
                grp = blocks[i0:i0 + 8]
                b = B[4 + (i0 // 8) % 2]
                for q, blk in enumerate(grp):
                    self.proj_tm(wv, (0, 64), hnT, H + blk * 128, b, q * 64)
                s0 = grp[0] + 1
                f.op("act", lambda e, b=b, s0=s0, n=len(grp): e.copy(out=Vlo[:, s0:s0 + n, 0:64],
                                                                      in_=b[:, 0:n * 64].rearrange("p (a c) -> p a c", c=64)),
                     reads=[b], writes=[Vlo])
                f.op("act", lambda e, b=b, s0=s0, n=len(grp): e.copy(out=Vhi[:, s0:s0 + n, 64:128],
                                                                      in_=b[:, 0:n * 64].rearrange("p (a c) -> p a c", c=64)),
                     reads=[b], writes=[Vhi])
            grp_cnt = {}

            def unit(it, p, qb, hh):
                kbs = (qb - 1, qb, qb + 1)
                nk = 384
                k0 = H + kbs[0] * 128
                bo = B[6 + qb % 2]
                head = 4 * g + 2 * p + hh
                bs = B[it % AW]
                bt = B[AW + it % AW]
                pp_, pT_, sm_, dg_ = pn[it % AW], pT[it % AW], sm[it % AW], dgs[it % AW]
                kT_ = kTlo if hh == 0 else kThi
                f.op("pe", lambda e: e.matmul(bs[:, 0:nk], lhsT=self.identb[:], rhs=self.bhi[:, head, :], start=True, stop=False),
                     reads=[self.identb, self.bhi], writes=[bs])
                f.op("pe", lambda e: e.matmul(bs[:, 0:nk], lhsT=self.identb[:], rhs=self.blo[:, head, :], start=False, stop=False),
                     reads=[self.identb, self.blo], writes=[bs])
                f.op("pe", lambda e: e.matmul(bs[:, 0:nk], lhsT=qT[:, p, qb * 128:(qb + 1) * 128],
                                              rhs=kT_[:, k0:k0 + nk], start=False, stop=True),
                     reads=[qT, kT_], writes=[bs])
                yield
                if qb == 0:
                    f.op("dve", lambda e: e.tensor_scalar(out=bs[:, 0:128], in0=bs[:, 0:128], scalar1=self.fl(2, s), scalar2=None,
                                                          op0=ALU.add), reads=[bs, self.flags], writes=[bs])
                    yield
                if qb == NB - 1:
                    f.op("dve", lambda e: e.tensor_scalar(out=bs[:, 256:384], in0=bs[:, 256:384], scalar1=self.fl(3, s), scalar2=None,
                                                          op0=ALU.add), reads=[bs, self.flags], writes=[bs])
                    yield
                f.op("dve", lambda e: e.tensor_reduce(out=sm_[:, 0:1], in_=bs[:, 0:nk], axis=AX.X, op=ALU.max),
                     reads=[bs], writes=[sm_])
                yield
                f.op("dve", lambda e: e.tensor_scalar(out=sm_[:, 1:2], in0=sm_[:, 0:1], scalar1=ec["sink"][:, head:head + 1],
                                                      scalar2=-1.0, op0=ALU.max, op1=ALU.mult),
                     reads=[sm_, ec["sink"]], writes=[sm_])
                yield
                f.op("act", lambda e: e.activation(out=pp_[:, 0:nk], in_=bs[:, 0:nk], func=AF.Exp, bias=sm_[:, 1:2], scale=1.0,
                                                   accum_out=sm_[:, 2:3]), reads=[bs, sm_], writes=[pp_, sm_])
                yield
                f.op("act", lambda e: e.activation(out=sm_[:, 3:4], in_=sm_[:, 1:2], func=AF.Exp,
                                                   bias=ec["sink"][:, head:head + 1], scale=1.0),
                     reads=[sm_, ec["sink"]], writes=[sm_])
                yield
                f.op("dve", lambda e: e.tensor_tensor(out=sm_[:, 4:5], in0=sm_[:, 2:3], in1=sm_[:, 3:4], op=ALU.add),
                     reads=[sm_], writes=[sm_])
                yield
                f.op("dve", lambda e: e.reciprocal(out=sm_[:, 5:6], in_=sm_[:, 4:5]), reads=[sm_], writes=[sm_])
                yield
                f.op("dve", lambda e: e.tensor_scalar(out=dg_[:], in0=self.identb[:], scalar1=sm_[:, 5:6], scalar2=None, op0=ALU.mult),
                     reads=[self.identb, sm_], writes=[dg_])
                yield
                for jb in range(3):
                    f.op("pe", lambda e, jb=jb: e.matmul(bt[:, jb * 128:(jb + 1) * 128], lhsT=pp_[:, jb * 128:(jb + 1) * 128], rhs=dg_[:],
                                                         start=True, stop=True), reads=[pp_, dg_], writes=[bt])
                yield
                f.op("act", lambda e: e.copy(out=pT_[:, 0:3, :], in_=bt[:, 0:nk].rearrange("p (a b) -> p a b", b=128)),
                     reads=[bt], writes=[pT_])
                yield
                V_ = Vlo if hh == 0 else Vhi
                for jb, kb in enumerate(kbs):
                    c = grp_cnt.get((p, qb), 0)
                    grp_cnt[(p, qb)] = c + 1
                    f.op("pe", lambda e, jb=jb, kb=kb, c=c: e.matmul(
                        bo[:, 0:128], lhsT=V_[:, kb + 1, :], rhs=pT_[:, jb, :], start=(c == 0), stop=(c == 5)),
                        reads=[V_, pT_], writes=[bo])
                yield
                if grp_cnt[(p, qb)] == 6:
                    f.op("act", lambda e: e.copy(out=oTa[p][:, qb * 128:(qb + 1) * 128], in_=bo[:, 0:128]),
                         reads=[bo], writes=[oTa[p]])
                    if qb == NB - 1:
                        f.dma("sp", self.oT[s][:, 2 * g + p, :], oTa[p][:], reads=[oTa[p]], writes=[self.oT[s]])

            def units():
                it = 0
                for p in range(2):
                    for qb in range(NB):
                        for hh in range(2):
                            yield unit(it, p, qb, hh)
                            it += 1
            lockstep(units(), AW)

    def ssd_F(self, l, s, hnT, H, g, ec):
        f, cfg = self.f, self.cfg
        S = cfg.S
        NT = S // 128
        j = l // 2
        B = self.banks
        win = self.W["ev_w_in"]
        TC = min(256, S)
        sp = self.spill[(s, g)]
        with f.scope():
            wB = f.sbuf("swB", [128, DC, 128], BF16)
            wC = f.sbuf("swC", [128, DC, 128], BF16)
            wdt = f.sbuf("swdt", [128, DC, 16], BF16)
            xs_tm = f.sbuf("sxs", [128, NT, 512], BF16)
            BT = f.sbuf("sBT", [128, S], BF16)
            CT = f.sbuf("sCT", [128, S], BF16)
            B_tm = f.sbuf("sBtm", [128, NT, 128], BF16)
            dt = f.sbuf("sdt", [128, NT, 16], F32)
            dtA = f.sbuf("sdtA", [128, NT, 16], F32)
            y_acc = f.sbuf("syacc", [128, NT, 512], F32)

            def wsl(c0, n):
                return win[j, :, c0:c0 + n].rearrange("(c p) n -> p c n", p=128)
            f.op("pool", lambda e: e.memset(wB[:], 0.0), writes=[wB])
            f.op("pool", lambda e: e.memset(wC[:], 0.0), writes=[wC])
            f.dma("pool", wB[:, :, 0:64], wsl(2816 + g * 64, 64), writes=[wB])
            f.dma("pool", wC[:, :, 0:64], wsl(2944 + g * 64, 64), writes=[wC])
            f.dma("pool", wdt[:, :, 0:8], wsl(3072 + g * 8, 8), writes=[wdt])
            f.dma("pool", wdt[:, :, 8:16], wsl(3088 + g * 8, 8), writes=[wdt])
            if True:
                wx = f.sbuf("swx", [128, DC, 512], BF16)
                f.dma("pool", wx[:], wsl(1792 + g * 512, 512), writes=[wx])
                NW = 3
                acc = [f.sbuf("sacc%d" % i, [128, TC], F32) for i in range(NW)]
                xsT = [f.sbuf("sxsT%d" % i, [128, TC], BF16) for i in range(NW)]
                xcnt = {}

                def cunit(it, c0, cc):
                    b = B[it % NW]
                    a_, xo_ = acc[it % NW], xsT[it % NW]
                    if cc < 4:
                        wblk = wx[:, :, cc * 128:(cc + 1) * 128]
                        wdep = wx
                    else:
                        wdep = wB if cc == 4 else wC
                        wblk = wdep[:, :, :]
                    for dc in range(DC):
                        f.op("pe", lambda e, dc=dc: e.matmul(
                            b[:, 0:TC + 4], lhsT=wblk[:, dc, :], rhs=hnT[:, dc, H + c0 - 2:H + c0 + TC + 2],
                            start=(dc == 0), stop=(dc == DC - 1)), reads=[wdep, hnT], writes=[b])
                    yield
                    if cc < 4:
                        def wcol(k):
                            c = (j * 5 + k) * 10 + g * 4 + cc
                            return self.cwT[:, c:c + 1]
                        bcol = self.cbT[:, j * 10 + g * 4 + cc: j * 10 + g * 4 + cc + 1]
                        cdeps = [self.cwT, self.cbT]
                    else:
                        def wcol(k):
                            return self.cwBC[:, j, cc - 4, g, k:k + 1]
                        bcol = self.cbBC[:, j, cc - 4, g:g + 1]
                        cdeps = [self.cwBC, self.cbBC]
                    f.op("dve", lambda e: e.tensor_scalar(
                        out=a_[:, :], in0=b[:, 2:2 + TC], scalar1=wcol(2), scalar2=bcol, op0=ALU.mult, op1=ALU.add),
                        reads=[b] + cdeps, writes=[a_])
                    yield
                    for k in (0, 1, 3, 4):
                        f.op("dve", lambda e, k=k: e.scalar_tensor_tensor(
                            out=a_[:, :], in0=b[:, k:k + TC], scalar=wcol(k), in1=a_[:, :], op0=ALU.mult, op1=ALU.add),
                            reads=[b, a_] + cdeps, writes=[a_])
                        yield
                    if cc < 4:
                        f.op("act", lambda e: e.activation(out=xo_[:], in_=a_[:], func=AF.Silu), reads=[a_], writes=[xo_])
                        yield
                        for tt in range(TC // 128):
                            t = c0 // 128 + tt
                            bt = B[4 + (t % 2)]
                            btb = bt[:, 0:256].bitcast(BF16)
                            f.op("pe", lambda e, tt=tt, btb=btb: e.transpose(
                                btb[:, cc * 128:(cc + 1) * 128], xo_[:, tt * 128:(tt + 1) * 128], self.identb[:]),
                                reads=[xo_, self.identb], writes=[bt])
                        yield
                        xcnt[c0] = xcnt.get(c0, 0) + 1
                        if xcnt[c0] == 4:
                            for tt in range(TC // 128):
                                t = c0 // 128 + tt
                                bt = B[4 + (t % 2)]
                                btb = bt[:, 0:256].bitcast(BF16)
                                f.op("act", lambda e, btb=btb, t=t: e.copy(out=xs_tm[:, t, :], in_=btb[:, 0:512]), reads=[bt], writes=[xs_tm])
                            yield
                    else:
                        dst = BT if cc == 4 else CT
                        f.op("act", lambda e: e.activation(out=dst[:, c0:c0 + TC], in_=a_[:, :], func=AF.Silu),
                             reads=[a_], writes=[dst])
                        yield
                        if cc == 4:
                            for tt in range(TC // 128):
                                t = c0 // 128 + tt
                                bt = B[6 + (t % 2)]
                                btb = bt[:, 0:64].bitcast(BF16)
                                f.op("pe", lambda e, btb=btb, t=t: e.transpose(btb[:, 0:128], BT[:, t * 128:(t + 1) * 128], self.identb[:]),
                                     reads=[BT, self.identb], writes=[bt])
                                f.op("dve", lambda e, btb=btb, t=t: e.tensor_copy(out=B_tm[:, t, :], in_=btb[:, 0:128]), reads=[bt], writes=[B_tm])
                            yield

                def cunits():
                    it = 0
                    for c0 in range(0, S, TC):
                        for cc in range(6):
                            yield cunit(it, c0, cc)
                            it += 1
                lockstep(cunits(), NW)
            if True:
                t1 = f.sbuf("sdt1", [128, NT, 16], F32)
                t2 = f.sbuf("sdt2", [128, NT, 16], F32)
                b = B[0]
                for t in range(NT):
                    self.proj_tm(wdt, (0, 16), hnT, H + t * 128, b, t * 16)
                bv = b[:, 0:NT * 16].rearrange("p (t x) -> p t x", x=16)
                dtb16 = f.sbuf("sdtb16", [128, 16], F32)
                a16 = f.sbuf("sa16", [128, 16], F32)
                for d in range(2):
                    f.op("pool", lambda e, d=d: e.tensor_copy(out=dtb16[:, d * 8:(d + 1) * 8], in_=ec["dtb"][:, d * 16 + g * 8:d * 16 + g * 8 + 8]),
                         reads=[ec["dtb"]], writes=[dtb16])
                    f.op("pool", lambda e, d=d: e.tensor_copy(out=a16[:, d * 8:(d + 1) * 8], in_=ec["a"][:, d * 16 + g * 8:d * 16 + g * 8 + 8]),
                         reads=[ec["a"]], writes=[a16])
                f.op("dve", lambda e: e.tensor_tensor(out=t1[:], in0=bv, in1=dtb16[:, :].unsqueeze(1).to_broadcast([128, NT, 16]), op=ALU.add),
                     reads=[b, dtb16], writes=[t1])
                f.op("act", lambda e: e.activation(out=t2[:], in_=t1[:], func=AF.Abs), reads=[t1], writes=[t2])
                f.op("act", lambda e: e.activation(out=t2[:], in_=t2[:], func=AF.Exp, scale=-1.0), reads=[t2], writes=[t2])
                f.op("act", lambda e: e.activation(out=t2[:], in_=t2[:], func=AF.Ln, bias=1.0, scale=1.0), reads=[t2], writes=[t2])
                f.op("dve", lambda e: e.tensor_scalar(out=t1[:], in0=t1[:], scalar1=0.0, scalar2=None, op0=ALU.max), reads=[t1], writes=[t1])
                f.op("dve", lambda e: e.tensor_tensor(out=dt[:], in0=t1[:], in1=t2[:], op=ALU.add), reads=[t1, t2], writes=[dt])
                f.op("dve", lambda e: e.tensor_tensor(out=dtA[:], in0=dt[:], in1=a16[:, :].unsqueeze(1).to_broadcast([128, NT, 16]), op=ALU.mult),
                     reads=[dt, a16], writes=[dtA])
            self.ssd_scan(s, g, 0, ec, xs_tm, BT, CT, B_tm, dt, dtA, y_acc)
            f.dma("sp", sp["xs"][:, :, :], xs_tm[:], reads=[xs_tm], writes=[sp["xs"]])
            f.dma("sp", sp["BT"][:, :], BT[:], reads=[BT], writes=[sp["BT"]])
            f.dma("sp", sp["CT"][:, :], CT[:], reads=[CT], writes=[sp["CT"]])
            f.dma("sp", sp["Btm"][:, :, :], B_tm[:], reads=[B_tm], writes=[sp["Btm"]])
            f.dma("sp", sp["dt"][:, :, :], dt[:], reads=[dt], writes=[sp["dt"]])
            f.dma("sp", sp["dtA"][:, :, :], dtA[:], reads=[dtA], writes=[sp["dtA"]])
            f.dma("sp", sp["y"][:, :, :], y_acc[:], reads=[y_acc], writes=[sp["y"]])

    def ssd_scan(self, s, g, d, ec, xs_tm, BT, CT, B_tm, dt, dtA, y_acc):
        f, cfg = self.f, self.cfg
        NT = cfg.S // 128
        B = self.banks
        carry = ec["carry"][g]
        flag = self.fl(d, s)
        if True:
            NP = 3
            cbm = [f.sbuf("scb%d" % i, [128, 128], F32) for i in range(NP)]
            R1s = [f.sbuf("sR1_%d" % i, [128, 8, 128], F32) for i in range(2)]
            arg = [f.sbuf("sarg%d" % i, [128, 8, 128], F32) for i in range(NP)]
            Wt = [f.sbuf("sWt%d" % i, [128, 8, 128], BF16) for i in range(NP)]
            csts = [f.sbuf("scst%d" % i, [128, 8], F32) for i in range(2)]
            od = [f.sbuf("sod%d" % i, [128, 8], F32) for i in range(NP)]
            dcb = [f.sbuf("sdcb%d" % i, [128, 8], F32) for i in range(NP)]
            xc = [f.sbuf("sxc%d" % i, [128, 8, 64], BF16) for i in range(NP)]
            xw = [f.sbuf("sxw%d" % i, [128, 8, 64], BF16) for i in range(NP)]
            ytmp = f.sbuf("sytmp", [128, 8, 64], F32)
            hT = f.sbuf("shT", [128, 8, 64], F32)
            hTb = f.sbuf("shTb", [128, 8, 64], BF16)
            f.op("dve", lambda e: e.tensor_scalar(out=hT[:], in0=carry[:], scalar1=flag, scalar2=None, op0=ALU.mult),
                 reads=[carry, self.flags], writes=[hT])
            f.op("dve", lambda e: e.tensor_scalar(out=hTb[:], in0=carry[:], scalar1=flag, scalar2=None, op0=ALU.mult),
                 reads=[carry, self.flags], writes=[hTb])
            ecol = 127 if d == 0 else 0
            tri = self.tri[d]
            bC, bP0, bP1, bYo, bU = B[0], B[1], B[2], B[6], B[7]
            bYd = (B[3], B[4], B[5])

            def pre(ti):
                t = ti if d == 0 else NT - 1 - ti
                c0 = t * 128
                i2 = ti % NP
                R1, cst = R1s[ti % 2], csts[ti % 2]
                cb_, arg_, Wt_, od_, dcb_, xc_, xw_ = cbm[i2], arg[i2], Wt[i2], od[i2], dcb[i2], xc[i2], xw[i2]
                f.op("pe", lambda e: e.matmul(bC[:, 0:128], lhsT=BT[:, c0:c0 + 128], rhs=CT[:, c0:c0 + 128], start=True, stop=True),
                     reads=[BT, CT], writes=[bC])
                dtA_d = dtA[:, t, d * 8:(d + 1) * 8]
                for h in range(8):
                    f.op("act", lambda e, h=h: e.mul(out=R1[:, h, :], in_=tri[:, :], mul=dtA[:, t, d * 8 + h:d * 8 + h + 1]),
                         reads=[tri, dtA], writes=[R1])
                yield
                f.op("pe", lambda e: e.matmul(bC[:, 128:136], lhsT=tri[:], rhs=dtA_d, start=True, stop=True),
                     reads=[tri, dtA], writes=[bC])
                f.op("dve", lambda e: e.tensor_tensor(out=cb_[:], in0=bC[:, 0:128], in1=tri[:], op=ALU.mult), reads=[bC, tri], writes=[cb_])
                yield
                f.op("dve", lambda e: e.tensor_scalar(out=cst[:], in0=bC[:, 128:136], scalar1=-1.0, scalar2=None, op0=ALU.mult),
                     reads=[bC], writes=[cst])
                for hf, bP in enumerate((bP0, bP1)):
                    f.op("pe", lambda e, hf=hf, bP=bP: e.matmul(bP[:, 0:512], lhsT=self.ones32[:],
                                                                rhs=R1[:, hf * 4:(hf + 1) * 4, :].rearrange("p a b -> p (a b)"),
                                                                start=True, stop=True), reads=[self.ones32, R1], writes=[bP])
                yield
                f.op("act", lambda e: e.activation(out=od_[:], in_=cst[:], func=AF.Exp, scale=-1.0), reads=[cst], writes=[od_])
                yield
                for hf, bP in enumerate((bP0, bP1)):
                    for h4 in range(4):
                        hh_ = hf * 4 + h4
                        f.op("act", lambda e, hh_=hh_, h4=h4, bP=bP: e.activation(
                            out=arg_[:, hh_, :], in_=bP[:, h4 * 128:(h4 + 1) * 128], func=AF.Abs, bias=cst[:, hh_:hh_ + 1], scale=1.0),
                            reads=[bP, cst], writes=[arg_])
                    f.op("act", lambda e, hf=hf, bP=bP: e.activation(
                        out=dcb_[:, hf * 4:(hf + 1) * 4], in_=bP[:, 0:512].rearrange("p (a b) -> p a b", b=128)[:, :, ecol], func=AF.Exp),
                        reads=[bP], writes=[dcb_])
                    yield
                f.op("act", lambda e: e.activation(out=arg_[:], in_=arg_[:], func=AF.Exp, scale=-1.0), reads=[arg_], writes=[arg_])
                yield
                f.op("dve", lambda e: e.tensor_tensor(out=Wt_[:], in0=arg_[:], in1=cb_[:, :].unsqueeze(1).to_broadcast([128, 8, 128]),
                                                      op=ALU.mult), reads=[arg_, cb_], writes=[Wt_])
                f.op("dve", lambda e: e.tensor_tensor(out=xc_[:], in0=xs_tm[:, t, :].rearrange("p (h x) -> p h x", x=64),
                                                      in1=dt[:, t, d * 8:(d + 1) * 8].unsqueeze(2).to_broadcast([128, 8, 64]), op=ALU.mult),
                     reads=[xs_tm, dt], writes=[xc_])
                yield
                f.op("dve", lambda e: e.tensor_tensor(out=xw_[:], in0=xc_[:], in1=arg_[:, :, ecol:ecol + 1].to_broadcast([128, 8, 64]),
                                                      op=ALU.mult), reads=[xc_, arg_], writes=[xw_])
                yield
                bY = bYd[i2]
                for h in range(8):
                    f.op("pe", lambda e, h=h: e.matmul(bY[:, h * 64:(h + 1) * 64], lhsT=Wt_[:, h, :], rhs=xc_[:, h, :],
                                                       start=True, stop=True), reads=[Wt_, xc_], writes=[bY])
                    if h % 2 == 1:
                        yield

            def chain(ti):
                t = ti if d == 0 else NT - 1 - ti
                c0 = t * 128
                i2 = ti % NP
                od_, dcb_, xw_, bY = od[i2], dcb[i2], xw[i2], bYd[i2]
                f.op("pe", lambda e: e.matmul(bYo[:, 0:512], lhsT=CT[:, c0:c0 + 128], rhs=hTb[:].rearrange("p a b -> p (a b)"),
                                              start=True, stop=True), reads=[CT, hTb], writes=[bYo])
                f.op("pe", lambda e: e.matmul(bU[:, 0:512], lhsT=B_tm[:, t, :], rhs=xw_[:].rearrange("p a b -> p (a b)"),
                                              start=True, stop=True), reads=[B_tm, xw_], writes=[bU])
                yield
                f.op("dve", lambda e: e.tensor_tensor(out=hT[:], in0=hT[:], in1=dcb_[:, :].unsqueeze(2).to_broadcast([128, 8, 64]),
                                                      op=ALU.mult), reads=[hT, dcb_], writes=[hT])
                yield
                f.op("dve", lambda e: e.tensor_tensor(out=hT[:], in0=hT[:], in1=bU[:, 0:512].rearrange("p (a b) -> p a b", b=64),
                                                      op=ALU.add), reads=[hT, bU], writes=[hT])
                yield
                f.op("act", lambda e: e.copy(out=hTb[:], in_=hT[:]), reads=[hT], writes=[hTb])
                yield
                f.op("dve", lambda e: e.tensor_tensor(out=ytmp[:], in0=bYo[:, 0:512].rearrange("p (h x) -> p h x", x=64),
                                                      in1=od_[:, :].unsqueeze(2).to_broadcast([128, 8, 64]), op=ALU.mult),
                     reads=[bYo, od_], writes=[ytmp])
                yield
                ya = y_acc[:, t, :].rearrange("p (h x) -> p h x", x=64)
                if d == 1:
                    f.op("dve", lambda e: e.tensor_tensor(out=ytmp[:], in0=ytmp[:], in1=ya, op=ALU.add),
                         reads=[ytmp, y_acc], writes=[ytmp])
                    yield
                f.op("dve", lambda e: e.tensor_tensor(out=ya, in0=bY[:, 0:512].rearrange("p (h x) -> p h x", x=64), in1=ytmp[:],
                                                      op=ALU.add), reads=[bY, ytmp], writes=[y_acc])
                yield

            def pres():
                for ti in range(NT):
                    yield from pre(ti)
                    yield ("done", ti)

            def driver():
                pg = pres()
                done = -1
                while done < 0:
                    if next(pg) is not None:
                        done = 0
                    yield
                for ti in range(NT):
                    cg = chain(ti)
                    alive = True
                    while alive:
                        try:
                            next(cg)
                            yield
                        except StopIteration:
                            alive = False
                        if done < min(ti + 2, NT - 1):
                            r = next(pg)
                            if r is not None:
                                done = r[1]
                            yield
                    while done < min(ti + 1, NT - 1):
                        r = next(pg)
                        if r is not None:
                            done = r[1]
                        yield
            lockstep([driver()], 1)
            f.op("pool", lambda e: e.tensor_copy(out=carry[:], in_=hT[:]), reads=[hT], writes=[carry])

    def ssd_B(self, l, s, hnT, H, g, ec):
        f, cfg = self.f, self.cfg
        S = cfg.S
        NT = S // 128
        j = l // 2
        B = self.banks
        win = self.W["ev_w_in"]
        sp = self.spill[(s, g)]
        with f.scope():
            wz = f.sbuf("swz", [128, DC, 512], BF16)
            xs_tm = f.sbuf("sxs", [128, NT, 512], BF16)
            BT = f.sbuf("sBT", [128, S], BF16)
            CT = f.sbuf("sCT", [128, S], BF16)
            B_tm = f.sbuf("sBtm", [128, NT, 128], BF16)
            dt = f.sbuf("sdt", [128, NT, 16], F32)
            dtA = f.sbuf("sdtA", [128, NT, 16], F32)
            y_acc = f.sbuf("syacc", [128, NT, 512], F32)
            f.dma("pool", wz[:], win[j, :, 768 + g * 512:768 + (g + 1) * 512].rearrange("(c p) n -> p c n", p=128), writes=[wz])
            f.dma("sp", xs_tm[:], sp["xs"][:, :, :], reads=[sp["xs"]], writes=[xs_tm])
            f.dma("sp", BT[:], sp["BT"][:, :], reads=[sp["BT"]], writes=[BT])
            f.dma("sp", CT[:], sp["CT"][:, :], reads=[sp["CT"]], writes=[CT])
            f.dma("sp", B_tm[:], sp["Btm"][:, :, :], reads=[sp["Btm"]], writes=[B_tm])
            f.dma("sp", dt[:], sp["dt"][:, :, :], reads=[sp["dt"]], writes=[dt])
            f.dma("sp", dtA[:], sp["dtA"][:, :, :], reads=[sp["dtA"]], writes=[dtA])
            f.dma("sp", y_acc[:], sp["y"][:, :, :], reads=[sp["y"]], writes=[y_acc])
            self.ssd_scan(s, g, 1, ec, xs_tm, BT, CT, B_tm, dt, dtA, y_acc)
            if True:
                zs = [f.sbuf("szs%d" % i, [128, 512], F32) for i in range(2)]
                yn = [f.sbuf("syn%d" % i, [128, 512], BF16) for i in range(2)]
                sq = [f.sbuf("ssq%d" % i, [128, 512], F32) for i in range(2)]
                ssum = f.sbuf("sssum", [128, NT], F32)
                rs = f.sbuf("srs", [128, NT], F32)
                oTs = [f.sbuf("soTs%d" % i, [128, 4, 512], BF16) for i in range(2)]
                D8 = ec["D"][:, g * 8:(g + 1) * 8]

                def gate(t):
                    z_ = zs[t % 2]
                    bz = B[t % 2]
                    self.proj_tm(wz, (0, 512), hnT, H + t * 128, bz, 0)
                    f.op("dve", lambda e: e.tensor_tensor(out=z_[:].rearrange("p (h x) -> p h x", x=64),
                                                          in0=xs_tm[:, t, :].rearrange("p (h x) -> p h x", x=64),
                                                          in1=D8.unsqueeze(2).to_broadcast([128, 8, 64]), op=ALU.mult),
                         reads=[xs_tm, ec["D"]], writes=[z_])
                    yield
                    f.op("dve", lambda e: e.tensor_tensor(out=y_acc[:, t, :], in0=y_acc[:, t, :], in1=z_[:], op=ALU.add),
                         reads=[y_acc, z_], writes=[y_acc])
                    yield
                    f.op("act", lambda e: e.activation(out=z_[:], in_=bz[:, 0:512], func=AF.Silu), reads=[bz], writes=[z_])
                    yield
                    f.op("dve", lambda e: e.tensor_tensor(out=y_acc[:, t, :], in0=y_acc[:, t, :], in1=z_[:], op=ALU.mult),
                         reads=[y_acc, z_], writes=[y_acc])
                    yield
                    f.op("act", lambda e: e.activation(out=sq[t % 2][:], in_=y_acc[:, t, :], func=AF.Square, accum_out=ssum[:, t:t + 1]),
                         reads=[y_acc], writes=[sq[t % 2], ssum])
                    yield
                lockstep((gate(t) for t in range(NT)), 2)
                f.op("act", lambda e: e.activation(out=rs[:], in_=ssum[:], func=AF.Ln, bias=self.eps[:, 0:1], scale=1.0 / 512),
                     reads=[ssum, self.eps], writes=[rs])
                f.op("act", lambda e: e.activation(out=rs[:], in_=rs[:], func=AF.Exp, scale=-0.5), reads=[rs], writes=[rs])

                def fin(t):
                    y_ = yn[t % 2]
                    f.op("dve", lambda e: e.scalar_tensor_tensor(out=y_[:], in0=y_acc[:, t, :], scalar=rs[:, t:t + 1],
                                                                 in1=ec["nw"][:, g * 512:(g + 1) * 512], op0=ALU.mult, op1=ALU.mult),
                         reads=[y_acc, rs, ec["nw"]], writes=[y_])
                    yield
                    bt = B[2 + t % 2]
                    btb = bt[:, 0:256].bitcast(BF16)
                    for c in range(4):
                        f.op("pe", lambda e, c=c: e.transpose(btb[:, c * 128:(c + 1) * 128], y_[:, c * 128:(c + 1) * 128], self.identb[:]),
                             reads=[y_, self.identb], writes=[bt])
                    yield
                    o_ = oTs[(t // 4) % 2]
                    f.op("act", lambda e: e.copy(out=o_[:, :, (t % 4) * 128:(t % 4 + 1) * 128],
                                                 in_=btb[:, 0:512].rearrange("p (c x) -> p c x", x=128)),
                         reads=[bt], writes=[o_])
                    yield
                for t0 in range(0, NT, 4):
                    n4 = min(4, NT - t0)
                    lockstep((fin(t) for t in range(t0, t0 + n4)), 2)
                    o_ = oTs[(t0 // 4) % 2]
                    f.dma("sp", self.oT[s][:, 4 + 4 * g:8 + 4 * g, t0 * 128:(t0 + n4) * 128], o_[:, :, 0:n4 * 128], reads=[o_], writes=[self.oT[s]])

    def phase_c(self, l, s, cur, nxt, last):
        f, cfg = self.f, self.cfg
        S, TS = cfg.S, cfg.TS
        j = l // 2
        even = (l % 2 == 0)
        nO = 12 if even else 8
        wout = self.W["ev_w_out"] if even else self.W["od_w_out"]
        B = self.banks
        with f.scope():
            wo = f.sbuf("wo", [128, nO, D], BF16)
            xt = f.sbuf("xt", [128, DC, TS], F32)
            ots = [f.sbuf("ot%d" % i, [128, nO, TS], BF16) for i in range(2)]
            mt = f.sbuf("mt", [128, DC, TS], F32)
            sq = f.sbuf("sq", [128, DC, TS], BF16)
            rstd = f.sbuf("rstd", [128, TS], F32)
            hn2 = f.sbuf("hn2", [128, DC, TS], BF16)
            act = f.sbuf("actT", [128, NFC, TS], BF16)
            sg = [f.sbuf("sg", [128, TS], BF16) for _ in range(2)]
            NWB = 4
            wg = [f.sbuf("wg", [128, DC, 128], BF16) for _ in range(NWB)]
            wu = [f.sbuf("wu", [128, DC, 128], BF16) for _ in range(NWB)]
            wd = [f.sbuf("wd", [128, NFC, 128], BF16) for _ in range(3)]
            yo = [f.sbuf("yo", [128, D], F32) for _ in range(2)] if last else None
            if cfg.mixers:
                f.dma("sp", wo[:], self.wo_t[:, 0:nO * D].rearrange("p (c d) -> p c d", d=D), reads=[self.wo_t], writes=[wo])
            wi = 0
            if cfg.mixers:
                f.dma("sp", ots[0][:], self.oT[s][:, 0:nO, 0:TS], reads=[self.oT[s]], writes=[ots[0]])
            for st_i in range(S // TS):
                c0 = st_i * TS
                ot = ots[st_i % 2]
                f.dma("sp", xt[:], cur[s][:, :, c0:c0 + TS], reads=[cur[s]], writes=[xt])
                if cfg.mixers:
                    for dc in range(DC):
                        b = B[dc % 2]
                        for oc in range(nO):
                            f.op("pe", lambda e, b=b, oc=oc, dc=dc: e.matmul(
                                b[:, 0:TS], lhsT=wo[:, oc, dc * 128:(dc + 1) * 128], rhs=ot[:, oc, :],
                                start=(oc == 0), stop=(oc == nO - 1)), reads=[wo, ot], writes=[b])
                        f.op("act", lambda e, b=b, dc=dc: e.copy(out=mt[:, dc, :], in_=b[:, 0:TS]),
                             reads=[b], writes=[mt])
                        f.op("act", lambda e, b=b, dc=dc: e.activation(out=sq[:, dc, :], in_=b[:, 0:TS], func=AF.Square),
                             reads=[b], writes=[sq])
                    if st_i + 1 < S // TS:
                        o2 = ots[(st_i + 1) % 2]
                        f.dma("sp", o2[:], self.oT[s][:, 0:nO, c0 + TS:c0 + 2 * TS], reads=[self.oT[s]], writes=[o2])
                    self.rstd_from_sq(sq, rstd, B[2], TS)
                    for dc in range(DC):
                        f.op("dve", lambda e, dc=dc: e.scalar_tensor_tensor(
                            out=mt[:, dc, :], in0=mt[:, dc, :], scalar=self.gcol(l, 1, dc), in1=rstd[:, :],
                            op0=ALU.mult, op1=ALU.mult), reads=[mt, self.gall, rstd], writes=[mt])
                    for dc in range(DC):
                        f.op("dve", lambda e, dc=dc: e.tensor_tensor(out=xt[:, dc, :], in0=xt[:, dc, :], in1=mt[:, dc, :], op=ALU.add),
                             reads=[xt, mt], writes=[xt])
                f.op("act", lambda e: e.activation(out=sq[:], in_=xt[:], func=AF.Square), reads=[xt], writes=[sq])
                self.rstd_from_sq(sq, rstd, B[2], TS)
                for dc in range(DC):
                    f.op("dve", lambda e, dc=dc: e.scalar_tensor_tensor(
                        out=hn2[:, dc, :], in0=xt[:, dc, :], scalar=self.gcol(l, 2, dc), in1=rstd[:, :],
                        op0=ALU.mult, op1=ALU.mult), reads=[xt, self.gall, rstd], writes=[hn2])
                for fc in range(NFC):
                    g_, u_ = wg[wi % NWB], wu[wi % NWB]
                    wi += 1
                    f.dma("sp", g_[:], self.wg_t[fc].rearrange("p (c n) -> p c n", n=128), reads=[self.wg_t], writes=[g_])
                    f.dma("sp", u_[:], self.wu_t[fc].rearrange("p (c n) -> p c n", n=128), reads=[self.wu_t], writes=[u_])
                    bg, bu = B[4 + (fc % 2)], B[6 + (fc % 2)]
                    for dc in range(DC):
                        f.op("pe", lambda e, dc=dc, g_=g_, bg=bg: e.matmul(
                            bg[:, 0:TS], lhsT=g_[:, dc, :], rhs=hn2[:, dc, :], start=(dc == 0), stop=(dc == DC - 1)),
                            reads=[g_, hn2], writes=[bg])
                    for dc in range(DC):
                        f.op("pe", lambda e, dc=dc, u_=u_, bu=bu: e.matmul(
                            bu[:, 0:TS], lhsT=u_[:, dc, :], rhs=hn2[:, dc, :], start=(dc == 0), stop=(dc == DC - 1)),
                            reads=[u_, hn2], writes=[bu])
                    sgt = sg[fc % 2]
                    f.op("act", lambda e, sgt=sgt, bg=bg: e.activation(out=sgt[:], in_=bg[:, 0:TS], func=AF.Silu),
                         reads=[bg], writes=[sgt])
                    f.op("dve", lambda e, sgt=sgt, bu=bu, fc=fc: e.tensor_tensor(
                        out=act[:, fc, :], in0=sgt[:], in1=bu[:, 0:TS], op=ALU.mult),
                        reads=[sgt, bu], writes=[act])
                for dc in range(DC):
                    d_ = wd[dc % 3]
                    f.dma("sp", d_[:], self.wd_t[dc].rearrange("p (c n) -> p c n", n=128), reads=[self.wd_t], writes=[d_])
                    b = B[dc % 2]
                    for fc in range(NFC):
                        f.op("pe", lambda e, b=b, fc=fc, d_=d_: e.matmul(
                            b[:, 0:TS], lhsT=d_[:, fc, :], rhs=act[:, fc, :], start=(fc == 0), stop=(fc == NFC - 1)),
                            reads=[d_, act], writes=[b])
                    f.op("act", lambda e, b=b, dc=dc: e.copy(out=mt[:, dc, :], in_=b[:, 0:TS]), reads=[b], writes=[mt])
                    f.op("act", lambda e, b=b, dc=dc: e.activation(out=sq[:, dc, :], in_=b[:, 0:TS], func=AF.Square),
                         reads=[b], writes=[sq])
                self.rstd_from_sq(sq, rstd, B[2], TS)
                for dc in range(DC):
                    f.op("dve", lambda e, dc=dc: e.scalar_tensor_tensor(
                        out=mt[:, dc, :], in0=mt[:, dc, :], scalar=self.gcol(l, 3, dc), in1=rstd[:, :],
                        op0=ALU.mult, op1=ALU.mult), reads=[mt, self.gall, rstd], writes=[mt])
                for dc in range(DC):
                    f.op("dve", lambda e, dc=dc: e.tensor_tensor(out=xt[:, dc, :], in0=xt[:, dc, :], in1=mt[:, dc, :], op=ALU.add),
                         reads=[xt, mt], writes=[xt])
                if not last:
                    f.dma("sp", nxt[s][:, :, c0:c0 + TS], xt[:], reads=[xt], writes=[nxt[s]])
                else:
                    for tt in range(TS // 128):
                        y = yo[tt % 2]
                        for half in range(2):
                            b = B[2 + half]
                            for q in range(4):
                                dc = half * 4 + q
                                f.op("pe", lambda e, b=b, q=q, dc=dc, tt=tt: e.transpose(
                                    b[:, q * 128:(q + 1) * 128], xt[:, dc, tt * 128:(tt + 1) * 128], self.ident[:]),
                                    reads=[xt, self.ident], writes=[b])
                            if half == 0:
                                f.op("act", lambda e, b=b, y=y: e.copy(out=y[:, 0:512], in_=b[:]), reads=[b], writes=[y])
                            else:
                                f.op("dve", lambda e, b=b, y=y: e.tensor_copy(out=y[:, 512:1024], in_=b[:]),
                                     reads=[b], writes=[y])
                        r0 = c0 + tt * 128
                        f.dma("sp", self.y_out[s, r0:r0 + 128, :], y[:], reads=[y])


_CACHE = {}

NSEG = 8
ASSIGN = [[("s", i) for i in range(8)]]
_p = 0
for _c in range(1, 8):
    _n = 3 if _c <= 2 else 2
    ASSIGN.append([("p", _p + i) for i in range(_n)] + [None] * (NSEG - _n))
    _p += _n


def make_flags(chain_left):
    n = len(chain_left)
    cl = np.asarray(chain_left, dtype=np.float32)
    cr = np.concatenate([cl[1:], np.zeros(1, np.float32)])
    return np.concatenate([cl, cr, NEG * (1 - cl), NEG * (1 - cr)]).astype(np.float32)[None, :]


def _get_nc(cfg_key, cfg):
    if cfg_key not in _CACHE:
        _CACHE[cfg_key] = KB(cfg).build()
    return _CACHE[cfg_key]


def kernel(**inputs):
    cfg = Cfg()
    nc = _get_nc("full", cfg)
    xp = np.ascontiguousarray(inputs["x_prompt"], dtype=np.float32)
    xs = np.ascontiguousarray(inputs["x_sample"], dtype=np.float32)
    S = cfg.S
    in_maps = []
    hc = host_consts()
    wmap = {name: np.ascontiguousarray(inputs[name], dtype=np.float32) for name, _ in W_SPECS}
    for c in range(8):
        x_in = np.zeros((NSEG, S, D), np.float32)
        chain = [0] * NSEG
        for i, a in enumerate(ASSIGN[c]):
            if a is None:
                continue
            if a[0] == "s":
                x_in[i] = xs[0, a[1] * S:(a[1] + 1) * S]
                chain[i] = 1 if a[1] > 0 else 0
            else:
                x_in[i] = xp[a[1]]
        m = {"x_in": x_in, "flags": make_flags(chain)}
        m.update(hc)
        m.update(wmap)
        in_maps.append(m)
    res = run_bass_kernel_spmd(nc, in_maps, core_ids=list(range(8)))
    yp = np.empty_like(xp)
    ys = np.empty_like(xs)
    for c in range(8):
        y = np.asarray(res.results[c]["y_out"])
        for i, a in enumerate(ASSIGN[c]):
            if a is None:
                continue
            if a[0] == "s":
                ys[0, a[1] * S:(a[1] + 1) * S] = y[i]
            else:
                yp[a[1]] = y[i]
    return (yp, ys)
```
